# Optimizing a Trainium2 kernel written in Bass

```python
import jax, jax.numpy as jnp
from jax import lax
import numpy as np

D_MODEL = 1024
BATCH = 8
SEQ = 2048
DEPTH = 1
DEC_BATCH = 128
DEC_SEQ = 1
PAST_LEN = 16384
PAGE_SIZE = 128

D_CONV = D_MODEL
CONV_A_W = 3
N_HEADS = 8
HEAD_K = 128
HEAD_V = 128
D_K = N_HEADS * HEAD_K
D_V = N_HEADS * HEAD_V
D_QKV = 2 * D_K + D_V
CONV_B_W = 4
CHUNK = 64
EPS = 1e-6
IN_SIZES = (D_CONV, D_CONV, D_CONV, D_CONV, D_QKV, D_V, N_HEADS, N_HEADS, D_MODEL, D_MODEL)
IN_COLS = sum(IN_SIZES)

kernel_name = "hybrid_shortconv_gdn_decode_step"


def rmsnorm(x, w):
    xf = x.astype(jnp.float32)
    y = xf * lax.rsqrt(jnp.mean(xf * xf, axis=-1, keepdims=True) + EPS)
    return (y * w.astype(jnp.float32)).astype(x.dtype)


def l2norm(x):
    xf = x.astype(jnp.float32)
    return xf * lax.rsqrt(jnp.sum(xf * xf, axis=-1, keepdims=True) + EPS)


def causal_dwconv(x, buf, w):
    width = w.shape[0]
    L = x.shape[1]
    xp = jnp.concatenate([buf.astype(x.dtype), x], axis=1)
    y = xp[:, 0:L] * w[0]
    for j in range(1, width):
        y = y + xp[:, j:j + L] * w[j]
    return y, xp[:, L:]


def gated_delta_chunked(q, k, v, beta, g, s0):
    bsz, L = q.shape[0], q.shape[1]
    dv = v.shape[-1]
    C = min(CHUNK, L)
    n = -(-L // C)
    pad = n * C - L

    def prep(t):
        t = t.astype(jnp.float32)
        t = jnp.pad(t, [(0, 0), (0, pad)] + [(0, 0)] * (t.ndim - 2))
        t = t.reshape((bsz, n, C) + t.shape[2:])
        return jnp.moveaxis(t, 3, 1)

    q, k, v, beta, g = prep(q), prep(k), prep(v), prep(beta), prep(g)
    q = q * (HEAD_K ** -0.5)
    gc = jnp.cumsum(g, axis=-1)
    idx = jnp.arange(C)
    causal = idx[:, None] >= idx[None, :]
    strict = idx[:, None] > idx[None, :]
    decay = jnp.exp(jnp.where(causal, gc[..., :, None] - gc[..., None, :], -jnp.inf))
    kb = k * beta[..., None]
    a_mat = jnp.where(strict, jnp.einsum('bhncd,bhnsd->bhncs', kb, k) * decay, 0.0)
    a_mat = a_mat + jnp.eye(C, dtype=jnp.float32)
    rhs = jnp.concatenate([v * beta[..., None], kb * jnp.exp(gc)[..., None]], axis=-1)
    sol = lax.linalg.triangular_solve(a_mat, rhs, left_side=True, lower=True, unit_diagonal=True)
    u, w = sol[..., :dv], sol[..., dv:]
    attn_qk = jnp.einsum('bhncd,bhnsd->bhncs', q, k) * decay
    q_dec = q * jnp.exp(gc)[..., None]
    k_dec = k * jnp.exp(gc[..., -1:] - gc)[..., None]
    g_last = jnp.exp(gc[..., -1])

    def step(S, xs):
        u_c, w_c, qd_c, aqk_c, kd_c, gl_c = xs
        v_new = u_c - jnp.einsum('bhcd,bhde->bhce', w_c, S)
        o = jnp.einsum('bhcd,bhde->bhce', qd_c, S) + jnp.einsum('bhcs,bhse->bhce', aqk_c, v_new)
        S = S * gl_c[..., None, None] + jnp.einsum('bhcd,bhce->bhde', kd_c, v_new)
        return S, o

    xs = tuple(jnp.moveaxis(t, 2, 0) for t in (u, w, q_dec, attn_qk, k_dec, g_last))
    S, o = lax.scan(step, s0.astype(jnp.float32), xs)
    o = jnp.moveaxis(o, 0, 2).reshape(bsz, o.shape[2], n * C, dv)[:, :, :L]
    return jnp.transpose(o, (0, 2, 1, 3)), S


def mixer_layer(x, c, buf_a, buf_b, s0, ada_w, ada_b, norm_pre, w_in, conv_a_w, conv_b_w,
                a_log, dt_bias, onorm_w, w_out_a, w_out_b, w_o, norm_post):
    bsz, L, _ = x.shape
    mod = jax.nn.silu(c) @ ada_w + ada_b
    shift, scale, gate = jnp.split(mod, 3, axis=-1)
    h = rmsnorm(x, norm_pre) * (1.0 + scale[:, None]) + shift[:, None]
    proj = h @ w_in
    offs = [int(o) for o in np.cumsum(IN_SIZES)[:-1]]
    hA, cA, bA, zA, qkv, zB, beta_raw, alpha_raw, gA, gB = jnp.split(proj, offs, axis=-1)

    a_conv, new_buf_a = causal_dwconv(cA * hA, buf_a, conv_a_w)
    yA = ((bA * a_conv) * jax.nn.silu(zA)) @ w_out_a

    qkv_c, new_buf_b = causal_dwconv(qkv, buf_b, conv_b_w)
    qkv_c = jax.nn.silu(qkv_c)
    q, k, v = jnp.split(qkv_c, [D_K, 2 * D_K], axis=-1)
    q = l2norm(q.reshape(bsz, L, N_HEADS, HEAD_K))
    k = l2norm(k.reshape(bsz, L, N_HEADS, HEAD_K))
    v = v.reshape(bsz, L, N_HEADS, HEAD_V)
    beta = jax.nn.sigmoid(beta_raw.astype(jnp.float32))
    g = -jnp.exp(a_log.astype(jnp.float32)) * jax.nn.softplus(
        alpha_raw.astype(jnp.float32) + dt_bias.astype(jnp.float32))
    o, S = gated_delta_chunked(q, k, v, beta, g, s0)
    o = rmsnorm(o, onorm_w) * jax.nn.silu(zB.reshape(bsz, L, N_HEADS, HEAD_V).astype(jnp.float32))
    yB = o.reshape(bsz, L, D_V).astype(x.dtype) @ w_out_b

    merged = jax.nn.sigmoid(gA) * yA + jax.nn.sigmoid(gB) * yB
    post = rmsnorm(merged @ w_o, norm_post)
    y = (x + gate[:, None] * post).astype(x.dtype)
    return y, new_buf_a, new_buf_b, S


def setup_inputs(seed: int = 0) -> dict:
    key = jax.random.key(seed)
    ks = jax.random.split(key, 24)
    f32 = jnp.float32

    def nrm(k, shape, s):
        return jax.random.normal(k, shape, f32) * s

    dt = jnp.exp(jax.random.uniform(ks[14], (DEPTH, N_HEADS), f32, np.log(1e-3), np.log(1e-1)))
    dt_bias = dt + jnp.log(-jnp.expm1(-dt))
    return {
        "x_prompt": nrm(ks[0], (BATCH, SEQ, D_MODEL), 1.0),
        "x_sample": nrm(ks[1], (DEC_BATCH, DEC_SEQ, D_MODEL), 1.0),
        "c_prompt": nrm(ks[2], (BATCH, D_MODEL), 1.0),
        "c_sample": nrm(ks[3], (DEC_BATCH, D_MODEL), 1.0),
        "state_conv_a": nrm(ks[4], (DEPTH, DEC_BATCH, CONV_A_W - 1, D_CONV), 1.0),
        "state_conv_qkv": nrm(ks[5], (DEPTH, DEC_BATCH, CONV_B_W - 1, D_QKV), 1.0),
        "state_delta": nrm(ks[6], (DEPTH, DEC_BATCH, N_HEADS, HEAD_K, HEAD_V), 0.1),
        "ada_w": nrm(ks[7], (DEPTH, D_MODEL, 3 * D_MODEL), 0.5 * D_MODEL ** -0.5),
        "ada_b": nrm(ks[8], (DEPTH, 3 * D_MODEL), 0.01),
        "norm_pre": 1.0 + nrm(ks[9], (DEPTH, D_MODEL), 0.05),
        "w_in": nrm(ks[10], (DEPTH, D_MODEL, IN_COLS), D_MODEL ** -0.5),
        "conv_a_w": nrm(ks[11], (DEPTH, CONV_A_W, D_CONV), CONV_A_W ** -0.5),
        "conv_b_w": nrm(ks[12], (DEPTH, CONV_B_W, D_QKV), CONV_B_W ** -0.5),
        "a_log": jnp.log(jax.random.uniform(ks[13], (DEPTH, N_HEADS), f32, 1.0, 16.0)),
        "dt_bias": dt_bias,
        "onorm_w": 1.0 + nrm(ks[15], (DEPTH, HEAD_V), 0.05),
        "w_out_a": nrm(ks[16], (DEPTH, D_CONV, D_MODEL), D_CONV ** -0.5),
        "w_out_b": nrm(ks[17], (DEPTH, D_V, D_MODEL), D_V ** -0.5),
        "w_o": nrm(ks[18], (DEPTH, D_MODEL, D_MODEL), D_MODEL ** -0.5),
        "norm_post": 1.0 + nrm(ks[19], (DEPTH, D_MODEL), 0.05),
    }


def reference(x_prompt, x_sample, c_prompt, c_sample, state_conv_a, state_conv_qkv, state_delta,
              ada_w, ada_b, norm_pre, w_in, conv_a_w, conv_b_w, a_log, dt_bias, onorm_w,
              w_out_a, w_out_b, w_o, norm_post):
    yp, ys = x_prompt, x_sample
    bp = x_prompt.shape[0]
    pa, pb, pd, sa, sb, sd = [], [], [], [], [], []
    for l in range(DEPTH):
        weights = (ada_w[l], ada_b[l], norm_pre[l], w_in[l], conv_a_w[l], conv_b_w[l], a_log[l],
                   dt_bias[l], onorm_w[l], w_out_a[l], w_out_b[l], w_o[l], norm_post[l])
        zero_a = jnp.zeros((bp, CONV_A_W - 1, D_CONV), yp.dtype)
        zero_b = jnp.zeros((bp, CONV_B_W - 1, D_QKV), yp.dtype)
        zero_s = jnp.zeros((bp, N_HEADS, HEAD_K, HEAD_V), jnp.float32)
        yp, na, nb, ns = mixer_layer(yp, c_prompt, zero_a, zero_b, zero_s, *weights)
        pa.append(na)
        pb.append(nb)
        pd.append(ns.astype(state_delta.dtype))
        ys, ma, mb, ms = mixer_layer(ys, c_sample, state_conv_a[l], state_conv_qkv[l], state_delta[l], *weights)
        sa.append(ma)
        sb.append(mb)
        sd.append(ms.astype(state_delta.dtype))
    return (yp, ys, jnp.stack(pa), jnp.stack(pb), jnp.stack(pd), jnp.stack(sa), jnp.stack(sb), jnp.stack(sd))
```

```python
import numpy as np
from contextlib import ExitStack
import concourse.bass as bass
import concourse.mybir as mybir
from concourse.bass_utils import run_bass_kernel_spmd

F32 = mybir.dt.float32
BF16 = mybir.dt.bfloat16
ALU = mybir.AluOpType
AF = mybir.ActivationFunctionType
AX = mybir.AxisListType

NP_ = 2048
NS = 16
TT = NP_ + NS
D = 1024
INC = 10256
EPS = 1e-6
BIG = 60000.0
BLK = [(0, 512), (512, 512), (1024, 512), (1536, 512), (2048, 16)]
OFF_HA, OFF_CA, OFF_BA, OFF_ZA = 0, 1024, 2048, 3072
OFF_Q, OFF_K, OFF_V, OFF_ZB = 4096, 5120, 6144, 7168
OFF_BETA, OFF_ALPHA, OFF_GA, OFF_GB = 8192, 8200, 8208, 9232


class _Rec:
    def __init__(self):
        self.call = None

    def __getattr__(self, name):
        def f(*a, **kw):
            assert self.call is None
            self.call = (name, a, kw)
            return self
        return f


class Prog:
    ENGS = ("pe", "act", "dve", "pool", "sp")
    SEM_LIMIT = 30000

    def __init__(self, nc, stack):
        self.nc, self.stack = nc, stack
        self.ops = {e: [] for e in self.ENGS}
        self.eng_sem, self.eng_cnt = {}, {}
        self.waited = {e: {} for e in self.ENGS}
        self.writers, self.readers = {}, {}
        self.dma_sem, self.dma_cnt = {}, {}
        self.all_sems = {}
        self.nsem = 0
        for e in self.ENGS:
            self._new_eng_sem(e)

    def _sem(self, name):
        self.nsem += 1
        return self.stack.enter_context(self.nc.semaphore(name))

    def _new_eng_sem(self, e):
        self.eng_sem[e] = self._sem(f"s_{e}_{self.nsem}")
        self.eng_cnt[e] = 0

    @staticmethod
    def _bank(key):
        if isinstance(key, str) and len(key) >= 2 and key[0] == "B" and key[1].isdigit():
            return key[:2]
        return None

    def _need(self, eng, tok, waits, raw):
        sem, val, teng = tok
        if teng == eng and (not raw or eng == "pe"):
            return
        w = self.waited[eng]
        if w.get(id(sem), 0) < val:
            w[id(sem)] = val
            waits[id(sem)] = (sem, val)

    def op(self, eng, fn, reads=(), writes=(), dma=False):
        waits = {}
        reads = [self._bank(b) or b for b in reads]
        writes = [self._bank(b) or b for b in writes]
        for b in reads:
            for tok in self.writers.get(b, {}).values():
                self._need(eng, tok, waits, True)
            if self._bank(b):
                for tok in self.readers.get(b, {}).values():
                    self._need(eng, tok, waits, False)
        for b in writes:
            for tok in self.writers.get(b, {}).values():
                self._need(eng, tok, waits, True)
            for tok in self.readers.get(b, {}).values():
                self._need(eng, tok, waits, False)
        if dma:
            key = writes[0] if writes else ("rd", reads[0])
            if key not in self.dma_sem:
                self.dma_sem[key] = self._sem(f"d{self.nsem}")
                self.dma_cnt[key] = 0
            self.dma_cnt[key] += 16
            tok = (self.dma_sem[key], self.dma_cnt[key], "dma")
            inc = (tok[0], 16)
        else:
            if self.eng_cnt[eng] >= self.SEM_LIMIT:
                self._new_eng_sem(eng)
            self.eng_cnt[eng] += 1
            tok = (self.eng_sem[eng], self.eng_cnt[eng], eng)
            inc = (tok[0], 1)
        self.all_sems[id(tok[0])] = tok
        rec = _Rec()
        fn(rec)
        assert rec.call is not None
        self.ops[eng].append((list(waits.values()), rec.call, inc))
        for b in reads:
            self.readers.setdefault(b, {})[id(tok[0])] = tok
        for b in writes:
            self.writers.setdefault(b, {})[id(tok[0])] = tok
        return tok

    def fence(self, engs=None):
        for e in (engs or self.ENGS):
            waits = {}
            for tok in self.all_sems.values():
                self._need(e, tok, waits, True)
            if waits:
                self.ops[e].append((list(waits.values()), None, None))

    def emit(self):
        with self.nc.Block() as block:
            def mk(ename):
                def body(eng):
                    for waits, fn, inc in self.ops[ename]:
                        for sem, val in waits:
                            eng.wait_ge(sem, val)
                        if fn is not None:
                            name, a, kw = fn
                            getattr(eng, name)(*a, **kw).then_inc(inc[0], inc[1])
                return body
            block.tensor(mk("pe"))
            block.scalar(mk("act"))
            block.vector(mk("dve"))
            block.gpsimd(mk("pool"))
            block.sync(mk("sp"))


class Arena:
    def __init__(self, tile, ncols):
        self.tile, self.ncols, self.off = tile, ncols, 0

    def reset(self):
        self.off = 0

    def alloc(self, cols):
        cols = (cols + 1) // 2 * 2
        assert self.off + cols <= self.ncols, ("arena overflow", self.off, cols, self.ncols)
        ap = self.tile[:, self.off:self.off + cols]
        self.off += cols
        return ap


class _Stop(Exception):
    pass


def build_nc(stop=99):
    nc = bass.Bass("TRN2", target_bir_lowering=False)

    def ckpt(k):
        if stop <= k:
            raise _Stop()

    def din(name, shape, dt=F32):
        return nc.dram_tensor(name, list(shape), dt, kind="ExternalInput").ap()

    def dout(name, shape, dt=F32):
        return nc.dram_tensor(name, list(shape), dt, kind="ExternalOutput").ap()

    xp = din("xp", [NP_, D]); xsm = din("xsm", [NS, D])
    cp_bc = din("cp_bc", [128, D]); cs = din("cs", [NS, D])
    cbufa_d = din("cbufa", [128, 8 * 2 * NS]); cbufq_d = din("cbufq", [128, 24 * 3 * NS])
    sd = din("sd", [NS, 8, 128, 128])
    ada_w = din("ada_w", [D, 3 * D]); adab_bc_d = din("adab_bc", [128, D]); npost_bc_d = din("npost_bc", [128, D])
    vecs_d = din("vecs", [128, 160]); hv_d = din("hv", [8, 2])
    w_in = din("w_in", [D, INC]); w_out_a = din("w_out_a", [D, D]); w_out_b = din("w_out_b", [D, D]); w_o = din("w_o", [D, D])
    ident_d = din("ident", [128, 128]); biasA_d = din("biasA", [128, 128]); biasB_d = din("biasB", [128, 128])
    negmA_d = din("negmA", [128, 512]); sel_d = din("sel", [8, 8 * 128]); bm_d = din("bm", [8, 128])
    mks_d = din("mks", [128, 7 * 128])

    y_p = dout("y_p", [NP_, D]); y_s = dout("y_s", [NS, D])
    tail_a_p_d = dout("tail_a_p", [128, 8 * 2]); tail_a_s_d = dout("tail_a_s", [128, 8 * 2 * NS])
    tail_q_p_d = dout("tail_q_p", [128, 24 * 3]); tail_q_s_d = dout("tail_q_s", [128, 24 * 3 * NS])
    S_p_d = dout("S_p", [8, 128, 128]); S_s_d = dout("S_s", [NS, 8, 128, 128])

    def _body(st, P):

        def sb(name, shape, dt=F32):
            return st.enter_context(nc.sbuf_tensor("sb_" + name, list(shape), dt))

        def psb(name, shape, dt=F32):
            return st.enter_context(nc.psum_tensor("ps_" + name, list(shape), dt))

        hT = sb("hT", [128, 8, TT], BF16)
        XBs = sb("XBs", [128, 8, NS], BF16)
        xbs_d = nc.dram_tensor("xbs_scratch", [8, 128, TT], BF16).ap()
        wts = [sb(f"wt{i}", [128, 8, 512], BF16) for i in range(2)]
        ident = sb("ident", [128, 128]); identb = sb("identb", [128, 128], BF16)
        ones_f = sb("ones_f", [128, 128]); ones_b = sb("ones_b", [128, 128], BF16)
        biasA = sb("biasA", [128, 128]); biasB = sb("biasB", [128, 128])
        negmA = sb("negmA", [128, 512]); mkb = sb("mkb", [128, 7, 512], BF16); identb2 = sb("identb2", [128, 512], BF16)
        sel = sb("sel", [8, 8 * 128]); bm = sb("bm", [8, 128])
        vecs = sb("vecs", [128, 160]); hv = sb("hv", [8, 2])
        modfm = sb("modfm", [128, 16, 17])
        gn = sb("gn", [128, D]); gns = sb("gns", [NS, D])
        a_p = sb("a_p", [128, 8]); A_s = sb("A_s", [128, 8, NS])
        cbufa = sb("cbufa", [128, 8, 2, NS]); cbufq = sb("cbufq", [128, 24, 3, NS])
        tail_a_p = sb("tail_a_p", [128, 8, 2]); tail_a_s = sb("tail_a_s", [128, 8, 2, NS])
        tail_q_p = sb("tail_q_p", [128, 24, 3]); tail_q_s = sb("tail_q_s", [128, 24, 3, NS])
        gc_tok = sb("gc_tok", [128, 128]); sb_tok = sb("sb_tok", [128, 128])
        egc = sb("egc", [128, 128]); ekd = sb("ekd", [128, 128]); egl = sb("egl", [128, 128])
        gcT_full = sb("gcT", [128, NP_]); gcT = gcT_full[0:8, :]; beta_s = sb("beta_s", [8, NS]); g_s = sb("g_s", [8, NS])
        qs_s = sb("qs_s", [128, 8, NS]); ks_s = sb("ks_s", [128, 8, NS]); vs_s = sb("vs_s", [128, 8, NS])
        ARENA_F = 27800
        arena_t = sb("arena", [128, ARENA_F])
        arena = Arena(arena_t, ARENA_F)

        def af(cols):
            return arena.alloc(cols)

        def ab(cols):
            return arena.alloc((cols + 1) // 2).bitcast(BF16)

        B = [psb(f"B{i}", [128, 512]) for i in range(5)]
        B5 = psb("B5", [128, 1024], BF16); B6 = psb("B6", [128, 1024], BF16)
        B7 = psb("B7", [128, 512])

        def dve(fn, r, w): return P.op("dve", fn, reads=r, writes=w)
        def act(fn, r, w): return P.op("act", fn, reads=r, writes=w)
        def pool(fn, r, w): return P.op("pool", fn, reads=r, writes=w)
        def pe(fn, r, w): return P.op("pe", fn, reads=r, writes=w)
        def dma(fn, r, w, eng="sp"): return P.op(eng, fn, reads=r, writes=w, dma=True)

        def rsqrt_small(out_ap, in_ap, scale, eps, rk, wk, tmp_ap, tmpk):
            dve(lambda e: e.tensor_scalar(out=tmp_ap, in0=in_ap, scalar1=scale, scalar2=eps, op0=ALU.mult, op1=ALU.add), rk, [tmpk])
            act(lambda e: e.activation(out=tmp_ap, in_=tmp_ap, func=AF.Sqrt), [tmpk], [tmpk])
            dve(lambda e: e.reciprocal(out=out_ap, in_=tmp_ap), [tmpk], wk)

        wstate = {"n": 0}

        def load_w(dram, col_offs, width):
            i = wstate["n"] % 2
            wstate["n"] += 1
            wt = wts[i]
            key = f"wt{i}"
            runs = []
            for j, c in enumerate(col_offs):
                if runs and runs[-1][1] + runs[-1][2] == c:
                    runs[-1][2] += width
                else:
                    runs.append([j * width, c, width])
            for dst, c, wd in runs:
                src = dram[:, c:c + wd].rearrange("(k p) n -> p k n", p=128)
                dma(lambda e, dst=dst, wd=wd, src=src: e.dma_start(out=wt[:, :, dst:dst + wd], in_=src),
                    [], [key], eng="pool")
            return wt, key

        wq = {}

        def get_w(tag, specs, i):
            for j in (i, i + 1):
                if j < len(specs) and (tag, j) not in wq:
                    wq[(tag, j)] = load_w(*specs[j])
            return wq[(tag, i)]

        pj = {"n": 0}

        def proj_fm(wt, wkey, j, width, evac, blocks=BLK, rhs=None, rkey="hT", M=128, bankset=(0, 1)):
            rhs = hT if rhs is None else rhs
            for nb, (t0, n) in enumerate(blocks):
                bi = bankset[pj["n"] % 2]
                pj["n"] += 1
                bank, bkey = B[bi], f"B{bi}"
                for k in range(8):
                    pe(lambda e, k=k, bank=bank, t0=t0, n=n: e.matmul(
                        out=bank[0:M, 0:n], lhsT=wt[:, k, j * width:j * width + M], rhs=rhs[:, k, t0:t0 + n],
                        start=(k == 0), stop=(k == 7)), [wkey, rkey], [bkey])
                evac(nb, bank[0:M, 0:n], t0, n, bkey)

        for t, d, k in [(ident, ident_d, "ident"), (biasA, biasA_d, "biasA"), (biasB, biasB_d, "biasB"),
                        (negmA, negmA_d, "negmA"), (sel, sel_d, "sel"), (bm, bm_d, "bm"), (vecs, vecs_d, "vecs"),
                        (hv, hv_d, "hv"), (gn, npost_bc_d, "gn")]:
            dma(lambda e, t=t, d=d: e.dma_start(out=t[:], in_=d), [], [k])
        dma(lambda e: e.dma_start(out=cbufa[:].rearrange("p a b c -> p (a b c)"), in_=cbufa_d), [], ["cbufa"])
        dma(lambda e: e.dma_start(out=cbufq[:].rearrange("p a b c -> p (a b c)"), in_=cbufq_d), [], ["cbufq"])
        pool(lambda e: e.memset(ones_f[:], 1.0), [], ["ones_f"])
        pool(lambda e: e.memset(ones_b[:], 1.0), [], ["ones_b"])
        dve(lambda e: e.tensor_copy(out=identb[:], in_=ident[:]), ["ident"], ["identb"])
        for c in range(4):
            dve(lambda e, c=c: e.tensor_copy(out=identb2[:, c * 128:(c + 1) * 128], in_=ident[:]), ["ident"], ["identb2"])
            dma(lambda e, c=c: e.dma_start(out=mkb[:, :, c * 128:(c + 1) * 128], in_=mks_d.rearrange("p (l n) -> p l n", l=7)), [], ["mkb"], eng="pool")
        V_ADAB, V_NPRE, V_CA, V_ON, V_CB = 0, 24, 32, 56, 64

        ckpt(0.1)
        arena.reset()
        cpt = af(D); cst = af(D); gt = af(D)
        dma(lambda e: e.dma_start(out=gt, in_=adab_bc_d), [], ["gt"])
        scTp = ab(8 * 128).rearrange("p (k n) -> p k n", k=8)
        sc17 = ab(8 * 17).rearrange("p (k n) -> p k n", k=8)
        dma(lambda e: e.dma_start(out=cpt, in_=cp_bc), [], ["cpt"])
        dma(lambda e: e.dma_start(out=cst[0:NS, :], in_=cs), [], ["cst"])
        act(lambda e: e.activation(out=cpt, in_=cpt, func=AF.Silu), ["cpt"], ["cpt"])
        act(lambda e: e.activation(out=cst[0:NS, :], in_=cst[0:NS, :], func=AF.Silu), ["cst"], ["cst"])
        for half in range(2):
            bank, bkey = B[2 + half], f"B{2 + half}"
            for kk in range(4):
                k = half * 4 + kk
                pe(lambda e, k=k, kk=kk, bank=bank: e.transpose(out=bank[:, kk * 128:(kk + 1) * 128], in_=cpt[:, k * 128:(k + 1) * 128],
                                                              identity=ident[:]), ["cpt", "ident"], [bkey])
            dve(lambda e, half=half, bank=bank: e.tensor_copy(out=scTp[:, half * 4:half * 4 + 4, :],
                                                              in_=bank[:, :].rearrange("p (k n) -> p k n", k=4)), [bkey], ["scTp"])
        for k in range(8):
            pe(lambda e, k=k: e.transpose(out=B[4][:, k * NS:(k + 1) * NS], in_=cst[0:NS, k * 128:(k + 1) * 128],
                                          identity=ident[0:NS, 0:NS]), ["cst", "ident"], ["B4"])
        dve(lambda e: e.tensor_copy(out=sc17[:, :, 1:17], in_=B[4][:, 0:8 * NS].rearrange("p (k n) -> p k n", k=8)), ["B4"], ["sc17"])
        dve(lambda e: e.tensor_copy(out=sc17[:, :, 0:1], in_=scTp[:, :, 0:1]), ["scTp"], ["sc17"])
        ckpt(0.2)
        for g in range(6):
            wt, wkey = get_w("ada", [(ada_w, [gg * 512], 512) for gg in range(6)], g)
            if g < 4:
                for j in range(4):
                    fc = g * 4 + j
                    for k in range(8):
                        pe(lambda e, k=k, j=j, wt=wt: e.matmul(out=B[2][:, 0:17], lhsT=wt[:, k, j * 128:(j + 1) * 128], rhs=sc17[:, k, :],
                                                               start=(k == 0), stop=(k == 7)), [wkey, "sc17"], ["B2"])
                    act(lambda e, fc=fc: e.activation(out=modfm[:, fc, :], in_=B[2][:, 0:17], func=AF.Identity,
                                                      bias=vecs[:, V_ADAB + fc:V_ADAB + fc + 1], scale=1.0), ["B2", "vecs"], ["modfm"])
            else:
                n0 = (g - 4) * 512
                for k in range(8):
                    pe(lambda e, k=k, wt=wt: e.matmul(out=B[3][0:NS, :], lhsT=sc17[:, k, 1:17], rhs=wt[:, k, :],
                                                      start=(k == 0), stop=(k == 7)), [wkey, "sc17"], ["B3"])
                dve(lambda e, n0=n0: e.tensor_tensor(out=gns[:, n0:n0 + 512], in0=B[3][0:NS, :], in1=gt[0:NS, n0:n0 + 512], op=ALU.add),
                    ["B3", "gt"], ["gns"])
                dve(lambda e, n0=n0: e.tensor_tensor(out=gns[:, n0:n0 + 512], in0=gns[:, n0:n0 + 512], in1=gn[0:NS, n0:n0 + 512], op=ALU.mult),
                    ["gns", "gn"], ["gns"])
                for k in range(8):
                    pe(lambda e, k=k, wt=wt: e.matmul(out=B[4][:, :], lhsT=scTp[:, k, :], rhs=wt[:, k, :],
                                                      start=(k == 0), stop=(k == 7)), [wkey, "scTp"], ["B4"])
                dve(lambda e, n0=n0: e.tensor_tensor(out=gt[:, n0:n0 + 512], in0=B[4][:, :], in1=gt[:, n0:n0 + 512], op=ALU.add),
                    ["B4", "gt", "gns"], ["gt"])
                dve(lambda e, n0=n0: e.tensor_tensor(out=gn[:, n0:n0 + 512], in0=gn[:, n0:n0 + 512], in1=gt[:, n0:n0 + 512], op=ALU.mult),
                    ["gt", "gn", "gns"], ["gn"])
        ckpt(0.3)
        dve(lambda e: e.tensor_scalar(out=a_p[:], in0=modfm[:, 8:16, 0], scalar1=1.0, scalar2=None, op0=ALU.add), ["modfm"], ["a_p"])
        dve(lambda e: e.tensor_tensor(out=a_p[:], in0=a_p[:], in1=vecs[:, V_NPRE:V_NPRE + 8], op=ALU.mult), ["a_p", "vecs"], ["a_p"])
        ckpt(0.4)
        dve(lambda e: e.tensor_scalar(out=A_s[:], in0=modfm[:, 8:16, 1:17], scalar1=1.0, scalar2=None, op0=ALU.add), ["modfm"], ["A_s"])
        ckpt(0.5)
        for k in range(8):
            dve(lambda e, k=k: e.tensor_scalar(out=A_s[:, k, :], in0=A_s[:, k, :], scalar1=vecs[:, V_NPRE + k:V_NPRE + k + 1], scalar2=None,
                                               op0=ALU.mult), ["A_s", "vecs"], ["A_s"])

        ckpt(1)
        P.fence(); arena.reset()
        xt = [af(D), af(D)]
        xh = [af(D), af(D)]
        junk = ab(D)
        ssx = af(18); rstdx = af(18); tmpx = af(18)
        tmps = af(8 * NS)

        def p1_stage1(i):
            rows = 128 if i < 16 else NS
            xi, xk = xt[i % 2], f"xt{i % 2}"
            hi, hk = xh[i % 2], f"xh{i % 2}"
            src = xp[i * 128:(i + 1) * 128, :] if i < 16 else xsm
            dma(lambda e: e.dma_start(out=xi[0:rows, :], in_=src), [], [xk])
            act(lambda e: e.activation(out=junk[0:rows, :], in_=xi[0:rows, :], func=AF.Square, accum_out=ssx[0:rows, i:i + 1]),
                [xk], ["junk", f"ssx{i}"])
            rsqrt_small(rstdx[0:rows, i:i + 1], ssx[0:rows, i:i + 1], 1.0 / D, EPS, [f"ssx{i}"], [f"rstdx{i}"], tmpx[0:rows, i:i + 1], f"tmpx{i}")
            dve(lambda e: e.tensor_scalar(out=hi[0:rows, :], in0=xi[0:rows, :], scalar1=rstdx[0:rows, i:i + 1], scalar2=None, op0=ALU.mult),
                [xk, f"rstdx{i}"], [hk])

        def p1_stage2(i):
            hi, hk = xh[i % 2], f"xh{i % 2}"
            pair = [(B[2], "B2"), (B[3], "B3")] if i % 2 == 0 else [(B[4], "B4"), (B[1], "B1")]
            if i < 16:
                for half in range(2):
                    bank, bkey = pair[half]
                    for kk in range(4):
                        k = half * 4 + kk
                        pe(lambda e, k=k, kk=kk, bank=bank: e.transpose(out=bank[:, kk * 128:(kk + 1) * 128], in_=hi[:, k * 128:(k + 1) * 128],
                                                                     identity=ident[:]), [hk, "ident"], [bkey])
                    for kk in range(4):
                        k = half * 4 + kk
                        if kk % 2 == 0:
                            act(lambda e, k=k, kk=kk, bank=bank: e.activation(
                                out=hT[:, k, i * 128:(i + 1) * 128], in_=bank[:, kk * 128:(kk + 1) * 128], func=AF.Identity,
                                bias=modfm[:, k, 0:1], scale=a_p[:, k:k + 1]), [bkey, "modfm", "a_p"], [f"hT{i // 4}"])
                        else:
                            dve(lambda e, k=k, kk=kk, bank=bank: e.tensor_scalar(
                                out=hT[:, k, i * 128:(i + 1) * 128], in0=bank[:, kk * 128:(kk + 1) * 128], scalar1=a_p[:, k:k + 1],
                                scalar2=modfm[:, k, 0:1], op0=ALU.mult, op1=ALU.add), [bkey, "modfm", "a_p"], [f"hT{i // 4}"])
            else:
                bank, bkey = pair[0]
                for k in range(8):
                    pe(lambda e, k=k: e.transpose(out=bank[:, k * NS:(k + 1) * NS], in_=hi[0:NS, k * 128:(k + 1) * 128],
                                                  identity=ident[0:NS, 0:NS]), [hk, "ident"], [bkey])
                t3 = tmps.rearrange("p (k n) -> p k n", k=8)
                dve(lambda e: e.tensor_tensor(out=t3, in0=bank[:, 0:8 * NS].rearrange("p (k n) -> p k n", k=8), in1=A_s[:], op=ALU.mult),
                    [bkey, "A_s"], ["tmps"])
                dve(lambda e: e.tensor_tensor(out=hT[:, :, NP_:TT], in0=t3, in1=modfm[:, 0:8, 1:17], op=ALU.add), ["tmps", "modfm"], ["hT4"])

        p1_stage1(0)
        p1_stage1(1)
        for i in range(17):
            p1_stage2(i)
            if i + 2 < 17:
                p1_stage1(i + 2)

        ckpt(2)
        ckpt(3)
        def col(t, c, h):
            return t[:, c * 8 + h:c * 8 + h + 1]

        P.fence(); arena.reset()
        GC = 4
        GW = GC * 128
        NG = 16 // GC
        qTs = [ab(TT), ab(TT)]; kTs = [ab(TT), ab(TT)]; vTs = [ab(TT), ab(TT)]; szBs = [ab(TT), ab(TT)]
        pre = af(3 + TT); acc = af(TT); sqs = ab(NP_)
        hsc = [af(16 * 8), af(16 * 8)]
        o_alias = arena.off
        setT = []
        for s_ in range(2):
            setT.append(dict(
                Ktok=ab(GW), Kg=ab(GW), KpT=ab(GW), Dm=af(GW), Dm2=af(GW), En=af(GW),
                Mb=ab(GW), Nb=ab(GW), D=ab(GW), Xp=ab(GW), Xb=ab(GW),
                Tb=ab(GW), nW=ab(GW), Vs=ab(GW), Kd=ab(GW), at=ab(GW),
                oraw=af(GW), og=ab(GW), xo=ab(GW), sso=af(GC), ro=af(GC),
                T=(B5 if s_ == 0 else B6), Tk=("B5" if s_ == 0 else "B6"),
                X=B[1 + 2 * s_], Xk=f"B{1 + 2 * s_}", Y=B[2 + 2 * s_], Yk=f"B{2 + 2 * s_}"))
        S = af(128); Sb = ab(128); vnb = ab(128); o1s = af(128); junk2 = ab(128)
        dve(lambda e: e.memset(pre[:, 0:3], 0.0), [], ["pre"])
        done = set()
        o_end = arena.off
        arena.off = o_alias
        xg = af(TT); axg = af(TT); lg = axg; negA = af(2)
        betaT = af(TT)[0:8, :]; gT = af(TT)[0:8, :]; sbT = af(NP_)[0:8, :]
        tmpl = af(128)
        assert arena.off <= o_end, (arena.off, o_end)
        arena.off = o_end

        def wait_for(key):
            while key not in done:
                yield "blocked"

        def rsqrt_ln(out_ap, in_ap, scale, eps, rk, wk, tmp_ap, tmpk):
            dve(lambda e: e.tensor_scalar(out=tmp_ap, in0=in_ap, scalar1=scale, scalar2=eps, op0=ALU.mult, op1=ALU.add), rk, [tmpk])
            act(lambda e: e.activation(out=tmp_ap, in_=tmp_ap, func=AF.Ln), [tmpk], [tmpk])
            act(lambda e: e.activation(out=out_ap, in_=tmp_ap, func=AF.Exp, scale=-0.5), [tmpk], wk)

        def conv_task(h):
            hb = h % 2
            qT, kT, vT, szB = qTs[hb], kTs[hb], vTs[hb], szBs[hb]
            HS = hsc[hb]
            ssq, rqk, fsc, fg, fd, sbh, rqe, tmpq = (HS[:, 0:32], HS[:, 32:64], HS[:, 64:80], HS[:, 80:96], HS[:, 96:112],
                                                     HS[:, 112:128], None, None)
            wt, wkey = get_w("head", [(w_in, [OFF_Q + hh * 128, OFF_K + hh * 128, OFF_V + hh * 128, OFF_ZB + hh * 128], 128) for hh in range(8)], h)
            for j, (dst, dname, dsamp) in enumerate([(qT, f"qT{hb}", qs_s), (kT, f"kT{hb}", ks_s), (vT, f"vT{hb}", vs_s)]):
                ch = j * 8 + h
                for nb, (t0, n) in enumerate(BLK):
                    for k in range(8):
                        pe(lambda e, k=k, t0=t0, n=n: e.matmul(out=B[0][:, 0:n], lhsT=wt[:, k, j * 128:(j + 1) * 128], rhs=hT[:, k, t0:t0 + n],
                                                              start=(k == 0), stop=(k == 7)), [wkey, "hT"], ["B0"])
                        if k == 3 and n > 16:
                            yield
                    if nb % 2 == 0:
                        act(lambda e, t0=t0, n=n: e.activation(out=pre[:, 3 + t0:3 + t0 + n], in_=B[0][:, 0:n], func=AF.Copy), ["B0"], ["pre"])
                    else:
                        dve(lambda e, t0=t0, n=n: e.tensor_copy(out=pre[:, 3 + t0:3 + t0 + n], in_=B[0][:, 0:n]), ["B0"], ["pre"])
                    yield
                pool(lambda e, ch=ch: e.tensor_copy(out=tail_q_p[:, ch, :], in_=pre[:, NP_:NP_ + 3]), ["pre"], ["tail_q_p"])
                pool(lambda e, ch=ch: e.tensor_copy(out=tail_q_s[:, ch, 0:2, :], in_=cbufq[:, ch, 1:3, :]), ["cbufq"], ["tail_q_s"])
                pool(lambda e, ch=ch: e.tensor_copy(out=tail_q_s[:, ch, 2, :], in_=pre[:, 3 + NP_:3 + TT]), ["pre"], ["tail_q_s"])
                wc = [vecs[:, V_CB + t * 24 + ch:V_CB + t * 24 + ch + 1] for t in range(4)]
                HB = NP_ // 2
                for hf in range(2):
                    c0 = hf * HB
                    pool(lambda e, wc=wc, c0=c0: e.tensor_scalar(out=acc[:, c0:c0 + HB], in0=pre[:, c0:c0 + HB], scalar1=wc[0], scalar2=0.0,
                                                                 op0=ALU.mult, op1=ALU.add), ["pre", "vecs"], [f"acc{hf}"])
                    yield
                for t in range(1, 4):
                    for hf in range(2):
                        c0 = hf * HB
                        dve(lambda e, t=t, wc=wc, c0=c0: e.scalar_tensor_tensor(out=acc[:, c0:c0 + HB], in0=pre[:, c0 + t:c0 + t + HB], scalar=wc[t],
                                                                                in1=acc[:, c0:c0 + HB], op0=ALU.mult, op1=ALU.add),
                            ["pre", f"acc{hf}", "vecs"], [f"acc{hf}"])
                        yield
                pool(lambda e, wc=wc: e.tensor_scalar(out=acc[:, NP_:TT], in0=pre[:, 3 + NP_:3 + TT], scalar1=wc[3], scalar2=0.0, op0=ALU.mult, op1=ALU.add),
                     ["pre", "vecs"], ["acc"])
                for t in range(3):
                    dve(lambda e, t=t, wc=wc, ch=ch: e.scalar_tensor_tensor(out=acc[:, NP_:TT], in0=cbufq[:, ch, t, :], scalar=wc[t], in1=acc[:, NP_:TT],
                                                                            op0=ALU.mult, op1=ALU.add), ["cbufq", "acc", "vecs"], ["acc"])
                for hf in range(2):
                    c0 = hf * HB
                    act(lambda e, dst=dst, c0=c0: e.activation(out=dst[:, c0:c0 + HB], in_=acc[:, c0:c0 + HB], func=AF.Silu), [f"acc{hf}"], [dname])
                    yield
                act(lambda e, dsamp=dsamp: e.activation(out=dsamp[:, h, :], in_=acc[:, NP_:TT], func=AF.Silu), ["acc"], [dname + "_s"])
                yield
            for nb, (t0, n) in enumerate(BLK):
                for k in range(8):
                    pe(lambda e, k=k, t0=t0, n=n: e.matmul(out=B[0][:, 0:n], lhsT=wt[:, k, 384:512], rhs=hT[:, k, t0:t0 + n],
                                                          start=(k == 0), stop=(k == 7)), [wkey, "hT"], ["B0"])
                    if k == 3 and n > 16:
                        yield
                act(lambda e, t0=t0, n=n: e.activation(out=szB[:, t0:t0 + n], in_=B[0][:, 0:n], func=AF.Silu), ["B0"], [f"szB{hb}"])
                yield
            dve(lambda e: e.tensor_copy(out=XBs[:, h, :], in_=szB[:, NP_:TT]), [f"szB{hb}"], ["XBs"])
            for qi, (src, sname) in enumerate([(qT, f"qT{hb}"), (kT, f"kT{hb}")]):
                act(lambda e, src=src: e.activation(out=sqs, in_=src[:, 0:NP_], func=AF.Square), [sname], ["sqs"])
                for c in range(16):
                    pe(lambda e, c=c, qi=qi: e.matmul(out=B[0][:, qi * 16 + c:qi * 16 + c + 1], lhsT=sqs[:, c * 128:(c + 1) * 128],
                                                      rhs=ones_b[:, 0:1], start=True, stop=True), ["sqs", "ones_b"], ["B0"])
                yield
            hk = f"hsc{hb}"
            dve(lambda e: e.tensor_copy(out=ssq, in_=B[0][:, 0:32]), ["B0"], [hk])
            rsqrt_ln(rqk, ssq, 1.0, EPS, [hk], [hk], ssq, hk)
            dve(lambda e: e.tensor_scalar(out=rqk[:, 0:16], in0=rqk[:, 0:16], scalar1=128.0 ** -0.5, scalar2=None, op0=ALU.mult), [hk], [hk])
            sbv = sb_tok[:, :].rearrange("p (c h) -> p c h", c=16)[:, :, h]
            egv = egc[:, :].rearrange("p (c h) -> p c h", c=16)[:, :, h]
            ekv = ekd[:, :].rearrange("p (c h) -> p c h", c=16)[:, :, h]
            dve(lambda e: e.tensor_copy(out=sbh, in_=sbv), ["sb_tok"], [hk])
            dve(lambda e: e.tensor_tensor(out=fsc, in0=rqk[:, 16:32], in1=sbv, op=ALU.mult), [hk, "sb_tok"], [hk])
            dve(lambda e: e.tensor_tensor(out=fg, in0=fsc, in1=egv, op=ALU.mult), [hk, "egc"], [hk])
            dve(lambda e: e.tensor_tensor(out=fd, in0=fsc, in1=ekv, op=ALU.mult), [hk, "ekd"], [hk])
            dve(lambda e: e.tensor_tensor(out=ssq[:, 0:16], in0=rqk[:, 0:16], in1=egv, op=ALU.mult), [hk, "egc"], [hk])
            done.add(("conv", h))
            yield

        def prep_task(h, g):
            hb = h % 2
            s_ = g % 2
            Z = setT[s_]
            qT, kT, vT = qTs[hb], kTs[hb], vTs[hb]
            HS = hsc[hb]
            fsc, fg, fd, sbh = HS[:, 64:80], HS[:, 80:96], HS[:, 96:112], HS[:, 112:128]
            hk = f"hsc{hb}"
            T, Tk, X, Xk, Y, Yk = Z["T"], Z["Tk"], Z["X"], Z["Xk"], Z["Y"], Z["Yk"]
            K = lambda n: f"{n}{s_}"
            g0 = g * GW
            pe(lambda e: e.matmul(out=X[:, 0:GW], lhsT=sel[:, h * 128:(h + 1) * 128], rhs=gcT[:, g0:g0 + GW], start=True, stop=True),
               ["sel", "gcT"], [Xk])
            for cc in range(GC):
                c0 = g0 + cc * 128
                pe(lambda e, cc=cc, c0=c0: e.transpose(out=T[:, cc * 128:(cc + 1) * 128], in_=kT[:, c0:c0 + 128], identity=identb[:]),
                   [f"kT{hb}", "identb"], [Tk])
                pe(lambda e, cc=cc, c0=c0: e.transpose(out=T[:, GW + cc * 128:GW + (cc + 1) * 128], in_=vT[:, c0:c0 + 128], identity=identb[:]),
                   [f"vT{hb}", "identb"], [Tk])
            yield
            for cc in range(GC):
                c = g * GC + cc
                sl = slice(cc * 128, (cc + 1) * 128)
                vsl = slice(GW + cc * 128, GW + (cc + 1) * 128)
                dve(lambda e, sl=sl, c=c: e.tensor_scalar(out=Z["Ktok"][:, sl], in0=T[:, sl], scalar1=fsc[:, c:c + 1], scalar2=None, op0=ALU.mult),
                    [Tk, hk], [K("Ktok")])
                dve(lambda e, sl=sl, c=c: e.tensor_scalar(out=Z["Kg"][:, sl], in0=T[:, sl], scalar1=fg[:, c:c + 1], scalar2=None, op0=ALU.mult),
                    [Tk, hk], [K("Kg")])
                act(lambda e, sl=sl, c=c: e.activation(out=Z["Kd"][:, sl], in_=T[:, sl], func=AF.Identity, scale=fd[:, c:c + 1]),
                    [Tk, hk], [K("Kd")])
                act(lambda e, sl=sl, vsl=vsl, c=c: e.activation(out=Z["Vs"][:, sl], in_=T[:, vsl], func=AF.Identity, scale=sbh[:, c:c + 1]),
                    [Tk, hk], [K("Vs")])
                dve(lambda e, sl=sl, c=c: e.scalar_tensor_tensor(out=Z["Dm"][:, sl], in0=X[:, sl], scalar=col(gc_tok, c, h), in1=biasA[:],
                                                                 op0=ALU.subtract, op1=ALU.add), [Xk, "gc_tok", "biasA"], [K("Dm")])
                dve(lambda e, sl=sl, c=c: e.scalar_tensor_tensor(out=Z["Dm2"][:, sl], in0=X[:, sl], scalar=col(gc_tok, c, h), in1=biasB[:],
                                                                 op0=ALU.subtract, op1=ALU.add), [Xk, "gc_tok", "biasB"], [K("Dm2")])
            yield
            for cc in range(GC):
                sl = slice(cc * 128, (cc + 1) * 128)
                pe(lambda e, sl=sl, cc=cc: e.transpose(out=T[:, cc * 128:(cc + 1) * 128], in_=Z["Ktok"][:, sl], identity=identb[:]),
                   [K("Ktok"), "identb"], [Tk])
            yield
            act(lambda e: e.activation(out=Z["KpT"], in_=T[:, 0:GW], func=AF.Copy), [Tk], [K("KpT")])
            act(lambda e: e.activation(out=Z["Dm"], in_=Z["Dm"], func=AF.Exp), [K("Dm")], [K("Dm")])
            act(lambda e: e.activation(out=Z["Dm2"], in_=Z["Dm2"], func=AF.Exp, scale=-1.0), [K("Dm2")], [K("Dm2")])
            dve(lambda e: e.tensor_tensor(out=Z["En"], in0=Z["Dm"], in1=negmA[:, 0:GW], op=ALU.mult), [K("Dm"), "negmA"], [K("En")])
            yield
            for cc in range(GC):
                sl = slice(cc * 128, (cc + 1) * 128)
                c0 = g0 + cc * 128
                pe(lambda e, sl=sl: e.matmul(out=Y[:, sl], lhsT=Z["KpT"][:, sl], rhs=Z["KpT"][:, sl], start=True, stop=True), [K("KpT")], [Yk])
                pe(lambda e, sl=sl, cc=cc, c0=c0: e.matmul(out=X[:, cc * 128:(cc + 1) * 128], lhsT=Z["KpT"][:, sl], rhs=qT[:, c0:c0 + 128],
                                                           start=True, stop=True), [K("KpT"), f"qT{hb}"], [Xk])
            yield
            dve(lambda e: e.tensor_tensor(out=Z["at"], in0=X[:, 0:GW], in1=Z["Dm"], op=ALU.mult), [Xk, K("Dm")], [K("at")])
            dve(lambda e: e.tensor_tensor(out=Z["Mb"], in0=Y[:, 0:GW], in1=Z["En"], op=ALU.mult), [Yk, K("En")], [K("Mb")])
            dve(lambda e: e.scalar_tensor_tensor(out=Z["Nb"], in0=Y[:, 0:GW], scalar=-1.0, in1=Z["Dm2"], op0=ALU.mult, op1=ALU.mult),
                [Yk, K("Dm2")], [K("Nb")])
            yield
            Dt = Z["Tb"]
            D = Z["D"]
            U16 = mybir.dt.uint16
            mku = mkb[:].bitcast(U16)
            dve(lambda e: e.tensor_copy(out=D, in_=identb2[:, 0:GW]), ["identb2"], [K("D")])
            act(lambda e: e.activation(out=Dt, in_=identb2[:, 0:GW], func=AF.Copy), ["identb2"], [K("Tb")])
            yield
            dve(lambda e: e.copy_predicated(out=D, mask=mku[:, 0, 0:GW], data=Z["Nb"]), [K("Nb"), "mkb", K("D")], [K("D")])
            dve(lambda e: e.copy_predicated(out=Dt, mask=mku[:, 0, 0:GW], data=Z["Mb"]), [K("Mb"), "mkb", K("Tb")], [K("Tb")])
            yield
            for lvl in range(1, 7):
                last = (lvl == 6)
                for cc in range(GC):
                    sl = slice(cc * 128, (cc + 1) * 128)
                    pe(lambda e, sl=sl, cc=cc: e.matmul(out=Y[:, cc * 128:(cc + 1) * 128], lhsT=Z["Nb"][:, sl], rhs=Dt[:, sl],
                                                        start=True, stop=True), [K("Nb"), K("Tb")], [Yk])
                if not last:
                    for cc in range(GC):
                        sl = slice(cc * 128, (cc + 1) * 128)
                        pe(lambda e, sl=sl, cc=cc: e.matmul(out=X[:, cc * 128:(cc + 1) * 128], lhsT=Z["Mb"][:, sl], rhs=D[:, sl],
                                                            start=True, stop=True), [K("Mb"), K("D")], [Xk])
                yield
                act(lambda e: e.activation(out=Z["Xb"], in_=Y[:, 0:GW], func=AF.Copy), [Yk], [K("Xb")])
                if not last:
                    act(lambda e: e.activation(out=Z["Xp"], in_=X[:, 0:GW], func=AF.Copy), [Xk], [K("Xp")])
                yield
                for cc in range(GC):
                    sl = slice(cc * 128, (cc + 1) * 128)
                    pe(lambda e, sl=sl, cc=cc: e.matmul(out=Y[:, cc * 128:(cc + 1) * 128], lhsT=D[:, sl], rhs=Z["Xb"][:, sl],
                                                        start=True, stop=True), [K("D"), K("Xb")], [Yk])
                if not last:
                    for cc in range(GC):
                        sl = slice(cc * 128, (cc + 1) * 128)
                        pe(lambda e, sl=sl, cc=cc: e.matmul(out=X[:, cc * 128:(cc + 1) * 128], lhsT=Dt[:, sl], rhs=Z["Xp"][:, sl],
                                                            start=True, stop=True), [K("Tb"), K("Xp")], [Xk])
                yield
                dve(lambda e, lvl=lvl: e.copy_predicated(out=Dt, mask=mku[:, lvl, 0:GW], data=Y[:, 0:GW]), [Yk, "mkb", K("Tb")], [K("Tb")])
                if not last:
                    dve(lambda e, lvl=lvl: e.copy_predicated(out=D, mask=mku[:, lvl, 0:GW], data=X[:, 0:GW]), [Xk, "mkb", K("D")], [K("D")])
                yield
            for cc in range(GC):
                sl = slice(cc * 128, (cc + 1) * 128)
                pe(lambda e, sl=sl: e.matmul(out=X[:, sl], lhsT=Z["Kg"][:, sl], rhs=Z["Tb"][:, sl], start=True, stop=True), [K("Kg"), K("Tb")], [Xk])
            yield
            act(lambda e: e.activation(out=Z["nW"], in_=X[:, 0:GW], func=AF.Identity, scale=-1.0), [Xk], [K("nW")])
            done.add(("prep", h, g))
            yield

        def prep_chain(h, parity):
            yield from wait_for(("conv", h))
            yield from wait_for("3a")
            for g in range(parity, NG, 2):
                if g >= 2:
                    yield from wait_for(("scan", h, g - 2))
                elif h > 0:
                    yield from wait_for(("scan", h - 1, NG - 2 + parity))
                yield from prep_task(h, g)

        def scan_chain(h):
            hb = h % 2
            qT, szB = qTs[hb], szBs[hb]
            HS = hsc[hb]
            rqk, rqe = HS[:, 32:64], HS[:, 0:16]
            hk = f"hsc{hb}"
            if h > 0:
                yield from wait_for(("scanhead", h - 1))
            else:
                yield from wait_for("3a")
            dve(lambda e: e.memset(S, 0.0), [], ["S"])
            dve(lambda e: e.memset(Sb, 0.0), [], ["Sb"])
            for g in range(NG):
                yield from wait_for(("prep", h, g))
                s_ = g % 2
                Z = setT[s_]
                K = lambda n: f"{n}{s_}"
                for cc in range(GC):
                    c = g * GC + cc
                    sl = slice(cc * 128, (cc + 1) * 128)
                    qsl = slice(c * 128, (c + 1) * 128)
                    pe(lambda e, sl=sl: e.matmul(out=B7[:, 0:128], lhsT=Z["Tb"][:, sl], rhs=Z["Vs"][:, sl], start=True, stop=False), [K("Tb"), K("Vs")], ["B7"])
                    pe(lambda e, sl=sl: e.matmul(out=B7[:, 0:128], lhsT=Z["nW"][:, sl], rhs=Sb, start=False, stop=True), [K("nW"), "Sb"], ["B7"])
                    act(lambda e: e.activation(out=vnb, in_=B7[:, 0:128], func=AF.Copy), ["B7"], ["vnb"])
                    pe(lambda e, sl=sl: e.matmul(out=B7[:, 384:512], lhsT=Z["Kd"][:, sl], rhs=vnb, start=True, stop=True), [K("Kd"), "vnb"], ["B7"])
                    pe(lambda e, qsl=qsl: e.matmul(out=B7[:, 128:256], lhsT=qT[:, qsl], rhs=Sb, start=True, stop=True), [f"qT{hb}", "Sb"], ["B7"])
                    pe(lambda e, sl=sl: e.matmul(out=B7[:, 256:384], lhsT=Z["at"][:, sl], rhs=vnb, start=True, stop=True), [K("at"), "vnb"], ["B7"])
                    dve(lambda e, c=c: e.scalar_tensor_tensor(out=S, in0=S, scalar=col(egl, c, h), in1=B7[:, 384:512], op0=ALU.mult, op1=ALU.add),
                        ["S", "egl", "B7"], ["S"])
                    act(lambda e: e.activation(out=Sb, in_=S, func=AF.Copy), ["S"], ["Sb"])
                    act(lambda e, c=c: e.activation(out=o1s, in_=B7[:, 128:256], func=AF.Identity, scale=rqe[:, c:c + 1]), ["B7", hk], ["o1s"])
                    dve(lambda e, c=c, sl=sl: e.scalar_tensor_tensor(out=Z["oraw"][:, sl], in0=B7[:, 256:384], scalar=rqk[:, c:c + 1], in1=o1s,
                                                                     op0=ALU.mult, op1=ALU.add), ["B7", hk, "o1s"], [K("oraw")])
                    act(lambda e, cc=cc, sl=sl: e.activation(out=junk2, in_=Z["oraw"][:, sl], func=AF.Square, accum_out=Z["sso"][:, cc:cc + 1]),
                        [K("oraw")], ["junk2", K("sso")])
                    yield
                rsqrt_ln(Z["ro"], Z["sso"], 1.0 / 128, EPS, [K("sso")], [K("ro")], Z["sso"], K("sso"))
                for cc in range(GC):
                    sl = slice(cc * 128, (cc + 1) * 128)
                    pool(lambda e, cc=cc, sl=sl: e.tensor_scalar(out=Z["og"][:, sl], in0=Z["oraw"][:, sl], scalar1=Z["ro"][:, cc:cc + 1], scalar2=0.0,
                                                                 op0=ALU.mult, op1=ALU.add), [K("oraw"), K("ro")], [K("og")])
                    pe(lambda e, cc=cc, sl=sl: e.transpose(out=Z["T"][:, GW + cc * 128:GW + (cc + 1) * 128], in_=Z["og"][:, sl], identity=identb[:]),
                       [K("og"), "identb"], [Z["Tk"]])
                t0 = g * GW
                dve(lambda e, t0=t0: e.scalar_tensor_tensor(out=Z["xo"], in0=Z["T"][:, GW:2 * GW], scalar=vecs[:, V_ON:V_ON + 1],
                                                            in1=szB[:, t0:t0 + GW], op0=ALU.mult, op1=ALU.mult),
                    [Z["Tk"], "vecs", f"szB{hb}"], [K("xo")])
                dma(lambda e, t0=t0: e.dma_start(out=xbs_d[h][:, t0:t0 + GW], in_=Z["xo"]), [K("xo")], ["xbs_dram"])
                done.add(("scan", h, g))
                yield
            dma(lambda e: e.dma_start(out=S_p_d[h], in_=S), ["S"], ["S_p_dram"])
            done.add(("scanhead", h))
            yield

        def phase3a_chain():
            HT_ALL = [f"hT{n}" for n in range(5)]
            wt, wkey = load_w(w_in, [OFF_BETA], 16)

            def ev_beta(nb, ps, t0, n, bkey):
                act(lambda e: e.activation(out=betaT[:, t0:t0 + n], in_=ps, func=AF.Sigmoid), [bkey], ["betaT"])
            proj_fm(wt, wkey, 0, 8, ev_beta, M=8, bankset=(1, 2))

            def ev_alpha(nb, ps, t0, n, bkey):
                dve(lambda e: e.tensor_scalar(out=xg[0:8, t0:t0 + n], in0=ps, scalar1=hv[:, 1:2], scalar2=None, op0=ALU.add), [bkey, "hv"], ["xg"])
            proj_fm(wt, wkey, 1, 8, ev_alpha, M=8, bankset=(1, 2))
            yield
            dve(lambda e: e.scalar_tensor_tensor(out=axg[0:8, :], in0=xg[0:8, :], scalar=-1.0, in1=xg[0:8, :], op0=ALU.mult, op1=ALU.max), ["xg"], ["axg"])
            act(lambda e: e.activation(out=axg[0:8, :], in_=axg[0:8, :], func=AF.Exp, scale=-1.0), ["axg"], ["axg"])
            act(lambda e: e.activation(out=lg[0:8, :], in_=axg[0:8, :], func=AF.Ln, bias=1.0, scale=1.0), ["axg"], ["lg"])
            act(lambda e: e.activation(out=negA[0:8, 0:1], in_=hv[:, 0:1], func=AF.Exp), ["hv"], ["negA"])
            dve(lambda e: e.tensor_scalar(out=negA[0:8, 0:1], in0=negA[0:8, 0:1], scalar1=-1.0, scalar2=None, op0=ALU.mult), ["negA"], ["negA"])
            dve(lambda e: e.scalar_tensor_tensor(out=lg[0:8, :], in0=xg[0:8, :], scalar=0.0, in1=lg[0:8, :], op0=ALU.max, op1=ALU.add),
                ["xg", "lg"], ["lg"])
            dve(lambda e: e.tensor_scalar(out=gT[:, :], in0=lg[0:8, :], scalar1=negA[0:8, 0:1], scalar2=None, op0=ALU.mult), ["lg", "negA"], ["gT"])
            act(lambda e: e.activation(out=sbT[:, :], in_=betaT[:, 0:NP_], func=AF.Sqrt), ["betaT"], ["sbT"])
            for c in range(16):
                dve(lambda e, c=c: e.tensor_tensor_scan(out=gcT[:, c * 128:(c + 1) * 128], data0=ones_f[0:8, 0:128], data1=gT[:, c * 128:(c + 1) * 128],
                                                        initial=0.0, op0=ALU.mult, op1=ALU.add), ["gT", "ones_f"], ["gcT"])
            yield
            for c in range(16):
                pe(lambda e, c=c: e.transpose(out=B[2][:, c * 8:(c + 1) * 8], in_=gcT[:, c * 128:(c + 1) * 128], identity=ident[0:8, 0:8]),
                   ["gcT", "ident"], ["B2"])
                pe(lambda e, c=c: e.transpose(out=B[3][:, c * 8:(c + 1) * 8], in_=sbT[:, c * 128:(c + 1) * 128], identity=ident[0:8, 0:8]),
                   ["sbT", "ident"], ["B3"])
            dve(lambda e: e.tensor_copy(out=gc_tok[:], in_=B[2][:, 0:128]), ["B2"], ["gc_tok"])
            dve(lambda e: e.tensor_copy(out=sb_tok[:], in_=B[3][:, 0:128]), ["B3"], ["sb_tok"])
            yield
            tl3 = tmpl[0:8, :].rearrange("p (c h) -> p c h", c=16)
            gl_src = gcT[:, :].rearrange("p (c t) -> p c t", c=16)[:, :, 127:128]
            dve(lambda e: e.tensor_tensor(out=tl3, in0=bm[:, :].rearrange("p (c h) -> p c h", c=16), in1=gl_src.to_broadcast([8, 16, 8]), op=ALU.mult),
                ["bm", "gcT"], ["tmpl"])
            pe(lambda e: e.matmul(out=B[4][:, 0:128], lhsT=ones_f[0:8, :], rhs=tmpl[0:8, :], start=True, stop=True), ["tmpl", "ones_f"], ["B4"])
            act(lambda e: e.activation(out=egl[:], in_=B[4][:, 0:128], func=AF.Exp), ["B4"], ["egl"])
            dve(lambda e: e.tensor_tensor(out=ekd[:], in0=B[4][:, 0:128], in1=gc_tok[:], op=ALU.subtract), ["B4", "gc_tok"], ["ekd"])
            act(lambda e: e.activation(out=ekd[:], in_=ekd[:], func=AF.Exp), ["ekd"], ["ekd"])
            act(lambda e: e.activation(out=egc[:], in_=gc_tok[:], func=AF.Exp), ["gc_tok"], ["egc"])
            dve(lambda e: e.tensor_copy(out=beta_s[:, :], in_=betaT[:, NP_:TT]), ["betaT"], ["beta_s"])
            dve(lambda e: e.tensor_copy(out=g_s[:, :], in_=gT[:, NP_:TT]), ["gT"], ["g_s"])


            P.fence()
            done.add("3a")
            yield

        def conv_chain():
            for h in range(8):
                if h >= 2:
                    yield from wait_for(("scanhead", h - 2))
                yield from conv_task(h)

        def run_chains(chains):
            active = list(chains)
            while active:
                progressed = False
                for ch in list(active):
                    try:
                        r = next(ch)
                        if r != "blocked":
                            progressed = True
                    except StopIteration:
                        active.remove(ch)
                        progressed = True
                assert progressed, "chain deadlock"

        chains = [phase3a_chain(), conv_chain()]
        for h in range(8):
            chains += [prep_chain(h, 0), prep_chain(h, 1), scan_chain(h)]
        run_chains(chains)

        ckpt(4)
        P.fence(); arena.reset()
        ZA = ab(8 * TT).rearrange("p (k t) -> p k t", k=8)
        za_end = arena.off
        B5f = B5[:, :].bitcast(F32)
        B6f = B6[:, :].bitcast(F32)

        def sample_chain():
            def bcast_hs(dst, srcT, skey, dkey):
                t = af(128)
                dve(lambda e: e.tensor_tensor(out=t[0:8, :].rearrange("p (h s) -> p h s", h=8), in0=bm[:, :].rearrange("p (s h) -> p h s", s=16),
                                              in1=srcT.unsqueeze(1).to_broadcast([8, 8, NS]), op=ALU.mult), ["bm", skey], [dkey + "_t"])
                pe(lambda e: e.matmul(out=B[2][:, 0:128], lhsT=ones_f[0:8, :], rhs=t[0:8, :], start=True, stop=True), [dkey + "_t", "ones_f"], ["B2"])
                dve(lambda e: e.tensor_copy(out=dst, in_=B[2][:, 0:128]), ["B2"], [dkey])
            beta_bc = af(128); g_bc = af(128); eg_bc = af(128)
            bcast_hs(beta_bc, beta_s[:, :], "beta_s", "beta_bc")
            bcast_hs(g_bc, g_s[:, :], "g_s", "g_bc")
            act(lambda e: e.activation(out=eg_bc, in_=g_bc, func=AF.Exp), ["g_bc"], ["eg_bc"])
            yield
            HS = 8 * NS
            q2 = qs_s[:].rearrange("p h s -> p (h s)"); k2 = ks_s[:].rearrange("p h s -> p (h s)"); v2 = vs_s[:].rearrange("p h s -> p (h s)")
            sq2 = af(HS); rq2 = af(HS); rk2 = af(HS); tmp2 = af(HS)
            for (src, skey, dst, dkey, scl) in [(q2, "qT_s", rq2, "rq2", 128.0 ** -0.5), (k2, "kT_s", rk2, "rk2", 1.0)]:
                dve(lambda e, src=src: e.tensor_tensor(out=sq2, in0=src, in1=src, op=ALU.mult), [skey], ["sq2"])
                pe(lambda e: e.matmul(out=B[3][:, 0:HS], lhsT=ones_f[:], rhs=sq2, start=True, stop=True), ["sq2", "ones_f"], ["B3"])
                rsqrt_ln(dst, B[3][:, 0:HS], 1.0, EPS, ["B3"], [dkey], tmp2, "tmp2")
                dve(lambda e, src=src, dst=dst, scl=scl: e.scalar_tensor_tensor(out=src, in0=src, scalar=scl, in1=dst, op0=ALU.mult, op1=ALU.mult),
                    [skey, dkey], [skey])
                yield
            wcol = af(HS); ucol = af(HS); qg = af(HS); qk = af(HS); qk_bc = af(HS)
            dve(lambda e: e.tensor_tensor(out=wcol, in0=k2, in1=beta_bc, op=ALU.mult), ["kT_s", "beta_bc"], ["wcol"])
            dve(lambda e: e.tensor_tensor(out=wcol, in0=wcol, in1=eg_bc, op=ALU.mult), ["wcol", "eg_bc"], ["wcol"])
            dve(lambda e: e.tensor_tensor(out=ucol, in0=v2, in1=beta_bc, op=ALU.mult), ["vT_s", "beta_bc"], ["ucol"])
            dve(lambda e: e.tensor_tensor(out=qg, in0=q2, in1=eg_bc, op=ALU.mult), ["qT_s", "eg_bc"], ["qg"])
            dve(lambda e: e.tensor_tensor(out=qk, in0=q2, in1=k2, op=ALU.mult), ["qT_s", "kT_s"], ["qk"])
            pe(lambda e: e.matmul(out=B[4][:, 0:HS], lhsT=ones_f[:], rhs=qk, start=True, stop=True), ["qk", "ones_f"], ["B4"])
            dve(lambda e: e.tensor_copy(out=qk_bc, in_=B[4][:, 0:HS]), ["B4"], ["qk_bc"])
            yield
            os_all = af(HS)
            w3 = wcol.rearrange("p (h s) -> p h s", h=8); u3 = ucol.rearrange("p (h s) -> p h s", h=8)
            qg3 = qg.rearrange("p (h s) -> p h s", h=8); qk3 = qk_bc.rearrange("p (h s) -> p h s", h=8)
            eg3 = eg_bc.rearrange("p (h s) -> p h s", h=8); o3 = os_all.rearrange("p (h s) -> p h s", h=8)
            Ss = [af(1024), af(1024)]
            vn_all = af(HS); tmpS = af(1024)
            vn3 = vn_all.rearrange("p (h s) -> p h s", h=8)
            rowk = af(1024); rowv = af(1024); rmask = [af(1024), af(1024)]
            for s in range(NS):
                Si, sk = Ss[s % 2], f"Ss{s % 2}"
                S3 = Si.rearrange("p (h e) -> p h e", h=8)
                bank, bkey = (B7, "B7") if s % 2 == 0 else (B[4], "B4")
                dma(lambda e: e.dma_start(out=S3, in_=sd[s].rearrange("h d e -> d h e")), [], [sk])
                for hh in range(8):
                    pe(lambda e, hh=hh: e.matmul(out=bank[:, hh:hh + 1], lhsT=S3[:, hh, :], rhs=w3[:, hh, s:s + 1], start=True, stop=True),
                       [sk, "wcol"], [bkey])
                    pe(lambda e, hh=hh: e.matmul(out=bank[:, 8 + hh:9 + hh], lhsT=S3[:, hh, :], rhs=qg3[:, hh, s:s + 1], start=True, stop=True),
                       [sk, "qg"], [bkey])
                dve(lambda e: e.tensor_tensor(out=vn3[:, :, s], in0=u3[:, :, s], in1=bank[:, 0:8], op=ALU.subtract), ["ucol", bkey], ["vn_all"])
                dve(lambda e: e.tensor_tensor(out=o3[:, :, s], in0=qk3[:, :, s], in1=vn3[:, :, s], op=ALU.mult), ["qk_bc", "vn_all"], ["os_all"])
                dve(lambda e: e.tensor_tensor(out=o3[:, :, s], in0=o3[:, :, s], in1=bank[:, 8:16], op=ALU.add), ["os_all", bkey], ["os_all"])
                yield
            for hh in range(8):
                bkr, bkrk = (B[3], "B3") if hh < 4 else (B[4], "B4")
                bv, bvk = (B[1], "B1") if hh < 4 else (B[2], "B2")
                c0 = (hh % 4) * 128
                pe(lambda e, hh=hh, c0=c0, bkr=bkr: e.transpose(out=bkr[0:NS, c0:c0 + 128], in_=ks_s[:, hh, :], identity=ident[:]), ["kT_s", "ident"], [bkrk])
                pe(lambda e, hh=hh, c0=c0, bv=bv: e.transpose(out=bv[0:NS, c0:c0 + 128], in_=vn3[:, hh, :], identity=ident[:]), ["vn_all", "ident"], [bvk])
            yield
            dve(lambda e: e.tensor_copy(out=rowk[0:NS, 0:512], in_=B[3][0:NS, :]), ["B3"], ["rowk"])
            dve(lambda e: e.tensor_copy(out=rowk[0:NS, 512:1024], in_=B[4][0:NS, :]), ["B4"], ["rowk"])
            act(lambda e: e.activation(out=rowv[0:NS, 0:512], in_=B[1][0:NS, :], func=AF.Copy), ["B1"], ["rowv"])
            act(lambda e: e.activation(out=rowv[0:NS, 512:1024], in_=B[2][0:NS, :], func=AF.Copy), ["B2"], ["rowv"])
            yield
            for s in range(NS):
                Si, sk = Ss[s % 2], f"Ss{s % 2}"
                S3 = Si.rearrange("p (h e) -> p h e", h=8)
                rm, rmk = rmask[s % 2], f"rmask{s % 2}"
                dma(lambda e: e.dma_start(out=S3, in_=sd[s].rearrange("h d e -> d h e")), [], [sk])
                dve(lambda e: e.tensor_scalar(out=rm[0:NS, :], in0=rowv[0:NS, :], scalar1=ident[0:NS, s:s + 1], scalar2=None, op0=ALU.mult),
                    ["rowv", "ident"], [rmk])
                yield
                for hh in range(8):
                    bo, bok = (B[1], "B1") if hh < 4 else (B[2], "B2")
                    c0 = (hh % 4) * 128
                    pe(lambda e, hh=hh, c0=c0, bo=bo: e.matmul(out=bo[:, c0:c0 + 128], lhsT=rowk[0:NS, hh * 128:(hh + 1) * 128],
                                                               rhs=rm[0:NS, hh * 128:(hh + 1) * 128], start=True, stop=True), ["rowk", rmk], [bok])
                T3 = tmpS.rearrange("p (h e) -> p h e", h=8)
                dve(lambda e: e.tensor_tensor(out=T3, in0=S3, in1=eg3[:, :, s:s + 1].to_broadcast([128, 8, 128]), op=ALU.mult), [sk, "eg_bc"], ["tmpS"])
                yield
                dve(lambda e: e.tensor_tensor(out=tmpS[:, 0:512], in0=tmpS[:, 0:512], in1=B[1][:, :], op=ALU.add), ["tmpS", "B1"], ["tmpS"])
                dve(lambda e: e.tensor_tensor(out=tmpS[:, 512:1024], in0=tmpS[:, 512:1024], in1=B[2][:, :], op=ALU.add), ["tmpS", "B2"], ["tmpS"])
                dma(lambda e: e.dma_start(out=S_s_d[s].rearrange("h d e -> d h e"), in_=T3), ["tmpS"], ["S_s_dram"])
                yield
            so2 = af(HS); rso = af(HS)
            dve(lambda e: e.tensor_tensor(out=so2, in0=os_all, in1=os_all, op=ALU.mult), ["os_all"], ["so2"])
            pe(lambda e: e.matmul(out=B[2][:, 0:HS], lhsT=ones_f[:], rhs=so2, start=True, stop=True), ["so2", "ones_f"], ["B2"])
            rsqrt_ln(rso, B[2][:, 0:HS], 1.0 / 128, EPS, ["B2"], ["rso"], tmp2, "tmp2")
            dve(lambda e: e.scalar_tensor_tensor(out=os_all, in0=os_all, scalar=vecs[:, V_ON:V_ON + 1], in1=rso, op0=ALU.mult, op1=ALU.mult),
                ["os_all", "vecs", "rso"], ["os_all"])
            dve(lambda e: e.tensor_tensor(out=XBs[:], in0=XBs[:], in1=o3, op=ALU.mult), ["XBs", "os_all"], ["XBs"])
            dma(lambda e: e.dma_start(out=xbs_d.rearrange("h p t -> p h t")[:, :, NP_:TT], in_=XBs[:]), ["XBs"], ["xbs_dram"])
            yield

        pre_w = {}

        def branchA_chain():
            ub = af(2 + TT); acca = af(TT); szf = ab(TT); tz = af(512)
            dve(lambda e: e.memset(ub[:, 0:2], 0.0), [], ["ub"])
            banks = [(B[0], "B0"), (B5f, "B5"), (B6f, "B6")]
            cnt = [0]

            def proj(wt, wkey, j, evac):
                for nb, (t0, n) in enumerate(BLK):
                    bank, bkey = banks[cnt[0] % 3]
                    cnt[0] += 1
                    for k in range(8):
                        pe(lambda e, k=k, bank=bank, t0=t0, n=n: e.matmul(out=bank[:, 0:n], lhsT=wt[:, k, j * 128:(j + 1) * 128], rhs=hT[:, k, t0:t0 + n],
                                                                         start=(k == 0), stop=(k == 7)), [wkey, "hT"], [bkey])
                    evac(nb, bank[:, 0:n], t0, n, bkey)
                    yield

            for f in range(8):
                wt, wkey = get_w("brA", [(w_in, [OFF_HA + ff * 128, OFF_CA + ff * 128, OFF_BA + ff * 128, OFF_ZA + ff * 128], 128) for ff in range(8)], f)
                if f == 7:
                    pre_w["wA0"] = load_w(w_out_a, [0], 512)

                def ev_ha(nb, ps, t0, n, bkey):
                    act(lambda e: e.activation(out=ub[:, 2 + t0:2 + t0 + n], in_=ps, func=AF.Copy), [bkey], ["ub"])
                yield from proj(wt, wkey, 0, ev_ha)

                def ev_ca(nb, ps, t0, n, bkey):
                    dve(lambda e: e.tensor_tensor(out=ub[:, 2 + t0:2 + t0 + n], in0=ps, in1=ub[:, 2 + t0:2 + t0 + n], op=ALU.mult), [bkey, "ub"], ["ub"])
                yield from proj(wt, wkey, 1, ev_ca)
                pool(lambda e, f=f: e.tensor_copy(out=tail_a_p[:, f, :], in_=ub[:, NP_:NP_ + 2]), ["ub"], ["tail_a_p"])
                pool(lambda e, f=f: e.tensor_copy(out=tail_a_s[:, f, 0, :], in_=cbufa[:, f, 1, :]), ["cbufa"], ["tail_a_s"])
                pool(lambda e, f=f: e.tensor_copy(out=tail_a_s[:, f, 1, :], in_=ub[:, 2 + NP_:2 + TT]), ["ub"], ["tail_a_s"])
                wc = [vecs[:, V_CA + t * 8 + f:V_CA + t * 8 + f + 1] for t in range(3)]
                HB = NP_ // 2
                for hf in range(2):
                    c0 = hf * HB
                    pool(lambda e, wc=wc, c0=c0: e.tensor_scalar(out=acca[:, c0:c0 + HB], in0=ub[:, c0:c0 + HB], scalar1=wc[0], scalar2=0.0,
                                                                 op0=ALU.mult, op1=ALU.add), ["ub", "vecs"], ["acca"])
                    yield
                for t in range(1, 3):
                    for hf in range(2):
                        c0 = hf * HB
                        dve(lambda e, t=t, wc=wc, c0=c0: e.scalar_tensor_tensor(out=acca[:, c0:c0 + HB], in0=ub[:, c0 + t:c0 + t + HB], scalar=wc[t],
                                                                                in1=acca[:, c0:c0 + HB], op0=ALU.mult, op1=ALU.add),
                            ["ub", "acca", "vecs"], ["acca"])
                        yield
                pool(lambda e, wc=wc: e.tensor_scalar(out=acca[:, NP_:TT], in0=ub[:, 2 + NP_:2 + TT], scalar1=wc[2], scalar2=0.0, op0=ALU.mult, op1=ALU.add),
                     ["ub", "vecs"], ["acca"])
                for t in range(2):
                    dve(lambda e, t=t, wc=wc, f=f: e.scalar_tensor_tensor(out=acca[:, NP_:TT], in0=cbufa[:, f, t, :], scalar=wc[t], in1=acca[:, NP_:TT],
                                                                          op0=ALU.mult, op1=ALU.add), ["cbufa", "acca", "vecs"], ["acca"])

                def ev_za(nb, ps, t0, n, bkey, f=f):
                    act(lambda e: e.activation(out=szf[:, t0:t0 + n], in_=ps, func=AF.Silu), [bkey], ["szf"])
                yield from proj(wt, wkey, 3, ev_za)

                def ev_ba(nb, ps, t0, n, bkey, f=f):
                    dve(lambda e: e.tensor_tensor(out=tz[:, 0:n], in0=ps, in1=acca[:, t0:t0 + n], op=ALU.mult), [bkey, "acca"], ["tz"])
                    dve(lambda e: e.tensor_tensor(out=ZA[:, f, t0:t0 + n], in0=tz[:, 0:n], in1=szf[:, t0:t0 + n], op=ALU.mult), ["tz", "szf"], [f"ZA{nb}"])
                yield from proj(wt, wkey, 2, ev_ba)
            pre_w["wB0"] = load_w(w_out_b, [0], 512)
            yield

        run_chains([sample_chain(), branchA_chain()])

        ckpt(6)
        P.fence()
        arena.off = za_end
        MG = ab(8 * TT).rearrange("p (k t) -> p k t", k=8)
        mg_end = arena.off
        XB = ab(8 * TT).rearrange("p (k t) -> p k t", k=8)
        for nb_, (t0_, n_) in enumerate(BLK):
            dma(lambda e, t0_=t0_, n_=n_: e.dma_start(out=XB[:, :, t0_:t0_ + n_], in_=xbs_d.rearrange("h p t -> p h t")[:, :, t0_:t0_ + n_]),
                ["xbs_dram"], [f"XB{nb_}"])
        sA = ab(512); sB = ab(512); tA = af(512); tB = af(512)
        gwv = gcT_full[:, 0:2048].bitcast(BF16)
        gws = [gwv[:, 0:2048].rearrange("p (k n) -> p k n", k=8), gwv[:, 2048:4096].rearrange("p (k n) -> p k n", k=8)]

        def load_gates(fo):
            gwt = gws[fo % 2]
            for gi_, goff in enumerate([OFF_GA, OFF_GB]):
                src = w_in[:, goff + fo * 128:goff + (fo + 1) * 128].rearrange("(k p) n -> p k n", p=128)
                dma(lambda e, gi_=gi_, src=src, gwt=gwt: e.dma_start(out=gwt[:, :, gi_ * 128:(gi_ + 1) * 128], in_=src), [], [f"gw{fo % 2}"], eng="pool")
        load_gates(0)

        def mm8(bank, bkey, M, lhs_fn, rhs_fn, rkeys, n):
            for k in range(8):
                pe(lambda e, k=k: e.matmul(out=bank[0:M, 0:n], lhsT=lhs_fn(k), rhs=rhs_fn(k), start=(k == 0), stop=(k == 7)), rkeys, [bkey])

        for half in range(2):
            wA, wAk = pre_w["wA0"] if half == 0 else load_w(w_out_a, [half * 512], 512)
            wB, wBk = pre_w["wB0"] if half == 0 else load_w(w_out_b, [half * 512], 512)
            for jj in range(4):
                fo = half * 4 + jj
                if fo + 1 < 8:
                    load_gates(fo + 1)
                gw, gwk = gws[fo % 2], f"gw{fo % 2}"
                for nb, (t0, n) in enumerate(BLK):
                    mm8(B[0], "B0", 128, lambda k, gw=gw: gw[:, k, 0:128], lambda k, t0=t0, n=n: hT[:, k, t0:t0 + n], [gwk, f"hT{nb}"], n)
                    act(lambda e, n=n: e.activation(out=sA[:, 0:n], in_=B[0][:, 0:n], func=AF.Sigmoid), ["B0"], ["sA"])
                    mm8(B[1], "B1", 128, lambda k, jj=jj, wA=wA: wA[:, k, jj * 128:(jj + 1) * 128], lambda k, t0=t0, n=n: ZA[:, k, t0:t0 + n],
                        [wAk, f"ZA{nb}"], n)
                    dve(lambda e, n=n: e.tensor_tensor(out=tA[:, 0:n], in0=B[1][:, 0:n], in1=sA[:, 0:n], op=ALU.mult), ["B1", "sA"], ["tA"])
                    mm8(B[2], "B2", 128, lambda k, gw=gw: gw[:, k, 128:256], lambda k, t0=t0, n=n: hT[:, k, t0:t0 + n], [gwk, f"hT{nb}"], n)
                    act(lambda e, n=n: e.activation(out=sB[:, 0:n], in_=B[2][:, 0:n], func=AF.Sigmoid), ["B2"], ["sB"])
                    mm8(B[3], "B3", 128, lambda k, jj=jj, wB=wB: wB[:, k, jj * 128:(jj + 1) * 128], lambda k, t0=t0, n=n: XB[:, k, t0:t0 + n],
                        [wBk, f"XB{nb}"], n)
                    dve(lambda e, n=n: e.tensor_tensor(out=tB[:, 0:n], in0=B[3][:, 0:n], in1=sB[:, 0:n], op=ALU.mult), ["B3", "sB"], ["tB"])
                    pool(lambda e, fo=fo, t0=t0, n=n: e.tensor_tensor(out=MG[:, fo, t0:t0 + n], in0=tA[:, 0:n], in1=tB[:, 0:n], op=ALU.add),
                         ["tA", "tB"], [f"MG{nb}"])

        ckpt(7)
        P.fence()
        arena.off = 0
        wo_b = ab(8 * 1024).rearrange("p (k n) -> p k n", k=8)
        for n2 in range(2):
            src = w_o[:, n2 * 512:(n2 + 1) * 512].rearrange("(k p) n -> p k n", p=128)
            dma(lambda e, n2=n2, src=src: e.dma_start(out=wo_b[:, :, n2 * 512:(n2 + 1) * 512], in_=src), [], [f"wo_b{n2}"], eng="pool")
        xt = [af(D), af(D)]
        yt0 = af(D)
        assert arena.off <= za_end
        arena.off = mg_end
        yt = [yt0, af(D)]
        junk = ab(512)
        ss2 = af(2); rs5 = af(2); tmp5 = af(2)
        for i in range(17):
            rows = 128 if i < 16 else NS
            t0 = i * 128
            mk = f"MG{i // 4}"
            xi, xk = xt[i % 2], f"xt{i % 2}"
            yi, yk = yt[i % 2], f"yt{i % 2}"
            src = xp[i * 128:(i + 1) * 128, :] if i < 16 else xsm
            dst = y_p[i * 128:(i + 1) * 128, :] if i < 16 else y_s
            gsrc = gn if i < 16 else gns
            dma(lambda e, xi=xi, rows=rows, src=src: e.dma_start(out=xi[0:rows, :], in_=src), [], [xk])
            bo = 2 * (i % 2)
            for n2 in range(2):
                mm8(B[bo + n2], f"B{bo + n2}", rows, lambda k, t0=t0, rows=rows: MG[:, k, t0:t0 + rows], lambda k, n2=n2: wo_b[:, k, n2 * 512:(n2 + 1) * 512],
                    [mk, f"wo_b{n2}"], 512)
                act(lambda e, n2=n2, rows=rows, bo=bo: e.activation(out=junk[0:rows, :], in_=B[bo + n2][0:rows, :], func=AF.Square,
                                                                  accum_out=ss2[0:rows, n2:n2 + 1]), [f"B{bo + n2}"], ["junk", "ss2"])
            dve(lambda e, rows=rows: e.tensor_tensor(out=ss2[0:rows, 0:1], in0=ss2[0:rows, 0:1], in1=ss2[0:rows, 1:2], op=ALU.add), ["ss2"], ["ss2"])
            rsqrt_small(rs5[0:rows, 0:1], ss2[0:rows, 0:1], 1.0 / D, EPS, ["ss2"], ["rs5"], tmp5[0:rows, 0:1], "tmp5")
            for n2 in range(2):
                cs_ = slice(n2 * 512, (n2 + 1) * 512)
                dve(lambda e, n2=n2, rows=rows, cs_=cs_, yi=yi, gsrc=gsrc, bo=bo: e.scalar_tensor_tensor(
                    out=yi[0:rows, cs_], in0=B[bo + n2][0:rows, :], scalar=rs5[0:rows, 0:1], in1=gsrc[0:rows, cs_], op0=ALU.mult, op1=ALU.mult),
                    [f"B{bo + n2}", "rs5", "gn", "gns"], [yk])
            dve(lambda e, rows=rows, yi=yi, xi=xi: e.tensor_tensor(out=yi[0:rows, :], in0=yi[0:rows, :], in1=xi[0:rows, :], op=ALU.add), [yk, xk], [yk])
            dma(lambda e, yi=yi, rows=rows, dst=dst: e.dma_start(out=dst, in_=yi[0:rows, :]), [yk], ["y_dram"])
        dma(lambda e: e.dma_start(out=tail_a_p_d, in_=tail_a_p[:].rearrange("p a b -> p (a b)")), ["tail_a_p"], ["tap_dram"])
        dma(lambda e: e.dma_start(out=tail_a_s_d, in_=tail_a_s[:].rearrange("p a b c -> p (a b c)")), ["tail_a_s"], ["tas_dram"])
        dma(lambda e: e.dma_start(out=tail_q_p_d, in_=tail_q_p[:].rearrange("p a b -> p (a b)")), ["tail_q_p"], ["tqp_dram"])
        dma(lambda e: e.dma_start(out=tail_q_s_d, in_=tail_q_s[:].rearrange("p a b c -> p (a b c)")), ["tail_q_s"], ["tqs_dram"])
        return

    with ExitStack() as st:
        P = Prog(nc, st)
        try:
            _body(st, P)
        except _Stop:
            pass
        P.fence()
        P.emit()
    return nc


_NC_CACHE = {}


def _consts():
    p = np.arange(128)[:, None]
    c = np.arange(128)[None, :]
    ident = np.eye(128, dtype=np.float32)
    biasA = np.where(c >= p, 0.0, -BIG).astype(np.float32)
    biasB = np.where(p > c, 0.0, BIG).astype(np.float32)
    negmA = np.tile(np.where(c > p, -1.0, 0.0).astype(np.float32), (1, 4))
    sel = np.zeros((8, 8, 128), np.float32)
    for h in range(8):
        sel[h, h, :] = 1.0
    bm = np.zeros((8, 16, 8), np.float32)
    for h in range(8):
        bm[h, :, h] = 1.0
    mks = np.zeros((128, 7, 128), np.float32)
    mks[:, 0, :] = (p // 2 == c // 2) & (p != c)
    for l in range(1, 7):
        b = 2 ** l
        mks[:, l, :] = (p // (2 * b) == c // (2 * b)) & (p // b != c // b)
    return dict(ident=ident, biasA=biasA, biasB=biasB, negmA=negmA, sel=sel.reshape(8, 1024), bm=bm.reshape(8, 128),
                mks=mks.reshape(128, 7 * 128))


def kernel(x_prompt, x_sample, c_prompt, c_sample, state_conv_a, state_conv_qkv, state_delta,
           ada_w, ada_b, norm_pre, w_in, conv_a_w, conv_b_w, a_log, dt_bias, onorm_w,
           w_out_a, w_out_b, w_o, norm_post):
    f = lambda a: np.ascontiguousarray(np.asarray(a, dtype=np.float32))
    x_prompt, x_sample, c_prompt, c_sample = f(x_prompt), f(x_sample), f(c_prompt), f(c_sample)
    state_conv_a, state_conv_qkv, state_delta = f(state_conv_a), f(state_conv_qkv), f(state_delta)
    if "nc" not in _NC_CACHE:
        _NC_CACHE["nc"] = build_nc(_NC_CACHE.get("stop", 99))
    nc = _NC_CACHE["nc"]
    n = 8
    vecs = np.zeros((128, 160), np.float32)
    vecs[:, 0:24] = f(ada_b)[0].reshape(24, 128).T
    vecs[:, 24:32] = f(norm_pre)[0].reshape(8, 128).T
    vecs[:, 32:56] = f(conv_a_w)[0].reshape(3, 8, 128).transpose(2, 0, 1).reshape(128, 24)
    vecs[:, 56] = f(onorm_w)[0]
    vecs[:, 64:160] = f(conv_b_w)[0].reshape(4, 24, 128).transpose(2, 0, 1).reshape(128, 96)
    hv = np.stack([f(a_log)[0], f(dt_bias)[0]], axis=1)
    shared = dict(ada_w=f(ada_w)[0], adab_bc=np.ascontiguousarray(np.broadcast_to(f(ada_b)[0, 2048:3072], (128, 1024))),
                  npost_bc=np.ascontiguousarray(np.broadcast_to(f(norm_post)[0], (128, 1024))),
                  vecs=vecs, hv=np.ascontiguousarray(hv), w_in=f(w_in)[0], w_out_a=f(w_out_a)[0], w_out_b=f(w_out_b)[0], w_o=f(w_o)[0])
    shared.update(_consts())
    in_maps = []
    for b in range(n):
        s0, s1 = b * NS, (b + 1) * NS
        ca = state_conv_a[0, s0:s1].reshape(NS, 2, 8, 128).transpose(3, 2, 1, 0)
        cq = state_conv_qkv[0, s0:s1].reshape(NS, 3, 24, 128).transpose(3, 2, 1, 0)
        m = dict(shared)
        m.update(xp=x_prompt[b], xsm=x_sample[s0:s1, 0, :], cp_bc=np.ascontiguousarray(np.broadcast_to(c_prompt[b], (128, 1024))),
                 cs=c_sample[s0:s1], cbufa=np.ascontiguousarray(ca).reshape(128, -1), cbufq=np.ascontiguousarray(cq).reshape(128, -1),
                 sd=state_delta[0, s0:s1])
        in_maps.append({k: np.ascontiguousarray(v) for k, v in m.items()})
    res = run_bass_kernel_spmd(nc, in_maps, core_ids=list(range(n)))
    R = res.results
    y_p = np.stack([R[b]["y_p"] for b in range(n)])
    y_s = np.concatenate([R[b]["y_s"] for b in range(n)])[:, None, :]
    nca_p = np.stack([R[b]["tail_a_p"].reshape(128, 8, 2).transpose(2, 1, 0).reshape(2, 1024) for b in range(n)])[None]
    ncq_p = np.stack([R[b]["tail_q_p"].reshape(128, 24, 3).transpose(2, 1, 0).reshape(3, 3072) for b in range(n)])[None]
    nd_p = np.stack([R[b]["S_p"] for b in range(n)])[None]
    nca_s = np.concatenate([R[b]["tail_a_s"].reshape(128, 8, 2, NS).transpose(3, 2, 1, 0).reshape(NS, 2, 1024) for b in range(n)])[None]
    ncq_s = np.concatenate([R[b]["tail_q_s"].reshape(128, 24, 3, NS).transpose(3, 2, 1, 0).reshape(NS, 3, 3072) for b in range(n)])[None]
    nd_s = np.concatenate([R[b]["S_s"] for b in range(n)])[None]
    out = (y_p, y_s, nca_p, ncq_p, nd_p, nca_s, ncq_s, nd_s)
    return tuple(np.ascontiguousarray(o, dtype=np.float32) for o in out)
```

```python
import numpy as np
from contextlib import ExitStack
import concourse.bass as bass
import concourse.mybir as mybir
from concourse.bass_utils import run_bass_kernel_spmd

F32 = mybir.dt.float32
BF16 = mybir.dt.bfloat16
ALU = mybir.AluOpType
AF = mybir.ActivationFunctionType
AX = mybir.AxisListType

NP_ = 2048
NS = 16
TT = NP_ + NS
D = 1024
INC = 10256
EPS = 1e-6
BIG = 60000.0
BLK = [(0, 512), (512, 512), (1024, 512), (1536, 512), (2048, 16)]
OFF_HA, OFF_CA, OFF_BA, OFF_ZA = 0, 1024, 2048, 3072
OFF_Q, OFF_K, OFF_V, OFF_ZB = 4096, 5120, 6144, 7168
OFF_BETA, OFF_ALPHA, OFF_GA, OFF_GB = 8192, 8200, 8208, 9232


class _Rec:
    def __init__(self):
        self.call = None

    def __getattr__(self, name):
        def f(*a, **kw):
            assert self.call is None
            self.call = (name, a, kw)
            return self
        return f


class Prog:
    ENGS = ("pe", "act", "dve", "pool", "sp")
    SEM_LIMIT = 30000

    def __init__(self, nc, stack):
        self.nc, self.stack = nc, stack
        self.ops = {e: [] for e in self.ENGS}
        self.eng_sem, self.eng_cnt = {}, {}
        self.waited = {e: {} for e in self.ENGS}
        self.writers, self.readers = {}, {}
        self.dma_sem, self.dma_cnt = {}, {}
        self.all_sems = {}
        self.nsem = 0
        for e in self.ENGS:
            self._new_eng_sem(e)

    def _sem(self, name):
        self.nsem += 1
        return self.stack.enter_context(self.nc.semaphore(name))

    def _new_eng_sem(self, e):
        self.eng_sem[e] = self._sem(f"s_{e}_{self.nsem}")
        self.eng_cnt[e] = 0

    @staticmethod
    def _bank(key):
        if isinstance(key, str) and len(key) >= 2 and key[0] == "B" and key[1].isdigit():
            return key[:2]
        return None

    def _need(self, eng, tok, waits, raw):
        sem, val, teng = tok
        if teng == eng and (not raw or eng == "pe"):
            return
        w = self.waited[eng]
        if w.get(id(sem), 0) < val:
            w[id(sem)] = val
            waits[id(sem)] = (sem, val)

    def op(self, eng, fn, reads=(), writes=(), dma=False):
        waits = {}
        reads = [self._bank(b) or b for b in reads]
        writes = [self._bank(b) or b for b in writes]
        for b in reads:
            for tok in self.writers.get(b, {}).values():
                self._need(eng, tok, waits, True)
            if self._bank(b):
                for tok in self.readers.get(b, {}).values():
                    self._need(eng, tok, waits, False)
        for b in writes:
            for tok in self.writers.get(b, {}).values():
                self._need(eng, tok, waits, True)
            for tok in self.readers.get(b, {}).values():
                self._need(eng, tok, waits, False)
        if dma:
            key = writes[0] if writes else ("rd", reads[0])
            if key not in self.dma_sem:
                self.dma_sem[key] = self._sem(f"d{self.nsem}")
                self.dma_cnt[key] = 0
            self.dma_cnt[key] += 16
            tok = (self.dma_sem[key], self.dma_cnt[key], "dma")
            inc = (tok[0], 16)
        else:
            if self.eng_cnt[eng] >= self.SEM_LIMIT:
                self._new_eng_sem(eng)
            self.eng_cnt[eng] += 1
            tok = (self.eng_sem[eng], self.eng_cnt[eng], eng)
            inc = (tok[0], 1)
        self.all_sems[id(tok[0])] = tok
        rec = _Rec()
        fn(rec)
        assert rec.call is not None
        self.ops[eng].append((list(waits.values()), rec.call, inc))
        for b in reads:
            self.readers.setdefault(b, {})[id(tok[0])] = tok
        for b in writes:
            self.writers.setdefault(b, {})[id(tok[0])] = tok
        return tok

    def fence(self, engs=None):
        for e in (engs or self.ENGS):
            waits = {}
            for tok in self.all_sems.values():
                self._need(e, tok, waits, True)
            if waits:
                self.ops[e].append((list(waits.values()), None, None))

    def emit(self):
        with self.nc.Block() as block:
            def mk(ename):
                def body(eng):
                    for waits, fn, inc in self.ops[ename]:
                        for sem, val in waits:
                            eng.wait_ge(sem, val)
                        if fn is not None:
                            name, a, kw = fn
                            getattr(eng, name)(*a, **kw).then_inc(inc[0], inc[1])
                return body
            block.tensor(mk("pe"))
            block.scalar(mk("act"))
            block.vector(mk("dve"))
            block.gpsimd(mk("pool"))
            block.sync(mk("sp"))


class Arena:
    def __init__(self, tile, ncols):
        self.tile, self.ncols, self.off = tile, ncols, 0

    def reset(self):
        self.off = 0

    def alloc(self, cols):
        cols = (cols + 1) // 2 * 2
        assert self.off + cols <= self.ncols, ("arena overflow", self.off, cols, self.ncols)
        ap = self.tile[:, self.off:self.off + cols]
        self.off += cols
        return ap


class _Stop(Exception):
    pass


def build_nc(stop=99):
    nc = bass.Bass("TRN2", target_bir_lowering=False)

    def ckpt(k):
        if stop <= k:
            raise _Stop()

    def din(name, shape, dt=F32):
        return nc.dram_tensor(name, list(shape), dt, kind="ExternalInput").ap()

    def dout(name, shape, dt=F32):
        return nc.dram_tensor(name, list(shape), dt, kind="ExternalOutput").ap()

    xp = din("xp", [NP_, D]); xsm = din("xsm", [NS, D])
    cp_bc = din("cp_bc", [128, D]); cs = din("cs", [NS, D])
    cbufa_d = din("cbufa", [128, 8 * 2 * NS]); cbufq_d = din("cbufq", [128, 24 * 3 * NS])
    sd = din("sd", [NS, 8, 128, 128])
    ada_w = din("ada_w", [D, 3 * D]); adab_bc_d = din("adab_bc", [128, D]); npost_bc_d = din("npost_bc", [128, D])
    vecs_d = din("vecs", [128, 160]); hv_d = din("hv", [8, 2])
    w_in = din("w_in", [D, INC]); w_out_a = din("w_out_a", [D, D]); w_out_b = din("w_out_b", [D, D]); w_o = din("w_o", [D, D])
    ident_d = din("ident", [128, 128]); biasA_d = din("biasA", [128, 128]); biasB_d = din("biasB", [128, 128])
    negmA_d = din("negmA", [128, 512]); sel_d = din("sel", [8, 8 * 128]); bm_d = din("bm", [8, 128])
    mks_d = din("mks", [128, 7 * 128])

    y_p = dout("y_p", [NP_, D]); y_s = dout("y_s", [NS, D])
    tail_a_p_d = dout("tail_a_p", [128, 8 * 2]); tail_a_s_d = dout("tail_a_s", [128, 8 * 2 * NS])
    tail_q_p_d = dout("tail_q_p", [128, 24 * 3]); tail_q_s_d = dout("tail_q_s", [128, 24 * 3 * NS])
    S_p_d = dout("S_p", [8, 128, 128]); S_s_d = dout("S_s", [NS, 8, 128, 128])

    def _body(st, P):

        def sb(name, shape, dt=F32):
            return st.enter_context(nc.sbuf_tensor("sb_" + name, list(shape), dt))

        def psb(name, shape, dt=F32):
            return st.enter_context(nc.psum_tensor("ps_" + name, list(shape), dt))

        hT = sb("hT", [128, 8, TT], BF16)
        XBs = sb("XBs", [128, 8, NS], BF16)
        xbs_d = nc.dram_tensor("xbs_scratch", [8, 128, TT], BF16).ap()
        wts = [sb(f"wt{i}", [128, 8, 512], BF16) for i in range(2)]
        ident = sb("ident", [128, 128]); identb = sb("identb", [128, 128], BF16)
        ones_f = sb("ones_f", [128, 128]); ones_b = sb("ones_b", [128, 128], BF16)
        biasA = sb("biasA", [128, 128]); biasB = sb("biasB", [128, 128])
        negmA = sb("negmA", [128, 512]); mkb = sb("mkb", [128, 7, 512], BF16); identb2 = sb("identb2", [128, 512], BF16)
        sel = sb("sel", [8, 8 * 128]); bm = sb("bm", [8, 128])
        vecs = sb("vecs", [128, 160]); hv = sb("hv", [8, 2])
        modfm = sb("modfm", [128, 16, 17])
        gn = sb("gn", [128, D]); gns = sb("gns", [NS, D])
        a_p = sb("a_p", [128, 8]); A_s = sb("A_s", [128, 8, NS])
        cbufa = sb("cbufa", [128, 8, 2, NS]); cbufq = sb("cbufq", [128, 24, 3, NS])
        tail_a_p = sb("tail_a_p", [128, 8, 2]); tail_a_s = sb("tail_a_s", [128, 8, 2, NS])
        tail_q_p = sb("tail_q_p", [128, 24, 3]); tail_q_s = sb("tail_q_s", [128, 24, 3, NS])
        gc_tok = sb("gc_tok", [128, 128]); sb_tok = sb("sb_tok", [128, 128])
        egc = sb("egc", [128, 128]); ekd = sb("ekd", [128, 128]); egl = sb("egl", [128, 128])
        gcT_full = sb("gcT", [128, NP_]); gcT = gcT_full[0:8, :]; beta_s = sb("beta_s", [8, NS]); g_s = sb("g_s", [8, NS])
        qs_s = sb("qs_s", [128, 8, NS]); ks_s = sb("ks_s", [128, 8, NS]); vs_s = sb("vs_s", [128, 8, NS])
        ARENA_F = 27800
        arena_t = sb("arena", [128, ARENA_F])
        arena = Arena(arena_t, ARENA_F)

        def af(cols):
            return arena.alloc(cols)

        def ab(cols):
            return arena.alloc((cols + 1) // 2).bitcast(BF16)

        B = [psb(f"B{i}", [128, 512]) for i in range(5)]
        B5 = psb("B5", [128, 1024], BF16); B6 = psb("B6", [128, 1024], BF16)
        B7 = psb("B7", [128, 512])

        def dve(fn, r, w): return P.op("dve", fn, reads=r, writes=w)
        def act(fn, r, w): return P.op("act", fn, reads=r, writes=w)
        def pool(fn, r, w): return P.op("pool", fn, reads=r, writes=w)
        def pe(fn, r, w): return P.op("pe", fn, reads=r, writes=w)
        def dma(fn, r, w, eng="sp"): return P.op(eng, fn, reads=r, writes=w, dma=True)

        def rsqrt_small(out_ap, in_ap, scale, eps, rk, wk, tmp_ap, tmpk):
            dve(lambda e: e.tensor_scalar(out=tmp_ap, in0=in_ap, scalar1=scale, scalar2=eps, op0=ALU.mult, op1=ALU.add), rk, [tmpk])
            act(lambda e: e.activation(out=tmp_ap, in_=tmp_ap, func=AF.Sqrt), [tmpk], [tmpk])
            dve(lambda e: e.reciprocal(out=out_ap, in_=tmp_ap), [tmpk], wk)

        wstate = {"n": 0}

        def load_w(dram, col_offs, width):
            i = wstate["n"] % 2
            wstate["n"] += 1
            wt = wts[i]
            key = f"wt{i}"
            runs = []
            for j, c in enumerate(col_offs):
                if runs and runs[-1][1] + runs[-1][2] == c:
                    runs[-1][2] += width
                else:
                    runs.append([j * width, c, width])
            for dst, c, wd in runs:
                src = dram[:, c:c + wd].rearrange("(k p) n -> p k n", p=128)
                dma(lambda e, dst=dst, wd=wd, src=src: e.dma_start(out=wt[:, :, dst:dst + wd], in_=src),
                    [], [key], eng="pool")
            return wt, key

        wq = {}

        def get_w(tag, specs, i):
            for j in (i, i + 1):
                if j < len(specs) and (tag, j) not in wq:
                    wq[(tag, j)] = load_w(*specs[j])
            return wq[(tag, i)]

        pj = {"n": 0}

        def proj_fm(wt, wkey, j, width, evac, blocks=BLK, rhs=None, rkey="hT", M=128, bankset=(0, 1)):
            rhs = hT if rhs is None else rhs
            for nb, (t0, n) in enumerate(blocks):
                bi = bankset[pj["n"] % 2]
                pj["n"] += 1
                bank, bkey = B[bi], f"B{bi}"
                for k in range(8):
                    pe(lambda e, k=k, bank=bank, t0=t0, n=n: e.matmul(
                        out=bank[0:M, 0:n], lhsT=wt[:, k, j * width:j * width + M], rhs=rhs[:, k, t0:t0 + n],
                        start=(k == 0), stop=(k == 7)), [wkey, rkey], [bkey])
                evac(nb, bank[0:M, 0:n], t0, n, bkey)

        for t, d, k in [(ident, ident_d, "ident"), (biasA, biasA_d, "biasA"), (biasB, biasB_d, "biasB"),
                        (negmA, negmA_d, "negmA"), (sel, sel_d, "sel"), (bm, bm_d, "bm"), (vecs, vecs_d, "vecs"),
                        (hv, hv_d, "hv"), (gn, npost_bc_d, "gn")]:
            dma(lambda e, t=t, d=d: e.dma_start(out=t[:], in_=d), [], [k])
        dma(lambda e: e.dma_start(out=cbufa[:].rearrange("p a b c -> p (a b c)"), in_=cbufa_d), [], ["cbufa"])
        dma(lambda e: e.dma_start(out=cbufq[:].rearrange("p a b c -> p (a b c)"), in_=cbufq_d), [], ["cbufq"])
        pool(lambda e: e.memset(ones_f[:], 1.0), [], ["ones_f"])
        pool(lambda e: e.memset(ones_b[:], 1.0), [], ["ones_b"])
        dve(lambda e: e.tensor_copy(out=identb[:], in_=ident[:]), ["ident"], ["identb"])
        for c in range(4):
            dve(lambda e, c=c: e.tensor_copy(out=identb2[:, c * 128:(c + 1) * 128], in_=ident[:]), ["ident"], ["identb2"])
            dma(lambda e, c=c: e.dma_start(out=mkb[:, :, c * 128:(c + 1) * 128], in_=mks_d.rearrange("p (l n) -> p l n", l=7)), [], ["mkb"], eng="pool")
        V_ADAB, V_NPRE, V_CA, V_ON, V_CB = 0, 24, 32, 56, 64

        ckpt(0.1)
        arena.reset()
        cpt = af(D); cst = af(D); gt = af(D)
        dma(lambda e: e.dma_start(out=gt, in_=adab_bc_d), [], ["gt"])
        scTp = ab(8 * 128).rearrange("p (k n) -> p k n", k=8)
        sc17 = ab(8 * 17).rearrange("p (k n) -> p k n", k=8)
        dma(lambda e: e.dma_start(out=cpt, in_=cp_bc), [], ["cpt"])
        dma(lambda e: e.dma_start(out=cst[0:NS, :], in_=cs), [], ["cst"])
        act(lambda e: e.activation(out=cpt, in_=cpt, func=AF.Silu), ["cpt"], ["cpt"])
        act(lambda e: e.activation(out=cst[0:NS, :], in_=cst[0:NS, :], func=AF.Silu), ["cst"], ["cst"])
        for half in range(2):
            bank, bkey = B[2 + half], f"B{2 + half}"
            for kk in range(4):
                k = half * 4 + kk
                pe(lambda e, k=k, kk=kk, bank=bank: e.transpose(out=bank[:, kk * 128:(kk + 1) * 128], in_=cpt[:, k * 128:(k + 1) * 128],
                                                              identity=ident[:]), ["cpt", "ident"], [bkey])
            dve(lambda e, half=half, bank=bank: e.tensor_copy(out=scTp[:, half * 4:half * 4 + 4, :],
                                                              in_=bank[:, :].rearrange("p (k n) -> p k n", k=4)), [bkey], ["scTp"])
        for k in range(8):
            pe(lambda e, k=k: e.transpose(out=B[4][:, k * NS:(k + 1) * NS], in_=cst[0:NS, k * 128:(k + 1) * 128],
                                          identity=ident[0:NS, 0:NS]), ["cst", "ident"], ["B4"])
        dve(lambda e: e.tensor_copy(out=sc17[:, :, 1:17], in_=B[4][:, 0:8 * NS].rearrange("p (k n) -> p k n", k=8)), ["B4"], ["sc17"])
        dve(lambda e: e.tensor_copy(out=sc17[:, :, 0:1], in_=scTp[:, :, 0:1]), ["scTp"], ["sc17"])
        ckpt(0.2)
        for g in range(6):
            wt, wkey = get_w("ada", [(ada_w, [gg * 512], 512) for gg in range(6)], g)
            if g < 4:
                for j in range(4):
                    fc = g * 4 + j
                    for k in range(8):
                        pe(lambda e, k=k, j=j, wt=wt: e.matmul(out=B[2][:, 0:17], lhsT=wt[:, k, j * 128:(j + 1) * 128], rhs=sc17[:, k, :],
                                                               start=(k == 0), stop=(k == 7)), [wkey, "sc17"], ["B2"])
                    act(lambda e, fc=fc: e.activation(out=modfm[:, fc, :], in_=B[2][:, 0:17], func=AF.Identity,
                                                      bias=vecs[:, V_ADAB + fc:V_ADAB + fc + 1], scale=1.0), ["B2", "vecs"], ["modfm"])
            else:
                n0 = (g - 4) * 512
                for k in range(8):
                    pe(lambda e, k=k, wt=wt: e.matmul(out=B[3][0:NS, :], lhsT=sc17[:, k, 1:17], rhs=wt[:, k, :],
                                                      start=(k == 0), stop=(k == 7)), [wkey, "sc17"], ["B3"])
                dve(lambda e, n0=n0: e.tensor_tensor(out=gns[:, n0:n0 + 512], in0=B[3][0:NS, :], in1=gt[0:NS, n0:n0 + 512], op=ALU.add),
                    ["B3", "gt"], ["gns"])
                dve(lambda e, n0=n0: e.tensor_tensor(out=gns[:, n0:n0 + 512], in0=gns[:, n0:n0 + 512], in1=gn[0:NS, n0:n0 + 512], op=ALU.mult),
                    ["gns", "gn"], ["gns"])
                for k in range(8):
                    pe(lambda e, k=k, wt=wt: e.matmul(out=B[4][:, :], lhsT=scTp[:, k, :], rhs=wt[:, k, :],
                                                      start=(k == 0), stop=(k == 7)), [wkey, "scTp"], ["B4"])
                dve(lambda e, n0=n0: e.tensor_tensor(out=gt[:, n0:n0 + 512], in0=B[4][:, :], in1=gt[:, n0:n0 + 512], op=ALU.add),
                    ["B4", "gt", "gns"], ["gt"])
                dve(lambda e, n0=n0: e.tensor_tensor(out=gn[:, n0:n0 + 512], in0=gn[:, n0:n0 + 512], in1=gt[:, n0:n0 + 512], op=ALU.mult),
                    ["gt", "gn", "gns"], ["gn"])
        ckpt(0.3)
        dve(lambda e: e.tensor_scalar(out=a_p[:], in0=modfm[:, 8:16, 0], scalar1=1.0, scalar2=None, op0=ALU.add), ["modfm"], ["a_p"])
        dve(lambda e: e.tensor_tensor(out=a_p[:], in0=a_p[:], in1=vecs[:, V_NPRE:V_NPRE + 8], op=ALU.mult), ["a_p", "vecs"], ["a_p"])
        ckpt(0.4)
        dve(lambda e: e.tensor_scalar(out=A_s[:], in0=modfm[:, 8:16, 1:17], scalar1=1.0, scalar2=None, op0=ALU.add), ["modfm"], ["A_s"])
        ckpt(0.5)
        for k in range(8):
            dve(lambda e, k=k: e.tensor_scalar(out=A_s[:, k, :], in0=A_s[:, k, :], scalar1=vecs[:, V_NPRE + k:V_NPRE + k + 1], scalar2=None,
                                               op0=ALU.mult), ["A_s", "vecs"], ["A_s"])

        ckpt(1)
        P.fence(); arena.reset()
        xt = [af(D), af(D)]
        xh = [af(D), af(D)]
        junk = ab(D)
        ssx = af(18); rstdx = af(18); tmpx = af(18)
        tmps = af(8 * NS)

        def p1_stage1(i):
            rows = 128 if i < 16 else NS
            xi, xk = xt[i % 2], f"xt{i % 2}"
            hi, hk = xh[i % 2], f"xh{i % 2}"
            src = xp[i * 128:(i + 1) * 128, :] if i < 16 else xsm
            dma(lambda e: e.dma_start(out=xi[0:rows, :], in_=src), [], [xk])
            act(lambda e: e.activation(out=junk[0:rows, :], in_=xi[0:rows, :], func=AF.Square, accum_out=ssx[0:rows, i:i + 1]),
                [xk], ["junk", f"ssx{i}"])
            rsqrt_small(rstdx[0:rows, i:i + 1], ssx[0:rows, i:i + 1], 1.0 / D, EPS, [f"ssx{i}"], [f"rstdx{i}"], tmpx[0:rows, i:i + 1], f"tmpx{i}")
            dve(lambda e: e.tensor_scalar(out=hi[0:rows, :], in0=xi[0:rows, :], scalar1=rstdx[0:rows, i:i + 1], scalar2=None, op0=ALU.mult),
                [xk, f"rstdx{i}"], [hk])

        def p1_stage2(i):
            hi, hk = xh[i % 2], f"xh{i % 2}"
            pair = [(B[2], "B2"), (B[3], "B3")] if i % 2 == 0 else [(B[4], "B4"), (B[1], "B1")]
            if i < 16:
                for half in range(2):
                    bank, bkey = pair[half]
                    for kk in range(4):
                        k = half * 4 + kk
                        pe(lambda e, k=k, kk=kk, bank=bank: e.transpose(out=bank[:, kk * 128:(kk + 1) * 128], in_=hi[:, k * 128:(k + 1) * 128],
                                                                     identity=ident[:]), [hk, "ident"], [bkey])
                    for kk in range(4):
                        k = half * 4 + kk
                        if kk % 2 == 0:
                            act(lambda e, k=k, kk=kk, bank=bank: e.activation(
                                out=hT[:, k, i * 128:(i + 1) * 128], in_=bank[:, kk * 128:(kk + 1) * 128], func=AF.Identity,
                                bias=modfm[:, k, 0:1], scale=a_p[:, k:k + 1]), [bkey, "modfm", "a_p"], [f"hT{i // 4}"])
                        else:
                            dve(lambda e, k=k, kk=kk, bank=bank: e.tensor_scalar(
                                out=hT[:, k, i * 128:(i + 1) * 128], in0=bank[:, kk * 128:(kk + 1) * 128], scalar1=a_p[:, k:k + 1],
                                scalar2=modfm[:, k, 0:1], op0=ALU.mult, op1=ALU.add), [bkey, "modfm", "a_p"], [f"hT{i // 4}"])
            else:
                bank, bkey = pair[0]
                for k in range(8):
                    pe(lambda e, k=k: e.transpose(out=bank[:, k * NS:(k + 1) * NS], in_=hi[0:NS, k * 128:(k + 1) * 128],
                                                  identity=ident[0:NS, 0:NS]), [hk, "ident"], [bkey])
                t3 = tmps.rearrange("p (k n) -> p k n", k=8)
                dve(lambda e: e.tensor_tensor(out=t3, in0=bank[:, 0:8 * NS].rearrange("p (k n) -> p k n", k=8), in1=A_s[:], op=ALU.mult),
                    [bkey, "A_s"], ["tmps"])
                dve(lambda e: e.tensor_tensor(out=hT[:, :, NP_:TT], in0=t3, in1=modfm[:, 0:8, 1:17], op=ALU.add), ["tmps", "modfm"], ["hT4"])

        p1_stage1(0)
        p1_stage1(1)
        for i in range(17):
            p1_stage2(i)
            if i + 2 < 17:
                p1_stage1(i + 2)

        ckpt(2)
        ckpt(3)
        def col(t, c, h):
            return t[:, c * 8 + h:c * 8 + h + 1]

        P.fence(); arena.reset()
        GC = 4
        GW = GC * 128
        NG = 16 // GC
        qTs = [ab(TT), ab(TT)]; kTs = [ab(TT), ab(TT)]; vTs = [ab(TT), ab(TT)]; szBs = [ab(TT), ab(TT)]
        pre = af(3 + TT); acc = af(TT); sqs = ab(NP_)
        hsc = [af(16 * 8), af(16 * 8)]
        o_alias = arena.off
        setT = []
        for s_ in range(2):
            setT.append(dict(
                Ktok=ab(GW), Kg=ab(GW), KpT=ab(GW), Dm=af(GW), Dm2=af(GW), En=af(GW),
                Mb=ab(GW), Nb=ab(GW), D=ab(GW), Xp=ab(GW), Xb=ab(GW),
                Tb=ab(GW), nW=ab(GW), Vs=ab(GW), Kd=ab(GW), at=ab(GW),
                oraw=af(GW), og=ab(GW), xo=ab(GW), sso=af(GC), ro=af(GC),
                T=(B5 if s_ == 0 else B6), Tk=("B5" if s_ == 0 else "B6"),
                X=B[1 + 2 * s_], Xk=f"B{1 + 2 * s_}", Y=B[2 + 2 * s_], Yk=f"B{2 + 2 * s_}"))
        S = af(128); Sb = ab(128); vnb = ab(128); o1s = af(128); junk2 = ab(128)
        dve(lambda e: e.memset(pre[:, 0:3], 0.0), [], ["pre"])
        done = set()
        o_end = arena.off
        arena.off = o_alias
        xg = af(TT); axg = af(TT); lg = axg; negA = af(2)
        betaT = af(TT)[0:8, :]; gT = af(TT)[0:8, :]; sbT = af(NP_)[0:8, :]
        tmpl = af(128)
        assert arena.off <= o_end, (arena.off, o_end)
        arena.off = o_end

        def wait_for(key):
            while key not in done:
                yield "blocked"

        def rsqrt_ln(out_ap, in_ap, scale, eps, rk, wk, tmp_ap, tmpk):
            dve(lambda e: e.tensor_scalar(out=tmp_ap, in0=in_ap, scalar1=scale, scalar2=eps, op0=ALU.mult, op1=ALU.add), rk, [tmpk])
            act(lambda e: e.activation(out=tmp_ap, in_=tmp_ap, func=AF.Ln), [tmpk], [tmpk])
            act(lambda e: e.activation(out=out_ap, in_=tmp_ap, func=AF.Exp, scale=-0.5), [tmpk], wk)

        def conv_task(h):
            hb = h % 2
            qT, kT, vT, szB = qTs[hb], kTs[hb], vTs[hb], szBs[hb]
            HS = hsc[hb]
            ssq, rqk, fsc, fg, fd, sbh, rqe, tmpq = (HS[:, 0:32], HS[:, 32:64], HS[:, 64:80], HS[:, 80:96], HS[:, 96:112],
                                                     HS[:, 112:128], None, None)
            wt, wkey = get_w("head", [(w_in, [OFF_Q + hh * 128, OFF_K + hh * 128, OFF_V + hh * 128, OFF_ZB + hh * 128], 128) for hh in range(8)], h)
            if h == 7:
                wq[("brA", 0)] = load_w(w_in, [OFF_HA, OFF_CA, OFF_BA, OFF_ZA], 128)
            for j, (dst, dname, dsamp) in enumerate([(qT, f"qT{hb}", qs_s), (kT, f"kT{hb}", ks_s), (vT, f"vT{hb}", vs_s)]):
                ch = j * 8 + h
                for nb, (t0, n) in enumerate(BLK):
                    for k in range(8):
                        pe(lambda e, k=k, t0=t0, n=n: e.matmul(out=B[0][:, 0:n], lhsT=wt[:, k, j * 128:(j + 1) * 128], rhs=hT[:, k, t0:t0 + n],
                                                              start=(k == 0), stop=(k == 7)), [wkey, "hT"], ["B0"])
                        if k == 3 and n > 16:
                            yield
                    if nb % 2 == 0:
                        act(lambda e, t0=t0, n=n: e.activation(out=pre[:, 3 + t0:3 + t0 + n], in_=B[0][:, 0:n], func=AF.Copy), ["B0"], ["pre"])
                    else:
                        dve(lambda e, t0=t0, n=n: e.tensor_copy(out=pre[:, 3 + t0:3 + t0 + n], in_=B[0][:, 0:n]), ["B0"], ["pre"])
                    yield
                pool(lambda e, ch=ch: e.tensor_copy(out=tail_q_p[:, ch, :], in_=pre[:, NP_:NP_ + 3]), ["pre"], ["tail_q_p"])
                pool(lambda e, ch=ch: e.tensor_copy(out=tail_q_s[:, ch, 0:2, :], in_=cbufq[:, ch, 1:3, :]), ["cbufq"], ["tail_q_s"])
                pool(lambda e, ch=ch: e.tensor_copy(out=tail_q_s[:, ch, 2, :], in_=pre[:, 3 + NP_:3 + TT]), ["pre"], ["tail_q_s"])
                wc = [vecs[:, V_CB + t * 24 + ch:V_CB + t * 24 + ch + 1] for t in range(4)]
                HB = NP_ // 2
                for hf in range(2):
                    c0 = hf * HB
                    pool(lambda e, wc=wc, c0=c0: e.tensor_scalar(out=acc[:, c0:c0 + HB], in0=pre[:, c0:c0 + HB], scalar1=wc[0], scalar2=0.0,
                                                                 op0=ALU.mult, op1=ALU.add), ["pre", "vecs"], [f"acc{hf}"])
                    yield
                for t in range(1, 4):
                    for hf in range(2):
                        c0 = hf * HB
                        dve(lambda e, t=t, wc=wc, c0=c0: e.scalar_tensor_tensor(out=acc[:, c0:c0 + HB], in0=pre[:, c0 + t:c0 + t + HB], scalar=wc[t],
                                                                                in1=acc[:, c0:c0 + HB], op0=ALU.mult, op1=ALU.add),
                            ["pre", f"acc{hf}", "vecs"], [f"acc{hf}"])
                        yield
                pool(lambda e, wc=wc: e.tensor_scalar(out=acc[:, NP_:TT], in0=pre[:, 3 + NP_:3 + TT], scalar1=wc[3], scalar2=0.0, op0=ALU.mult, op1=ALU.add),
                     ["pre", "vecs"], ["acc"])
                for t in range(3):
                    dve(lambda e, t=t, wc=wc, ch=ch: e.scalar_tensor_tensor(out=acc[:, NP_:TT], in0=cbufq[:, ch, t, :], scalar=wc[t], in1=acc[:, NP_:TT],
                                                                            op0=ALU.mult, op1=ALU.add), ["cbufq", "acc", "vecs"], ["acc"])
                for hf in range(2):
                    c0 = hf * HB
                    act(lambda e, dst=dst, c0=c0: e.activation(out=dst[:, c0:c0 + HB], in_=acc[:, c0:c0 + HB], func=AF.Silu), [f"acc{hf}"], [dname])
                    yield
                act(lambda e, dsamp=dsamp: e.activation(out=dsamp[:, h, :], in_=acc[:, NP_:TT], func=AF.Silu), ["acc"], [dname + "_s"])
                yield
            for nb, (t0, n) in enumerate(BLK):
                for k in range(8):
                    pe(lambda e, k=k, t0=t0, n=n: e.matmul(out=B[0][:, 0:n], lhsT=wt[:, k, 384:512], rhs=hT[:, k, t0:t0 + n],
                                                          start=(k == 0), stop=(k == 7)), [wkey, "hT"], ["B0"])
                    if k == 3 and n > 16:
                        yield
                act(lambda e, t0=t0, n=n: e.activation(out=szB[:, t0:t0 + n], in_=B[0][:, 0:n], func=AF.Silu), ["B0"], [f"szB{hb}"])
                yield
            dve(lambda e: e.tensor_copy(out=XBs[:, h, :], in_=szB[:, NP_:TT]), [f"szB{hb}"], ["XBs"])
            for qi, (src, sname) in enumerate([(qT, f"qT{hb}"), (kT, f"kT{hb}")]):
                act(lambda e, src=src: e.activation(out=sqs, in_=src[:, 0:NP_], func=AF.Square), [sname], ["sqs"])
                for c in range(16):
                    pe(lambda e, c=c, qi=qi: e.matmul(out=B[0][:, qi * 16 + c:qi * 16 + c + 1], lhsT=sqs[:, c * 128:(c + 1) * 128],
                                                      rhs=ones_b[:, 0:1], start=True, stop=True), ["sqs", "ones_b"], ["B0"])
                yield
            hk = f"hsc{hb}"
            dve(lambda e: e.tensor_copy(out=ssq, in_=B[0][:, 0:32]), ["B0"], [hk])
            rsqrt_ln(rqk, ssq, 1.0, EPS, [hk], [hk], ssq, hk)
            dve(lambda e: e.tensor_scalar(out=rqk[:, 0:16], in0=rqk[:, 0:16], scalar1=128.0 ** -0.5, scalar2=None, op0=ALU.mult), [hk], [hk])
            sbv = sb_tok[:, :].rearrange("p (c h) -> p c h", c=16)[:, :, h]
            egv = egc[:, :].rearrange("p (c h) -> p c h", c=16)[:, :, h]
            ekv = ekd[:, :].rearrange("p (c h) -> p c h", c=16)[:, :, h]
            dve(lambda e: e.tensor_copy(out=sbh, in_=sbv), ["sb_tok"], [hk])
            dve(lambda e: e.tensor_tensor(out=fsc, in0=rqk[:, 16:32], in1=sbv, op=ALU.mult), [hk, "sb_tok"], [hk])
            dve(lambda e: e.tensor_tensor(out=fg, in0=fsc, in1=egv, op=ALU.mult), [hk, "egc"], [hk])
            dve(lambda e: e.tensor_tensor(out=fd, in0=fsc, in1=ekv, op=ALU.mult), [hk, "ekd"], [hk])
            dve(lambda e: e.tensor_tensor(out=ssq[:, 0:16], in0=rqk[:, 0:16], in1=egv, op=ALU.mult), [hk, "egc"], [hk])
            done.add(("conv", h))
            yield

        def prep_task(h, g):
            hb = h % 2
            s_ = g % 2
            Z = setT[s_]
            qT, kT, vT = qTs[hb], kTs[hb], vTs[hb]
            HS = hsc[hb]
            fsc, fg, fd, sbh = HS[:, 64:80], HS[:, 80:96], HS[:, 96:112], HS[:, 112:128]
            hk = f"hsc{hb}"
            T, Tk, X, Xk, Y, Yk = Z["T"], Z["Tk"], Z["X"], Z["Xk"], Z["Y"], Z["Yk"]
            K = lambda n: f"{n}{s_}"
            g0 = g * GW
            pe(lambda e: e.matmul(out=X[:, 0:GW], lhsT=sel[:, h * 128:(h + 1) * 128], rhs=gcT[:, g0:g0 + GW], start=True, stop=True),
               ["sel", "gcT"], [Xk])
            for cc in range(GC):
                c0 = g0 + cc * 128
                pe(lambda e, cc=cc, c0=c0: e.transpose(out=T[:, cc * 128:(cc + 1) * 128], in_=kT[:, c0:c0 + 128], identity=identb[:]),
                   [f"kT{hb}", "identb"], [Tk])
                pe(lambda e, cc=cc, c0=c0: e.transpose(out=T[:, GW + cc * 128:GW + (cc + 1) * 128], in_=vT[:, c0:c0 + 128], identity=identb[:]),
                   [f"vT{hb}", "identb"], [Tk])
            yield
            for cc in range(GC):
                c = g * GC + cc
                sl = slice(cc * 128, (cc + 1) * 128)
                vsl = slice(GW + cc * 128, GW + (cc + 1) * 128)
                dve(lambda e, sl=sl, c=c: e.tensor_scalar(out=Z["Ktok"][:, sl], in0=T[:, sl], scalar1=fsc[:, c:c + 1], scalar2=None, op0=ALU.mult),
                    [Tk, hk], [K("Ktok")])
                dve(lambda e, sl=sl, c=c: e.tensor_scalar(out=Z["Kg"][:, sl], in0=T[:, sl], scalar1=fg[:, c:c + 1], scalar2=None, op0=ALU.mult),
                    [Tk, hk], [K("Kg")])
                act(lambda e, sl=sl, c=c: e.activation(out=Z["Kd"][:, sl], in_=T[:, sl], func=AF.Identity, scale=fd[:, c:c + 1]),
                    [Tk, hk], [K("Kd")])
                act(lambda e, sl=sl, vsl=vsl, c=c: e.activation(out=Z["Vs"][:, sl], in_=T[:, vsl], func=AF.Identity, scale=sbh[:, c:c + 1]),
                    [Tk, hk], [K("Vs")])
                dve(lambda e, sl=sl, c=c: e.scalar_tensor_tensor(out=Z["Dm"][:, sl], in0=X[:, sl], scalar=col(gc_tok, c, h), in1=biasA[:],
                                                                 op0=ALU.subtract, op1=ALU.add), [Xk, "gc_tok", "biasA"], [K("Dm")])
                dve(lambda e, sl=sl, c=c: e.scalar_tensor_tensor(out=Z["Dm2"][:, sl], in0=X[:, sl], scalar=col(gc_tok, c, h), in1=biasB[:],
                                                                 op0=ALU.subtract, op1=ALU.add), [Xk, "gc_tok", "biasB"], [K("Dm2")])
            yield
            for cc in range(GC):
                sl = slice(cc * 128, (cc + 1) * 128)
                pe(lambda e, sl=sl, cc=cc: e.transpose(out=T[:, cc * 128:(cc + 1) * 128], in_=Z["Ktok"][:, sl], identity=identb[:]),
                   [K("Ktok"), "identb"], [Tk])
            yield
            act(lambda e: e.activation(out=Z["KpT"], in_=T[:, 0:GW], func=AF.Copy), [Tk], [K("KpT")])
            act(lambda e: e.activation(out=Z["Dm"], in_=Z["Dm"], func=AF.Exp), [K("Dm")], [K("Dm")])
            act(lambda e: e.activation(out=Z["Dm2"], in_=Z["Dm2"], func=AF.Exp, scale=-1.0), [K("Dm2")], [K("Dm2")])
            dve(lambda e: e.tensor_tensor(out=Z["En"], in0=Z["Dm"], in1=negmA[:, 0:GW], op=ALU.mult), [K("Dm"), "negmA"], [K("En")])
            yield
            for cc in range(GC):
                sl = slice(cc * 128, (cc + 1) * 128)
                c0 = g0 + cc * 128
                pe(lambda e, sl=sl: e.matmul(out=Y[:, sl], lhsT=Z["KpT"][:, sl], rhs=Z["KpT"][:, sl], start=True, stop=True), [K("KpT")], [Yk])
                pe(lambda e, sl=sl, cc=cc, c0=c0: e.matmul(out=X[:, cc * 128:(cc + 1) * 128], lhsT=Z["KpT"][:, sl], rhs=qT[:, c0:c0 + 128],
                                                           start=True, stop=True), [K("KpT"), f"qT{hb}"], [Xk])
            yield
            dve(lambda e: e.tensor_tensor(out=Z["at"], in0=X[:, 0:GW], in1=Z["Dm"], op=ALU.mult), [Xk, K("Dm")], [K("at")])
            dve(lambda e: e.tensor_tensor(out=Z["Mb"], in0=Y[:, 0:GW], in1=Z["En"], op=ALU.mult), [Yk, K("En")], [K("Mb")])
            dve(lambda e: e.scalar_tensor_tensor(out=Z["Nb"], in0=Y[:, 0:GW], scalar=-1.0, in1=Z["Dm2"], op0=ALU.mult, op1=ALU.mult),
                [Yk, K("Dm2")], [K("Nb")])
            yield
            Dt = Z["Tb"]
            D = Z["D"]
            U16 = mybir.dt.uint16
            mku = mkb[:].bitcast(U16)
            dve(lambda e: e.tensor_copy(out=D, in_=identb2[:, 0:GW]), ["identb2"], [K("D")])
            act(lambda e: e.activation(out=Dt, in_=identb2[:, 0:GW], func=AF.Copy), ["identb2"], [K("Tb")])
            yield
            dve(lambda e: e.copy_predicated(out=D, mask=mku[:, 0, 0:GW], data=Z["Nb"]), [K("Nb"), "mkb", K("D")], [K("D")])
            dve(lambda e: e.copy_predicated(out=Dt, mask=mku[:, 0, 0:GW], data=Z["Mb"]), [K("Mb"), "mkb", K("Tb")], [K("Tb")])
            yield
            for lvl in range(1, 7):
                last = (lvl == 6)
                for cc in range(GC):
                    sl = slice(cc * 128, (cc + 1) * 128)
                    pe(lambda e, sl=sl, cc=cc: e.matmul(out=Y[:, cc * 128:(cc + 1) * 128], lhsT=Z["Nb"][:, sl], rhs=Dt[:, sl],
                                                        start=True, stop=True), [K("Nb"), K("Tb")], [Yk])
                if not last:
                    for cc in range(GC):
                        sl = slice(cc * 128, (cc + 1) * 128)
                        pe(lambda e, sl=sl, cc=cc: e.matmul(out=X[:, cc * 128:(cc + 1) * 128], lhsT=Z["Mb"][:, sl], rhs=D[:, sl],
                                                            start=True, stop=True), [K("Mb"), K("D")], [Xk])
                yield
                act(lambda e: e.activation(out=Z["Xb"], in_=Y[:, 0:GW], func=AF.Copy), [Yk], [K("Xb")])
                if not last:
                    act(lambda e: e.activation(out=Z["Xp"], in_=X[:, 0:GW], func=AF.Copy), [Xk], [K("Xp")])
                yield
                for cc in range(GC):
                    sl = slice(cc * 128, (cc + 1) * 128)
                    pe(lambda e, sl=sl, cc=cc: e.matmul(out=Y[:, cc * 128:(cc + 1) * 128], lhsT=D[:, sl], rhs=Z["Xb"][:, sl],
                                                        start=True, stop=True), [K("D"), K("Xb")], [Yk])
                if not last:
                    for cc in range(GC):
                        sl = slice(cc * 128, (cc + 1) * 128)
                        pe(lambda e, sl=sl, cc=cc: e.matmul(out=X[:, cc * 128:(cc + 1) * 128], lhsT=Dt[:, sl], rhs=Z["Xp"][:, sl],
                                                            start=True, stop=True), [K("Tb"), K("Xp")], [Xk])
                yield
                dve(lambda e, lvl=lvl: e.copy_predicated(out=Dt, mask=mku[:, lvl, 0:GW], data=Y[:, 0:GW]), [Yk, "mkb", K("Tb")], [K("Tb")])
                if not last:
                    dve(lambda e, lvl=lvl: e.copy_predicated(out=D, mask=mku[:, lvl, 0:GW], data=X[:, 0:GW]), [Xk, "mkb", K("D")], [K("D")])
                yield
            for cc in range(GC):
                sl = slice(cc * 128, (cc + 1) * 128)
                pe(lambda e, sl=sl: e.matmul(out=X[:, sl], lhsT=Z["Kg"][:, sl], rhs=Z["Tb"][:, sl], start=True, stop=True), [K("Kg"), K("Tb")], [Xk])
            yield
            act(lambda e: e.activation(out=Z["nW"], in_=X[:, 0:GW], func=AF.Identity, scale=-1.0), [Xk], [K("nW")])
            done.add(("prep", h, g))
            yield

        def prep_chain(h, parity):
            yield from wait_for(("conv", h))
            yield from wait_for("3a")
            for g in range(parity, NG, 2):
                if g >= 2:
                    yield from wait_for(("scan", h, g - 2))
                elif h > 0:
                    yield from wait_for(("scan", h - 1, NG - 2 + parity))
                yield from prep_task(h, g)

        def scan_chain(h):
            hb = h % 2
            qT, szB = qTs[hb], szBs[hb]
            HS = hsc[hb]
            rqk, rqe = HS[:, 32:64], HS[:, 0:16]
            hk = f"hsc{hb}"
            if h > 0:
                yield from wait_for(("scanhead", h - 1))
            else:
                yield from wait_for("3a")
            dve(lambda e: e.memset(S, 0.0), [], ["S"])
            dve(lambda e: e.memset(Sb, 0.0), [], ["Sb"])
            for g in range(NG):
                yield from wait_for(("prep", h, g))
                s_ = g % 2
                Z = setT[s_]
                K = lambda n: f"{n}{s_}"
                for cc in range(GC):
                    c = g * GC + cc
                    sl = slice(cc * 128, (cc + 1) * 128)
                    qsl = slice(c * 128, (c + 1) * 128)
                    pe(lambda e, sl=sl: e.matmul(out=B7[:, 0:128], lhsT=Z["Tb"][:, sl], rhs=Z["Vs"][:, sl], start=True, stop=False), [K("Tb"), K("Vs")], ["B7"])
                    pe(lambda e, sl=sl: e.matmul(out=B7[:, 0:128], lhsT=Z["nW"][:, sl], rhs=Sb, start=False, stop=True), [K("nW"), "Sb"], ["B7"])
                    act(lambda e: e.activation(out=vnb, in_=B7[:, 0:128], func=AF.Copy), ["B7"], ["vnb"])
                    pe(lambda e, sl=sl: e.matmul(out=B7[:, 384:512], lhsT=Z["Kd"][:, sl], rhs=vnb, start=True, stop=True), [K("Kd"), "vnb"], ["B7"])
                    pe(lambda e, qsl=qsl: e.matmul(out=B7[:, 128:256], lhsT=qT[:, qsl], rhs=Sb, start=True, stop=True), [f"qT{hb}", "Sb"], ["B7"])
                    pe(lambda e, sl=sl: e.matmul(out=B7[:, 256:384], lhsT=Z["at"][:, sl], rhs=vnb, start=True, stop=True), [K("at"), "vnb"], ["B7"])
                    dve(lambda e, c=c: e.scalar_tensor_tensor(out=S, in0=S, scalar=col(egl, c, h), in1=B7[:, 384:512], op0=ALU.mult, op1=ALU.add),
                        ["S", "egl", "B7"], ["S"])
                    act(lambda e: e.activation(out=Sb, in_=S, func=AF.Copy), ["S"], ["Sb"])
                    act(lambda e, c=c: e.activation(out=o1s, in_=B7[:, 128:256], func=AF.Identity, scale=rqe[:, c:c + 1]), ["B7", hk], ["o1s"])
                    dve(lambda e, c=c, sl=sl: e.scalar_tensor_tensor(out=Z["oraw"][:, sl], in0=B7[:, 256:384], scalar=rqk[:, c:c + 1], in1=o1s,
                                                                     op0=ALU.mult, op1=ALU.add), ["B7", hk, "o1s"], [K("oraw")])
                    act(lambda e, cc=cc, sl=sl: e.activation(out=junk2, in_=Z["oraw"][:, sl], func=AF.Square, accum_out=Z["sso"][:, cc:cc + 1]),
                        [K("oraw")], ["junk2", K("sso")])
                    yield
                rsqrt_ln(Z["ro"], Z["sso"], 1.0 / 128, EPS, [K("sso")], [K("ro")], Z["sso"], K("sso"))
                for cc in range(GC):
                    sl = slice(cc * 128, (cc + 1) * 128)
                    pool(lambda e, cc=cc, sl=sl: e.tensor_scalar(out=Z["og"][:, sl], in0=Z["oraw"][:, sl], scalar1=Z["ro"][:, cc:cc + 1], scalar2=0.0,
                                                                 op0=ALU.mult, op1=ALU.add), [K("oraw"), K("ro")], [K("og")])
                    pe(lambda e, cc=cc, sl=sl: e.transpose(out=Z["T"][:, GW + cc * 128:GW + (cc + 1) * 128], in_=Z["og"][:, sl], identity=identb[:]),
                       [K("og"), "identb"], [Z["Tk"]])
                t0 = g * GW
                dve(lambda e, t0=t0: e.scalar_tensor_tensor(out=Z["xo"], in0=Z["T"][:, GW:2 * GW], scalar=vecs[:, V_ON:V_ON + 1],
                                                            in1=szB[:, t0:t0 + GW], op0=ALU.mult, op1=ALU.mult),
                    [Z["Tk"], "vecs", f"szB{hb}"], [K("xo")])
                dma(lambda e, t0=t0: e.dma_start(out=xbs_d[h][:, t0:t0 + GW], in_=Z["xo"]), [K("xo")], ["xbs_dram"])
                done.add(("scan", h, g))
                yield
            dma(lambda e: e.dma_start(out=S_p_d[h], in_=S), ["S"], ["S_p_dram"])
            done.add(("scanhead", h))
            yield

        def phase3a_chain():
            HT_ALL = [f"hT{n}" for n in range(5)]
            wt, wkey = load_w(w_in, [OFF_BETA], 16)

            def ev_beta(nb, ps, t0, n, bkey):
                act(lambda e: e.activation(out=betaT[:, t0:t0 + n], in_=ps, func=AF.Sigmoid), [bkey], ["betaT"])
            proj_fm(wt, wkey, 0, 8, ev_beta, M=8, bankset=(1, 2))

            def ev_alpha(nb, ps, t0, n, bkey):
                dve(lambda e: e.tensor_scalar(out=xg[0:8, t0:t0 + n], in0=ps, scalar1=hv[:, 1:2], scalar2=None, op0=ALU.add), [bkey, "hv"], ["xg"])
            proj_fm(wt, wkey, 1, 8, ev_alpha, M=8, bankset=(1, 2))
            yield
            dve(lambda e: e.scalar_tensor_tensor(out=axg[0:8, :], in0=xg[0:8, :], scalar=-1.0, in1=xg[0:8, :], op0=ALU.mult, op1=ALU.max), ["xg"], ["axg"])
            act(lambda e: e.activation(out=axg[0:8, :], in_=axg[0:8, :], func=AF.Exp, scale=-1.0), ["axg"], ["axg"])
            act(lambda e: e.activation(out=lg[0:8, :], in_=axg[0:8, :], func=AF.Ln, bias=1.0, scale=1.0), ["axg"], ["lg"])
            act(lambda e: e.activation(out=negA[0:8, 0:1], in_=hv[:, 0:1], func=AF.Exp), ["hv"], ["negA"])
            dve(lambda e: e.tensor_scalar(out=negA[0:8, 0:1], in0=negA[0:8, 0:1], scalar1=-1.0, scalar2=None, op0=ALU.mult), ["negA"], ["negA"])
            dve(lambda e: e.scalar_tensor_tensor(out=lg[0:8, :], in0=xg[0:8, :], scalar=0.0, in1=lg[0:8, :], op0=ALU.max, op1=ALU.add),
                ["xg", "lg"], ["lg"])
            dve(lambda e: e.tensor_scalar(out=gT[:, :], in0=lg[0:8, :], scalar1=negA[0:8, 0:1], scalar2=None, op0=ALU.mult), ["lg", "negA"], ["gT"])
            act(lambda e: e.activation(out=sbT[:, :], in_=betaT[:, 0:NP_], func=AF.Sqrt), ["betaT"], ["sbT"])
            for c in range(16):
                dve(lambda e, c=c: e.tensor_tensor_scan(out=gcT[:, c * 128:(c + 1) * 128], data0=ones_f[0:8, 0:128], data1=gT[:, c * 128:(c + 1) * 128],
                                                        initial=0.0, op0=ALU.mult, op1=ALU.add), ["gT", "ones_f"], ["gcT"])
            yield
            for c in range(16):
                pe(lambda e, c=c: e.transpose(out=B[2][:, c * 8:(c + 1) * 8], in_=gcT[:, c * 128:(c + 1) * 128], identity=ident[0:8, 0:8]),
                   ["gcT", "ident"], ["B2"])
                pe(lambda e, c=c: e.transpose(out=B[3][:, c * 8:(c + 1) * 8], in_=sbT[:, c * 128:(c + 1) * 128], identity=ident[0:8, 0:8]),
                   ["sbT", "ident"], ["B3"])
            dve(lambda e: e.tensor_copy(out=gc_tok[:], in_=B[2][:, 0:128]), ["B2"], ["gc_tok"])
            dve(lambda e: e.tensor_copy(out=sb_tok[:], in_=B[3][:, 0:128]), ["B3"], ["sb_tok"])
            yield
            tl3 = tmpl[0:8, :].rearrange("p (c h) -> p c h", c=16)
            gl_src = gcT[:, :].rearrange("p (c t) -> p c t", c=16)[:, :, 127:128]
            dve(lambda e: e.tensor_tensor(out=tl3, in0=bm[:, :].rearrange("p (c h) -> p c h", c=16), in1=gl_src.to_broadcast([8, 16, 8]), op=ALU.mult),
                ["bm", "gcT"], ["tmpl"])
            pe(lambda e: e.matmul(out=B[4][:, 0:128], lhsT=ones_f[0:8, :], rhs=tmpl[0:8, :], start=True, stop=True), ["tmpl", "ones_f"], ["B4"])
            act(lambda e: e.activation(out=egl[:], in_=B[4][:, 0:128], func=AF.Exp), ["B4"], ["egl"])
            dve(lambda e: e.tensor_tensor(out=ekd[:], in0=B[4][:, 0:128], in1=gc_tok[:], op=ALU.subtract), ["B4", "gc_tok"], ["ekd"])
            act(lambda e: e.activation(out=ekd[:], in_=ekd[:], func=AF.Exp), ["ekd"], ["ekd"])
            act(lambda e: e.activation(out=egc[:], in_=gc_tok[:], func=AF.Exp), ["gc_tok"], ["egc"])
            dve(lambda e: e.tensor_copy(out=beta_s[:, :], in_=betaT[:, NP_:TT]), ["betaT"], ["beta_s"])
            dve(lambda e: e.tensor_copy(out=g_s[:, :], in_=gT[:, NP_:TT]), ["gT"], ["g_s"])


            P.fence()
            done.add("3a")
            yield

        def conv_chain():
            for h in range(8):
                if h >= 2:
                    yield from wait_for(("scanhead", h - 2))
                yield from conv_task(h)

        def run_chains(chains):
            active = list(chains)
            while active:
                progressed = False
                for ch in list(active):
                    try:
                        r = next(ch)
                        if r != "blocked":
                            progressed = True
                    except StopIteration:
                        active.remove(ch)
                        progressed = True
                assert progressed, "chain deadlock"

        chains = [phase3a_chain(), conv_chain()]
        for h in range(8):
            chains += [prep_chain(h, 0), prep_chain(h, 1), scan_chain(h)]
        run_chains(chains)

        ckpt(4)
        P.fence(); arena.reset()
        ZA = ab(8 * TT).rearrange("p (k t) -> p k t", k=8)
        za_end = arena.off
        B5f = B5[:, :].bitcast(F32)
        B6f = B6[:, :].bitcast(F32)

        def sample_chain():
            def bcast_hs(dst, srcT, skey, dkey):
                t = af(128)
                dve(lambda e: e.tensor_tensor(out=t[0:8, :].rearrange("p (h s) -> p h s", h=8), in0=bm[:, :].rearrange("p (s h) -> p h s", s=16),
                                              in1=srcT.unsqueeze(1).to_broadcast([8, 8, NS]), op=ALU.mult), ["bm", skey], [dkey + "_t"])
                pe(lambda e: e.matmul(out=B[2][:, 0:128], lhsT=ones_f[0:8, :], rhs=t[0:8, :], start=True, stop=True), [dkey + "_t", "ones_f"], ["B2"])
                dve(lambda e: e.tensor_copy(out=dst, in_=B[2][:, 0:128]), ["B2"], [dkey])
            beta_bc = af(128); g_bc = af(128); eg_bc = af(128)
            bcast_hs(beta_bc, beta_s[:, :], "beta_s", "beta_bc")
            bcast_hs(g_bc, g_s[:, :], "g_s", "g_bc")
            act(lambda e: e.activation(out=eg_bc, in_=g_bc, func=AF.Exp), ["g_bc"], ["eg_bc"])
            yield
            HS = 8 * NS
            q2 = qs_s[:].rearrange("p h s -> p (h s)"); k2 = ks_s[:].rearrange("p h s -> p (h s)"); v2 = vs_s[:].rearrange("p h s -> p (h s)")
            sq2 = af(HS); rq2 = af(HS); rk2 = af(HS); tmp2 = af(HS)
            for (src, skey, dst, dkey, scl) in [(q2, "qT_s", rq2, "rq2", 128.0 ** -0.5), (k2, "kT_s", rk2, "rk2", 1.0)]:
                dve(lambda e, src=src: e.tensor_tensor(out=sq2, in0=src, in1=src, op=ALU.mult), [skey], ["sq2"])
                pe(lambda e: e.matmul(out=B[3][:, 0:HS], lhsT=ones_f[:], rhs=sq2, start=True, stop=True), ["sq2", "ones_f"], ["B3"])
                rsqrt_ln(dst, B[3][:, 0:HS], 1.0, EPS, ["B3"], [dkey], tmp2, "tmp2")
                dve(lambda e, src=src, dst=dst, scl=scl: e.scalar_tensor_tensor(out=src, in0=src, scalar=scl, in1=dst, op0=ALU.mult, op1=ALU.mult),
                    [skey, dkey], [skey])
                yield
            wcol = af(HS); ucol = af(HS); qg = af(HS); qk = af(HS); qk_bc = af(HS)
            dve(lambda e: e.tensor_tensor(out=wcol, in0=k2, in1=beta_bc, op=ALU.mult), ["kT_s", "beta_bc"], ["wcol"])
            dve(lambda e: e.tensor_tensor(out=wcol, in0=wcol, in1=eg_bc, op=ALU.mult), ["wcol", "eg_bc"], ["wcol"])
            dve(lambda e: e.tensor_tensor(out=ucol, in0=v2, in1=beta_bc, op=ALU.mult), ["vT_s", "beta_bc"], ["ucol"])
            dve(lambda e: e.tensor_tensor(out=qg, in0=q2, in1=eg_bc, op=ALU.mult), ["qT_s", "eg_bc"], ["qg"])
            dve(lambda e: e.tensor_tensor(out=qk, in0=q2, in1=k2, op=ALU.mult), ["qT_s", "kT_s"], ["qk"])
            pe(lambda e: e.matmul(out=B[4][:, 0:HS], lhsT=ones_f[:], rhs=qk, start=True, stop=True), ["qk", "ones_f"], ["B4"])
            dve(lambda e: e.tensor_copy(out=qk_bc, in_=B[4][:, 0:HS]), ["B4"], ["qk_bc"])
            yield
            os_all = af(HS)
            w3 = wcol.rearrange("p (h s) -> p h s", h=8); u3 = ucol.rearrange("p (h s) -> p h s", h=8)
            qg3 = qg.rearrange("p (h s) -> p h s", h=8); qk3 = qk_bc.rearrange("p (h s) -> p h s", h=8)
            eg3 = eg_bc.rearrange("p (h s) -> p h s", h=8); o3 = os_all.rearrange("p (h s) -> p h s", h=8)
            Ss = [af(1024), af(1024)]
            vn_all = af(HS); tmpS = af(1024)
            vn3 = vn_all.rearrange("p (h s) -> p h s", h=8)
            rowk = af(1024); rowv = af(1024); rmask = [af(1024), af(1024)]
            for s in range(NS):
                Si, sk = Ss[s % 2], f"Ss{s % 2}"
                S3 = Si.rearrange("p (h e) -> p h e", h=8)
                bank, bkey = (B7, "B7") if s % 2 == 0 else (B[4], "B4")
                dma(lambda e: e.dma_start(out=S3, in_=sd[s].rearrange("h d e -> d h e")), [], [sk])
                for hh in range(8):
                    pe(lambda e, hh=hh: e.matmul(out=bank[:, hh:hh + 1], lhsT=S3[:, hh, :], rhs=w3[:, hh, s:s + 1], start=True, stop=True),
                       [sk, "wcol"], [bkey])
                    pe(lambda e, hh=hh: e.matmul(out=bank[:, 8 + hh:9 + hh], lhsT=S3[:, hh, :], rhs=qg3[:, hh, s:s + 1], start=True, stop=True),
                       [sk, "qg"], [bkey])
                dve(lambda e: e.tensor_tensor(out=vn3[:, :, s], in0=u3[:, :, s], in1=bank[:, 0:8], op=ALU.subtract), ["ucol", bkey], ["vn_all"])
                dve(lambda e: e.tensor_tensor(out=o3[:, :, s], in0=qk3[:, :, s], in1=vn3[:, :, s], op=ALU.mult), ["qk_bc", "vn_all"], ["os_all"])
                dve(lambda e: e.tensor_tensor(out=o3[:, :, s], in0=o3[:, :, s], in1=bank[:, 8:16], op=ALU.add), ["os_all", bkey], ["os_all"])
                yield
            for hh in range(8):
                bkr, bkrk = (B[3], "B3") if hh < 4 else (B[4], "B4")
                bv, bvk = (B[1], "B1") if hh < 4 else (B[2], "B2")
                c0 = (hh % 4) * 128
                pe(lambda e, hh=hh, c0=c0, bkr=bkr: e.transpose(out=bkr[0:NS, c0:c0 + 128], in_=ks_s[:, hh, :], identity=ident[:]), ["kT_s", "ident"], [bkrk])
                pe(lambda e, hh=hh, c0=c0, bv=bv: e.transpose(out=bv[0:NS, c0:c0 + 128], in_=vn3[:, hh, :], identity=ident[:]), ["vn_all", "ident"], [bvk])
            yield
            dve(lambda e: e.tensor_copy(out=rowk[0:NS, 0:512], in_=B[3][0:NS, :]), ["B3"], ["rowk"])
            dve(lambda e: e.tensor_copy(out=rowk[0:NS, 512:1024], in_=B[4][0:NS, :]), ["B4"], ["rowk"])
            act(lambda e: e.activation(out=rowv[0:NS, 0:512], in_=B[1][0:NS, :], func=AF.Copy), ["B1"], ["rowv"])
            act(lambda e: e.activation(out=rowv[0:NS, 512:1024], in_=B[2][0:NS, :], func=AF.Copy), ["B2"], ["rowv"])
            yield
            for s in range(NS):
                Si, sk = Ss[s % 2], f"Ss{s % 2}"
                S3 = Si.rearrange("p (h e) -> p h e", h=8)
                rm, rmk = rmask[s % 2], f"rmask{s % 2}"
                dma(lambda e: e.dma_start(out=S3, in_=sd[s].rearrange("h d e -> d h e")), [], [sk])
                dve(lambda e: e.tensor_scalar(out=rm[0:NS, :], in0=rowv[0:NS, :], scalar1=ident[0:NS, s:s + 1], scalar2=None, op0=ALU.mult),
                    ["rowv", "ident"], [rmk])
                yield
                for hh in range(8):
                    bo, bok = (B[1], "B1") if hh < 4 else (B[2], "B2")
                    c0 = (hh % 4) * 128
                    pe(lambda e, hh=hh, c0=c0, bo=bo: e.matmul(out=bo[:, c0:c0 + 128], lhsT=rowk[0:NS, hh * 128:(hh + 1) * 128],
                                                               rhs=rm[0:NS, hh * 128:(hh + 1) * 128], start=True, stop=True), ["rowk", rmk], [bok])
                T3 = tmpS.rearrange("p (h e) -> p h e", h=8)
                dve(lambda e: e.tensor_tensor(out=T3, in0=S3, in1=eg3[:, :, s:s + 1].to_broadcast([128, 8, 128]), op=ALU.mult), [sk, "eg_bc"], ["tmpS"])
                yield
                dve(lambda e: e.tensor_tensor(out=tmpS[:, 0:512], in0=tmpS[:, 0:512], in1=B[1][:, :], op=ALU.add), ["tmpS", "B1"], ["tmpS"])
                dve(lambda e: e.tensor_tensor(out=tmpS[:, 512:1024], in0=tmpS[:, 512:1024], in1=B[2][:, :], op=ALU.add), ["tmpS", "B2"], ["tmpS"])
                dma(lambda e: e.dma_start(out=S_s_d[s].rearrange("h d e -> d h e"), in_=T3), ["tmpS"], ["S_s_dram"])
                yield
            so2 = af(HS); rso = af(HS)
            dve(lambda e: e.tensor_tensor(out=so2, in0=os_all, in1=os_all, op=ALU.mult), ["os_all"], ["so2"])
            pe(lambda e: e.matmul(out=B[2][:, 0:HS], lhsT=ones_f[:], rhs=so2, start=True, stop=True), ["so2", "ones_f"], ["B2"])
            rsqrt_ln(rso, B[2][:, 0:HS], 1.0 / 128, EPS, ["B2"], ["rso"], tmp2, "tmp2")
            dve(lambda e: e.scalar_tensor_tensor(out=os_all, in0=os_all, scalar=vecs[:, V_ON:V_ON + 1], in1=rso, op0=ALU.mult, op1=ALU.mult),
                ["os_all", "vecs", "rso"], ["os_all"])
            dve(lambda e: e.tensor_tensor(out=XBs[:], in0=XBs[:], in1=o3, op=ALU.mult), ["XBs", "os_all"], ["XBs"])
            dma(lambda e: e.dma_start(out=xbs_d.rearrange("h p t -> p h t")[:, :, NP_:TT], in_=XBs[:]), ["XBs"], ["xbs_dram"])
            yield

        pre_w = {}

        def branchA_chain():
            ub = af(2 + TT); acca = af(TT); szf = ab(TT); tz = af(512)
            dve(lambda e: e.memset(ub[:, 0:2], 0.0), [], ["ub"])
            banks = [(B[0], "B0"), (B5f, "B5"), (B6f, "B6")]
            cnt = [0]

            def proj(wt, wkey, j, evac):
                for nb, (t0, n) in enumerate(BLK):
                    bank, bkey = banks[cnt[0] % 3]
                    cnt[0] += 1
                    for k in range(8):
                        pe(lambda e, k=k, bank=bank, t0=t0, n=n: e.matmul(out=bank[:, 0:n], lhsT=wt[:, k, j * 128:(j + 1) * 128], rhs=hT[:, k, t0:t0 + n],
                                                                         start=(k == 0), stop=(k == 7)), [wkey, "hT"], [bkey])
                    evac(nb, bank[:, 0:n], t0, n, bkey)
                    yield

            for f in range(8):
                wt, wkey = get_w("brA", [(w_in, [OFF_HA + ff * 128, OFF_CA + ff * 128, OFF_BA + ff * 128, OFF_ZA + ff * 128], 128) for ff in range(8)], f)
                if f == 7:
                    pre_w["wA0"] = load_w(w_out_a, [0], 512)

                def ev_ha(nb, ps, t0, n, bkey):
                    act(lambda e: e.activation(out=ub[:, 2 + t0:2 + t0 + n], in_=ps, func=AF.Copy), [bkey], ["ub"])
                yield from proj(wt, wkey, 0, ev_ha)

                def ev_ca(nb, ps, t0, n, bkey):
                    dve(lambda e: e.tensor_tensor(out=ub[:, 2 + t0:2 + t0 + n], in0=ps, in1=ub[:, 2 + t0:2 + t0 + n], op=ALU.mult), [bkey, "ub"], ["ub"])
                yield from proj(wt, wkey, 1, ev_ca)
                pool(lambda e, f=f: e.tensor_copy(out=tail_a_p[:, f, :], in_=ub[:, NP_:NP_ + 2]), ["ub"], ["tail_a_p"])
                pool(lambda e, f=f: e.tensor_copy(out=tail_a_s[:, f, 0, :], in_=cbufa[:, f, 1, :]), ["cbufa"], ["tail_a_s"])
                pool(lambda e, f=f: e.tensor_copy(out=tail_a_s[:, f, 1, :], in_=ub[:, 2 + NP_:2 + TT]), ["ub"], ["tail_a_s"])
                wc = [vecs[:, V_CA + t * 8 + f:V_CA + t * 8 + f + 1] for t in range(3)]
                HB = NP_ // 2
                for hf in range(2):
                    c0 = hf * HB
                    pool(lambda e, wc=wc, c0=c0: e.tensor_scalar(out=acca[:, c0:c0 + HB], in0=ub[:, c0:c0 + HB], scalar1=wc[0], scalar2=0.0,
                                                                 op0=ALU.mult, op1=ALU.add), ["ub", "vecs"], ["acca"])
                    yield
                for t in range(1, 3):
                    for hf in range(2):
                        c0 = hf * HB
                        dve(lambda e, t=t, wc=wc, c0=c0: e.scalar_tensor_tensor(out=acca[:, c0:c0 + HB], in0=ub[:, c0 + t:c0 + t + HB], scalar=wc[t],
                                                                                in1=acca[:, c0:c0 + HB], op0=ALU.mult, op1=ALU.add),
                            ["ub", "acca", "vecs"], ["acca"])
                        yield
                pool(lambda e, wc=wc: e.tensor_scalar(out=acca[:, NP_:TT], in0=ub[:, 2 + NP_:2 + TT], scalar1=wc[2], scalar2=0.0, op0=ALU.mult, op1=ALU.add),
                     ["ub", "vecs"], ["acca"])
                for t in range(2):
                    dve(lambda e, t=t, wc=wc, f=f: e.scalar_tensor_tensor(out=acca[:, NP_:TT], in0=cbufa[:, f, t, :], scalar=wc[t], in1=acca[:, NP_:TT],
                                                                          op0=ALU.mult, op1=ALU.add), ["cbufa", "acca", "vecs"], ["acca"])

                def ev_za(nb, ps, t0, n, bkey, f=f):
                    act(lambda e: e.activation(out=szf[:, t0:t0 + n], in_=ps, func=AF.Silu), [bkey], ["szf"])
                yield from proj(wt, wkey, 3, ev_za)

                def ev_ba(nb, ps, t0, n, bkey, f=f):
                    dve(lambda e: e.tensor_tensor(out=tz[:, 0:n], in0=ps, in1=acca[:, t0:t0 + n], op=ALU.mult), [bkey, "acca"], ["tz"])
                    dve(lambda e: e.tensor_tensor(out=ZA[:, f, t0:t0 + n], in0=tz[:, 0:n], in1=szf[:, t0:t0 + n], op=ALU.mult), ["tz", "szf"], [f"ZA{nb}"])
                yield from proj(wt, wkey, 2, ev_ba)
            pre_w["wB0"] = load_w(w_out_b, [0], 512)
            yield

        run_chains([sample_chain(), branchA_chain()])

        ckpt(6)
        P.fence()
        arena.off = za_end
        MG = ab(8 * TT).rearrange("p (k t) -> p k t", k=8)
        mg_end = arena.off
        XB = ab(8 * TT).rearrange("p (k t) -> p k t", k=8)
        for nb_, (t0_, n_) in enumerate(BLK):
            dma(lambda e, t0_=t0_, n_=n_: e.dma_start(out=XB[:, :, t0_:t0_ + n_], in_=xbs_d.rearrange("h p t -> p h t")[:, :, t0_:t0_ + n_]),
                ["xbs_dram"], [f"XB{nb_}"])
        sA = ab(512); sB = ab(512); tA = af(512); tB = af(512)
        gwv = gcT_full[:, 0:2048].bitcast(BF16)
        gws = [gwv[:, 0:2048].rearrange("p (k n) -> p k n", k=8), gwv[:, 2048:4096].rearrange("p (k n) -> p k n", k=8)]

        def load_gates(fo):
            gwt = gws[fo % 2]
            for gi_, goff in enumerate([OFF_GA, OFF_GB]):
                src = w_in[:, goff + fo * 128:goff + (fo + 1) * 128].rearrange("(k p) n -> p k n", p=128)
                dma(lambda e, gi_=gi_, src=src, gwt=gwt: e.dma_start(out=gwt[:, :, gi_ * 128:(gi_ + 1) * 128], in_=src), [], [f"gw{fo % 2}"], eng="pool")
        load_gates(0)

        def mm8(bank, bkey, M, lhs_fn, rhs_fn, rkeys, n):
            for k in range(8):
                pe(lambda e, k=k: e.matmul(out=bank[0:M, 0:n], lhsT=lhs_fn(k), rhs=rhs_fn(k), start=(k == 0), stop=(k == 7)), rkeys, [bkey])

        for half in range(2):
            wA, wAk = pre_w["wA0"] if half == 0 else load_w(w_out_a, [half * 512], 512)
            wB, wBk = pre_w["wB0"] if half == 0 else load_w(w_out_b, [half * 512], 512)
            for jj in range(4):
                fo = half * 4 + jj
                if fo + 1 < 8:
                    load_gates(fo + 1)
                gw, gwk = gws[fo % 2], f"gw{fo % 2}"
                for nb, (t0, n) in enumerate(BLK):
                    mm8(B[0], "B0", 128, lambda k, gw=gw: gw[:, k, 0:128], lambda k, t0=t0, n=n: hT[:, k, t0:t0 + n], [gwk, f"hT{nb}"], n)
                    act(lambda e, n=n: e.activation(out=sA[:, 0:n], in_=B[0][:, 0:n], func=AF.Sigmoid), ["B0"], ["sA"])
                    mm8(B[1], "B1", 128, lambda k, jj=jj, wA=wA: wA[:, k, jj * 128:(jj + 1) * 128], lambda k, t0=t0, n=n: ZA[:, k, t0:t0 + n],
                        [wAk, f"ZA{nb}"], n)
                    dve(lambda e, n=n: e.tensor_tensor(out=tA[:, 0:n], in0=B[1][:, 0:n], in1=sA[:, 0:n], op=ALU.mult), ["B1", "sA"], ["tA"])
                    mm8(B[2], "B2", 128, lambda k, gw=gw: gw[:, k, 128:256], lambda k, t0=t0, n=n: hT[:, k, t0:t0 + n], [gwk, f"hT{nb}"], n)
                    act(lambda e, n=n: e.activation(out=sB[:, 0:n], in_=B[2][:, 0:n], func=AF.Sigmoid), ["B2"], ["sB"])
                    mm8(B[3], "B3", 128, lambda k, jj=jj, wB=wB: wB[:, k, jj * 128:(jj + 1) * 128], lambda k, t0=t0, n=n: XB[:, k, t0:t0 + n],
                        [wBk, f"XB{nb}"], n)
                    dve(lambda e, n=n: e.tensor_tensor(out=tB[:, 0:n], in0=B[3][:, 0:n], in1=sB[:, 0:n], op=ALU.mult), ["B3", "sB"], ["tB"])
                    pool(lambda e, fo=fo, t0=t0, n=n: e.tensor_tensor(out=MG[:, fo, t0:t0 + n], in0=tA[:, 0:n], in1=tB[:, 0:n], op=ALU.add),
                         ["tA", "tB"], [f"MG{nb}"])

        ckpt(7)
        P.fence()
        arena.off = 0
        wo_b = ab(8 * 1024).rearrange("p (k n) -> p k n", k=8)
        for n2 in range(2):
            src = w_o[:, n2 * 512:(n2 + 1) * 512].rearrange("(k p) n -> p k n", p=128)
            dma(lambda e, n2=n2, src=src: e.dma_start(out=wo_b[:, :, n2 * 512:(n2 + 1) * 512], in_=src), [], [f"wo_b{n2}"], eng="pool")
        xt = [af(D), af(D)]
        yt0 = af(D)
        assert arena.off <= za_end
        arena.off = mg_end
        yt = [yt0, af(D)]
        junk = ab(512)
        ss2 = af(2); rs5 = af(2); tmp5 = af(2)
        for i in range(17):
            rows = 128 if i < 16 else NS
            t0 = i * 128
            mk = f"MG{i // 4}"
            xi, xk = xt[i % 2], f"xt{i % 2}"
            yi, yk = yt[i % 2], f"yt{i % 2}"
            src = xp[i * 128:(i + 1) * 128, :] if i < 16 else xsm
            dst = y_p[i * 128:(i + 1) * 128, :] if i < 16 else y_s
            gsrc = gn if i < 16 else gns
            dma(lambda e, xi=xi, rows=rows, src=src: e.dma_start(out=xi[0:rows, :], in_=src), [], [xk])
            bo = 2 * (i % 2)
            for n2 in range(2):
                mm8(B[bo + n2], f"B{bo + n2}", rows, lambda k, t0=t0, rows=rows: MG[:, k, t0:t0 + rows], lambda k, n2=n2: wo_b[:, k, n2 * 512:(n2 + 1) * 512],
                    [mk, f"wo_b{n2}"], 512)
                act(lambda e, n2=n2, rows=rows, bo=bo: e.activation(out=junk[0:rows, :], in_=B[bo + n2][0:rows, :], func=AF.Square,
                                                                  accum_out=ss2[0:rows, n2:n2 + 1]), [f"B{bo + n2}"], ["junk", "ss2"])
            dve(lambda e, rows=rows: e.tensor_tensor(out=ss2[0:rows, 0:1], in0=ss2[0:rows, 0:1], in1=ss2[0:rows, 1:2], op=ALU.add), ["ss2"], ["ss2"])
            rsqrt_small(rs5[0:rows, 0:1], ss2[0:rows, 0:1], 1.0 / D, EPS, ["ss2"], ["rs5"], tmp5[0:rows, 0:1], "tmp5")
            for n2 in range(2):
                cs_ = slice(n2 * 512, (n2 + 1) * 512)
                dve(lambda e, n2=n2, rows=rows, cs_=cs_, yi=yi, gsrc=gsrc, bo=bo: e.scalar_tensor_tensor(
                    out=yi[0:rows, cs_], in0=B[bo + n2][0:rows, :], scalar=rs5[0:rows, 0:1], in1=gsrc[0:rows, cs_], op0=ALU.mult, op1=ALU.mult),
                    [f"B{bo + n2}", "rs5", "gn", "gns"], [yk])
            dve(lambda e, rows=rows, yi=yi, xi=xi: e.tensor_tensor(out=yi[0:rows, :], in0=yi[0:rows, :], in1=xi[0:rows, :], op=ALU.add), [yk, xk], [yk])
            dma(lambda e, yi=yi, rows=rows, dst=dst: e.dma_start(out=dst, in_=yi[0:rows, :]), [yk], ["y_dram"])
        dma(lambda e: e.dma_start(out=tail_a_p_d, in_=tail_a_p[:].rearrange("p a b -> p (a b)")), ["tail_a_p"], ["tap_dram"])
        dma(lambda e: e.dma_start(out=tail_a_s_d, in_=tail_a_s[:].rearrange("p a b c -> p (a b c)")), ["tail_a_s"], ["tas_dram"])
        dma(lambda e: e.dma_start(out=tail_q_p_d, in_=tail_q_p[:].rearrange("p a b -> p (a b)")), ["tail_q_p"], ["tqp_dram"])
        dma(lambda e: e.dma_start(out=tail_q_s_d, in_=tail_q_s[:].rearrange("p a b c -> p (a b c)")), ["tail_q_s"], ["tqs_dram"])
        return

    with ExitStack() as st:
        P = Prog(nc, st)
        try:
            _body(st, P)
        except _Stop:
            pass
        P.fence()
        P.emit()
    return nc


_NC_CACHE = {}


def _consts():
    p = np.arange(128)[:, None]
    c = np.arange(128)[None, :]
    ident = np.eye(128, dtype=np.float32)
    biasA = np.where(c >= p, 0.0, -BIG).astype(np.float32)
    biasB = np.where(p > c, 0.0, BIG).astype(np.float32)
    negmA = np.tile(np.where(c > p, -1.0, 0.0).astype(np.float32), (1, 4))
    sel = np.zeros((8, 8, 128), np.float32)
    for h in range(8):
        sel[h, h, :] = 1.0
    bm = np.zeros((8, 16, 8), np.float32)
    for h in range(8):
        bm[h, :, h] = 1.0
    mks = np.zeros((128, 7, 128), np.float32)
    mks[:, 0, :] = (p // 2 == c // 2) & (p != c)
    for l in range(1, 7):
        b = 2 ** l
        mks[:, l, :] = (p // (2 * b) == c // (2 * b)) & (p // b != c // b)
    return dict(ident=ident, biasA=biasA, biasB=biasB, negmA=negmA, sel=sel.reshape(8, 1024), bm=bm.reshape(8, 128),
                mks=mks.reshape(128, 7 * 128))


def kernel(x_prompt, x_sample, c_prompt, c_sample, state_conv_a, state_conv_qkv, state_delta,
           ada_w, ada_b, norm_pre, w_in, conv_a_w, conv_b_w, a_log, dt_bias, onorm_w,
           w_out_a, w_out_b, w_o, norm_post):
    f = lambda a: np.ascontiguousarray(np.asarray(a, dtype=np.float32))
    x_prompt, x_sample, c_prompt, c_sample = f(x_prompt), f(x_sample), f(c_prompt), f(c_sample)
    state_conv_a, state_conv_qkv, state_delta = f(state_conv_a), f(state_conv_qkv), f(state_delta)
    if "nc" not in _NC_CACHE:
        _NC_CACHE["nc"] = build_nc(_NC_CACHE.get("stop", 99))
    nc = _NC_CACHE["nc"]
    n = 8
    vecs = np.zeros((128, 160), np.float32)
    vecs[:, 0:24] = f(ada_b)[0].reshape(24, 128).T
    vecs[:, 24:32] = f(norm_pre)[0].reshape(8, 128).T
    vecs[:, 32:56] = f(conv_a_w)[0].reshape(3, 8, 128).transpose(2, 0, 1).reshape(128, 24)
    vecs[:, 56] = f(onorm_w)[0]
    vecs[:, 64:160] = f(conv_b_w)[0].reshape(4, 24, 128).transpose(2, 0, 1).reshape(128, 96)
    hv = np.stack([f(a_log)[0], f(dt_bias)[0]], axis=1)
    shared = dict(ada_w=f(ada_w)[0], adab_bc=np.ascontiguousarray(np.broadcast_to(f(ada_b)[0, 2048:3072], (128, 1024))),
                  npost_bc=np.ascontiguousarray(np.broadcast_to(f(norm_post)[0], (128, 1024))),
                  vecs=vecs, hv=np.ascontiguousarray(hv), w_in=f(w_in)[0], w_out_a=f(w_out_a)[0], w_out_b=f(w_out_b)[0], w_o=f(w_o)[0])
    shared.update(_consts())
    in_maps = []
    for b in range(n):
        s0, s1 = b * NS, (b + 1) * NS
        ca = state_conv_a[0, s0:s1].reshape(NS, 2, 8, 128).transpose(3, 2, 1, 0)
        cq = state_conv_qkv[0, s0:s1].reshape(NS, 3, 24, 128).transpose(3, 2, 1, 0)
        m = dict(shared)
        m.update(xp=x_prompt[b], xsm=x_sample[s0:s1, 0, :], cp_bc=np.ascontiguousarray(np.broadcast_to(c_prompt[b], (128, 1024))),
                 cs=c_sample[s0:s1], cbufa=np.ascontiguousarray(ca).reshape(128, -1), cbufq=np.ascontiguousarray(cq).reshape(128, -1),
                 sd=state_delta[0, s0:s1])
        in_maps.append({k: np.ascontiguousarray(v) for k, v in m.items()})
    res = run_bass_kernel_spmd(nc, in_maps, core_ids=list(range(n)))
    R = res.results
    y_p = np.stack([R[b]["y_p"] for b in range(n)])
    y_s = np.concatenate([R[b]["y_s"] for b in range(n)])[:, None, :]
    nca_p = np.stack([R[b]["tail_a_p"].reshape(128, 8, 2).transpose(2, 1, 0).reshape(2, 1024) for b in range(n)])[None]
    ncq_p = np.stack([R[b]["tail_q_p"].reshape(128, 24, 3).transpose(2, 1, 0).reshape(3, 3072) for b in range(n)])[None]
    nd_p = np.stack([R[b]["S_p"] for b in range(n)])[None]
    nca_s = np.concatenate([R[b]["tail_a_s"].reshape(128, 8, 2, NS).transpose(3, 2, 1, 0).reshape(NS, 2, 1024) for b in range(n)])[None]
    ncq_s = np.concatenate([R[b]["tail_q_s"].reshape(128, 24, 3, NS).transpose(3, 2, 1, 0).reshape(NS, 3, 3072) for b in range(n)])[None]
    nd_s = np.concatenate([R[b]["S_s"] for b in range(n)])[None]
    out = (y_p, y_s, nca_p, ncq_p, nd_p, nca_s, ncq_s, nd_s)
    return tuple(np.ascontiguousarray(o, dtype=np.float32) for o in out)
```

```python
import numpy as np
from contextlib import ExitStack
import concourse.bass as bass
import concourse.mybir as mybir
from concourse.bass_utils import run_bass_kernel_spmd

F32 = mybir.dt.float32
BF16 = mybir.dt.bfloat16
ALU = mybir.AluOpType
AF = mybir.ActivationFunctionType
AX = mybir.AxisListType

NP_ = 2048
NS = 16
TT = NP_ + NS
D = 1024
INC = 10256
EPS = 1e-6
BIG = 60000.0
BLK = [(0, 512), (512, 512), (1024, 512), (1536, 512), (2048, 16)]
OFF_HA, OFF_CA, OFF_BA, OFF_ZA = 0, 1024, 2048, 3072
OFF_Q, OFF_K, OFF_V, OFF_ZB = 4096, 5120, 6144, 7168
OFF_BETA, OFF_ALPHA, OFF_GA, OFF_GB = 8192, 8200, 8208, 9232


class _Rec:
    def __init__(self):
        self.call = None

    def __getattr__(self, name):
        def f(*a, **kw):
            assert self.call is None
            self.call = (name, a, kw)
            return self
        return f


class Prog:
    ENGS = ("pe", "act", "dve", "pool", "sp")
    SEM_LIMIT = 30000

    def __init__(self, nc, stack):
        self.nc, self.stack = nc, stack
        self.ops = {e: [] for e in self.ENGS}
        self.eng_sem, self.eng_cnt = {}, {}
        self.waited = {e: {} for e in self.ENGS}
        self.writers, self.readers = {}, {}
        self.dma_sem, self.dma_cnt = {}, {}
        self.all_sems = {}
        self.nsem = 0
        for e in self.ENGS:
            self._new_eng_sem(e)

    def _sem(self, name):
        self.nsem += 1
        return self.stack.enter_context(self.nc.semaphore(name))

    def _new_eng_sem(self, e):
        self.eng_sem[e] = self._sem(f"s_{e}_{self.nsem}")
        self.eng_cnt[e] = 0

    @staticmethod
    def _bank(key):
        if isinstance(key, str) and len(key) >= 2 and key[0] == "B" and key[1].isdigit():
            return key[:2]
        return None

    def _need(self, eng, tok, waits, raw):
        sem, val, teng = tok
        if teng == eng and (not raw or eng == "pe"):
            return
        w = self.waited[eng]
        if w.get(id(sem), 0) < val:
            w[id(sem)] = val
            waits[id(sem)] = (sem, val)

    def op(self, eng, fn, reads=(), writes=(), dma=False):
        waits = {}
        reads = [self._bank(b) or b for b in reads]
        writes = [self._bank(b) or b for b in writes]
        for b in reads:
            for tok in self.writers.get(b, {}).values():
                self._need(eng, tok, waits, True)
            if self._bank(b):
                for tok in self.readers.get(b, {}).values():
                    self._need(eng, tok, waits, False)
        for b in writes:
            for tok in self.writers.get(b, {}).values():
                self._need(eng, tok, waits, True)
            for tok in self.readers.get(b, {}).values():
                self._need(eng, tok, waits, False)
        if dma:
            key = writes[0] if writes else ("rd", reads[0])
            if key not in self.dma_sem:
                self.dma_sem[key] = self._sem(f"d{self.nsem}")
                self.dma_cnt[key] = 0
            self.dma_cnt[key] += 16
            tok = (self.dma_sem[key], self.dma_cnt[key], "dma")
            inc = (tok[0], 16)
        else:
            if self.eng_cnt[eng] >= self.SEM_LIMIT:
                self._new_eng_sem(eng)
            self.eng_cnt[eng] += 1
            tok = (self.eng_sem[eng], self.eng_cnt[eng], eng)
            inc = (tok[0], 1)
        self.all_sems[id(tok[0])] = tok
        rec = _Rec()
        fn(rec)
        assert rec.call is not None
        self.ops[eng].append((list(waits.values()), rec.call, inc))
        for b in reads:
            self.readers.setdefault(b, {})[id(tok[0])] = tok
        for b in writes:
            self.writers.setdefault(b, {})[id(tok[0])] = tok
        return tok

    def fence(self, engs=None):
        for e in (engs or self.ENGS):
            waits = {}
            for tok in self.all_sems.values():
                self._need(e, tok, waits, True)
            if waits:
                self.ops[e].append((list(waits.values()), None, None))

    def emit(self):
        with self.nc.Block() as block:
            def mk(ename):
                def body(eng):
                    for waits, fn, inc in self.ops[ename]:
                        for sem, val in waits:
                            eng.wait_ge(sem, val)
                        if fn is not None:
                            name, a, kw = fn
                            getattr(eng, name)(*a, **kw).then_inc(inc[0], inc[1])
                return body
            block.tensor(mk("pe"))
            block.scalar(mk("act"))
            block.vector(mk("dve"))
            block.gpsimd(mk("pool"))
            block.sync(mk("sp"))


class Arena:
    def __init__(self, tile, ncols):
        self.tile, self.ncols, self.off = tile, ncols, 0

    def reset(self):
        self.off = 0

    def alloc(self, cols):
        cols = (cols + 1) // 2 * 2
        assert self.off + cols <= self.ncols, ("arena overflow", self.off, cols, self.ncols)
        ap = self.tile[:, self.off:self.off + cols]
        self.off += cols
        return ap


class _Stop(Exception):
    pass


def build_nc(stop=99):
    nc = bass.Bass("TRN2", target_bir_lowering=False)

    def ckpt(k):
        if stop <= k:
            raise _Stop()

    def din(name, shape, dt=F32):
        return nc.dram_tensor(name, list(shape), dt, kind="ExternalInput").ap()

    def dout(name, shape, dt=F32):
        return nc.dram_tensor(name, list(shape), dt, kind="ExternalOutput").ap()

    xp = din("xp", [NP_, D]); xsm = din("xsm", [NS, D])
    cp_bc = din("cp_bc", [128, D]); cs = din("cs", [NS, D])
    cbufa_d = din("cbufa", [128, 8 * 2 * NS]); cbufq_d = din("cbufq", [128, 24 * 3 * NS])
    sd = din("sd", [NS, 8, 128, 128])
    ada_w = din("ada_w", [D, 3 * D]); adab_bc_d = din("adab_bc", [128, D]); npost_bc_d = din("npost_bc", [128, D])
    vecs_d = din("vecs", [128, 160]); hv_d = din("hv", [8, 2])
    w_in = din("w_in", [D, INC]); w_out_a = din("w_out_a", [D, D]); w_out_b = din("w_out_b", [D, D]); w_o = din("w_o", [D, D])
    ident_d = din("ident", [128, 128]); biasA_d = din("biasA", [128, 128]); biasB_d = din("biasB", [128, 128])
    negmA_d = din("negmA", [128, 512]); sel_d = din("sel", [8, 8 * 128]); bm_d = din("bm", [8, 128])
    mks_d = din("mks", [128, 7 * 128])

    y_p = dout("y_p", [NP_, D]); y_s = dout("y_s", [NS, D])
    tail_a_p_d = dout("tail_a_p", [128, 8 * 2]); tail_a_s_d = dout("tail_a_s", [128, 8 * 2 * NS])
    tail_q_p_d = dout("tail_q_p", [128, 24 * 3]); tail_q_s_d = dout("tail_q_s", [128, 24 * 3 * NS])
    S_p_d = dout("S_p", [8, 128, 128]); S_s_d = dout("S_s", [NS, 8, 128, 128])

    def _body(st, P):

        def sb(name, shape, dt=F32):
            return st.enter_context(nc.sbuf_tensor("sb_" + name, list(shape), dt))

        def psb(name, shape, dt=F32):
            return st.enter_context(nc.psum_tensor("ps_" + name, list(shape), dt))

        hT = sb("hT", [128, 8, TT], BF16)
        XBs = sb("XBs", [128, 8, NS], BF16)
        xbs_d = nc.dram_tensor("xbs_scratch", [8, 128, TT], BF16).ap()
        wts = [sb(f"wt{i}", [128, 8, 512], BF16) for i in range(2)]
        ident = sb("ident", [128, 128]); identb = sb("identb", [128, 128], BF16)
        ones_f = sb("ones_f", [128, 128]); ones_b = sb("ones_b", [128, 128], BF16)
        biasA = sb("biasA", [128, 128]); biasB = sb("biasB", [128, 128])
        negmA = sb("negmA", [128, 512]); mkb = sb("mkb", [128, 7, 512], BF16); identb2 = sb("identb2", [128, 512], BF16)
        sel = sb("sel", [8, 8 * 128]); bm = sb("bm", [8, 128])
        vecs = sb("vecs", [128, 160]); hv = sb("hv", [8, 2])
        modfm = sb("modfm", [128, 16, 17])
        gn = sb("gn", [128, D]); gns = sb("gns", [NS, D])
        a_p = sb("a_p", [128, 8]); A_s = sb("A_s", [128, 8, NS])
        cbufa = sb("cbufa", [128, 8, 2, NS]); cbufq = sb("cbufq", [128, 24, 3, NS])
        tail_a_p = sb("tail_a_p", [128, 8, 2]); tail_a_s = sb("tail_a_s", [128, 8, 2, NS])
        tail_q_p = sb("tail_q_p", [128, 24, 3]); tail_q_s = sb("tail_q_s", [128, 24, 3, NS])
        gc_tok = sb("gc_tok", [128, 128]); sb_tok = sb("sb_tok", [128, 128])
        egc = sb("egc", [128, 128]); ekd = sb("ekd", [128, 128]); egl = sb("egl", [128, 128])
        gcT_full = sb("gcT", [128, NP_]); gcT = gcT_full[0:8, :]; beta_s = sb("beta_s", [8, NS]); g_s = sb("g_s", [8, NS])
        qs_s = sb("qs_s", [128, 8, NS]); ks_s = sb("ks_s", [128, 8, NS]); vs_s = sb("vs_s", [128, 8, NS])
        ARENA_F = 27800
        arena_t = sb("arena", [128, ARENA_F])
        arena = Arena(arena_t, ARENA_F)

        def af(cols):
            return arena.alloc(cols)

        def ab(cols):
            return arena.alloc((cols + 1) // 2).bitcast(BF16)

        B = [psb(f"B{i}", [128, 512]) for i in range(5)]
        B5 = psb("B5", [128, 1024], BF16); B6 = psb("B6", [128, 1024], BF16)
        B7 = psb("B7", [128, 512])

        def dve(fn, r, w): return P.op("dve", fn, reads=r, writes=w)
        def act(fn, r, w): return P.op("act", fn, reads=r, writes=w)
        def pool(fn, r, w): return P.op("pool", fn, reads=r, writes=w)
        def pe(fn, r, w): return P.op("pe", fn, reads=r, writes=w)
        def dma(fn, r, w, eng="sp"): return P.op(eng, fn, reads=r, writes=w, dma=True)

        def rsqrt_small(out_ap, in_ap, scale, eps, rk, wk, tmp_ap, tmpk):
            dve(lambda e: e.tensor_scalar(out=tmp_ap, in0=in_ap, scalar1=scale, scalar2=eps, op0=ALU.mult, op1=ALU.add), rk, [tmpk])
            act(lambda e: e.activation(out=tmp_ap, in_=tmp_ap, func=AF.Sqrt), [tmpk], [tmpk])
            dve(lambda e: e.reciprocal(out=out_ap, in_=tmp_ap), [tmpk], wk)

        wstate = {"n": 0}

        def load_w(dram, col_offs, width):
            i = wstate["n"] % 2
            wstate["n"] += 1
            wt = wts[i]
            key = f"wt{i}"
            runs = []
            for j, c in enumerate(col_offs):
                if runs and runs[-1][1] + runs[-1][2] == c:
                    runs[-1][2] += width
                else:
                    runs.append([j * width, c, width])
            for dst, c, wd in runs:
                src = dram[:, c:c + wd].rearrange("(k p) n -> p k n", p=128)
                dma(lambda e, dst=dst, wd=wd, src=src: e.dma_start(out=wt[:, :, dst:dst + wd], in_=src),
                    [], [key], eng="pool")
            return wt, key

        wq = {}

        def get_w(tag, specs, i):
            for j in (i, i + 1):
                if j < len(specs) and (tag, j) not in wq:
                    wq[(tag, j)] = load_w(*specs[j])
            return wq[(tag, i)]

        pj = {"n": 0}

        def proj_fm(wt, wkey, j, width, evac, blocks=BLK, rhs=None, rkey="hT", M=128, bankset=(0, 1)):
            rhs = hT if rhs is None else rhs
            for nb, (t0, n) in enumerate(blocks):
                bi = bankset[pj["n"] % 2]
                pj["n"] += 1
                bank, bkey = B[bi], f"B{bi}"
                for k in range(8):
                    pe(lambda e, k=k, bank=bank, t0=t0, n=n: e.matmul(
                        out=bank[0:M, 0:n], lhsT=wt[:, k, j * width:j * width + M], rhs=rhs[:, k, t0:t0 + n],
                        start=(k == 0), stop=(k == 7)), [wkey, rkey], [bkey])
                evac(nb, bank[0:M, 0:n], t0, n, bkey)

        for t, d, k in [(ident, ident_d, "ident"), (biasA, biasA_d, "biasA"), (biasB, biasB_d, "biasB"),
                        (negmA, negmA_d, "negmA"), (sel, sel_d, "sel"), (bm, bm_d, "bm"), (vecs, vecs_d, "vecs"),
                        (hv, hv_d, "hv"), (gn, npost_bc_d, "gn")]:
            dma(lambda e, t=t, d=d: e.dma_start(out=t[:], in_=d), [], [k])
        dma(lambda e: e.dma_start(out=cbufa[:].rearrange("p a b c -> p (a b c)"), in_=cbufa_d), [], ["cbufa"])
        dma(lambda e: e.dma_start(out=cbufq[:].rearrange("p a b c -> p (a b c)"), in_=cbufq_d), [], ["cbufq"])
        pool(lambda e: e.memset(ones_f[:], 1.0), [], ["ones_f"])
        pool(lambda e: e.memset(ones_b[:], 1.0), [], ["ones_b"])
        dve(lambda e: e.tensor_copy(out=identb[:], in_=ident[:]), ["ident"], ["identb"])
        for c in range(4):
            dve(lambda e, c=c: e.tensor_copy(out=identb2[:, c * 128:(c + 1) * 128], in_=ident[:]), ["ident"], ["identb2"])
            dma(lambda e, c=c: e.dma_start(out=mkb[:, :, c * 128:(c + 1) * 128], in_=mks_d.rearrange("p (l n) -> p l n", l=7)), [], ["mkb"], eng="pool")
        V_ADAB, V_NPRE, V_CA, V_ON, V_CB = 0, 24, 32, 56, 64

        ckpt(0.1)
        arena.reset()
        cpt = af(D); cst = af(D); gt = af(D)
        dma(lambda e: e.dma_start(out=gt, in_=adab_bc_d), [], ["gt"])
        scTp = ab(8 * 128).rearrange("p (k n) -> p k n", k=8)
        sc17 = ab(8 * 17).rearrange("p (k n) -> p k n", k=8)
        dma(lambda e: e.dma_start(out=cpt, in_=cp_bc), [], ["cpt"])
        dma(lambda e: e.dma_start(out=cst[0:NS, :], in_=cs), [], ["cst"])
        act(lambda e: e.activation(out=cpt, in_=cpt, func=AF.Silu), ["cpt"], ["cpt"])
        act(lambda e: e.activation(out=cst[0:NS, :], in_=cst[0:NS, :], func=AF.Silu), ["cst"], ["cst"])
        for half in range(2):
            bank, bkey = B[2 + half], f"B{2 + half}"
            for kk in range(4):
                k = half * 4 + kk
                pe(lambda e, k=k, kk=kk, bank=bank: e.transpose(out=bank[:, kk * 128:(kk + 1) * 128], in_=cpt[:, k * 128:(k + 1) * 128],
                                                              identity=ident[:]), ["cpt", "ident"], [bkey])
            dve(lambda e, half=half, bank=bank: e.tensor_copy(out=scTp[:, half * 4:half * 4 + 4, :],
                                                              in_=bank[:, :].rearrange("p (k n) -> p k n", k=4)), [bkey], ["scTp"])
        for k in range(8):
            pe(lambda e, k=k: e.transpose(out=B[4][:, k * NS:(k + 1) * NS], in_=cst[0:NS, k * 128:(k + 1) * 128],
                                          identity=ident[0:NS, 0:NS]), ["cst", "ident"], ["B4"])
        dve(lambda e: e.tensor_copy(out=sc17[:, :, 1:17], in_=B[4][:, 0:8 * NS].rearrange("p (k n) -> p k n", k=8)), ["B4"], ["sc17"])
        dve(lambda e: e.tensor_copy(out=sc17[:, :, 0:1], in_=scTp[:, :, 0:1]), ["scTp"], ["sc17"])
        ckpt(0.2)
        for g in range(6):
            wt, wkey = get_w("ada", [(ada_w, [gg * 512], 512) for gg in range(6)], g)
            if g < 4:
                for j in range(4):
                    fc = g * 4 + j
                    for k in range(8):
                        pe(lambda e, k=k, j=j, wt=wt: e.matmul(out=B[2][:, 0:17], lhsT=wt[:, k, j * 128:(j + 1) * 128], rhs=sc17[:, k, :],
                                                               start=(k == 0), stop=(k == 7)), [wkey, "sc17"], ["B2"])
                    act(lambda e, fc=fc: e.activation(out=modfm[:, fc, :], in_=B[2][:, 0:17], func=AF.Identity,
                                                      bias=vecs[:, V_ADAB + fc:V_ADAB + fc + 1], scale=1.0), ["B2", "vecs"], ["modfm"])
            else:
                n0 = (g - 4) * 512
                for k in range(8):
                    pe(lambda e, k=k, wt=wt: e.matmul(out=B[3][0:NS, :], lhsT=sc17[:, k, 1:17], rhs=wt[:, k, :],
                                                      start=(k == 0), stop=(k == 7)), [wkey, "sc17"], ["B3"])
                dve(lambda e, n0=n0: e.tensor_tensor(out=gns[:, n0:n0 + 512], in0=B[3][0:NS, :], in1=gt[0:NS, n0:n0 + 512], op=ALU.add),
                    ["B3", "gt"], ["gns"])
                dve(lambda e, n0=n0: e.tensor_tensor(out=gns[:, n0:n0 + 512], in0=gns[:, n0:n0 + 512], in1=gn[0:NS, n0:n0 + 512], op=ALU.mult),
                    ["gns", "gn"], ["gns"])
                for k in range(8):
                    pe(lambda e, k=k, wt=wt: e.matmul(out=B[4][:, :], lhsT=scTp[:, k, :], rhs=wt[:, k, :],
                                                      start=(k == 0), stop=(k == 7)), [wkey, "scTp"], ["B4"])
                dve(lambda e, n0=n0: e.tensor_tensor(out=gt[:, n0:n0 + 512], in0=B[4][:, :], in1=gt[:, n0:n0 + 512], op=ALU.add),
                    ["B4", "gt", "gns"], ["gt"])
                dve(lambda e, n0=n0: e.tensor_tensor(out=gn[:, n0:n0 + 512], in0=gn[:, n0:n0 + 512], in1=gt[:, n0:n0 + 512], op=ALU.mult),
                    ["gt", "gn", "gns"], ["gn"])
        ckpt(0.3)
        dve(lambda e: e.tensor_scalar(out=a_p[:], in0=modfm[:, 8:16, 0], scalar1=1.0, scalar2=None, op0=ALU.add), ["modfm"], ["a_p"])
        dve(lambda e: e.tensor_tensor(out=a_p[:], in0=a_p[:], in1=vecs[:, V_NPRE:V_NPRE + 8], op=ALU.mult), ["a_p", "vecs"], ["a_p"])
        ckpt(0.4)
        dve(lambda e: e.tensor_scalar(out=A_s[:], in0=modfm[:, 8:16, 1:17], scalar1=1.0, scalar2=None, op0=ALU.add), ["modfm"], ["A_s"])
        ckpt(0.5)
        for k in range(8):
            dve(lambda e, k=k: e.tensor_scalar(out=A_s[:, k, :], in0=A_s[:, k, :], scalar1=vecs[:, V_NPRE + k:V_NPRE + k + 1], scalar2=None,
                                               op0=ALU.mult), ["A_s", "vecs"], ["A_s"])

        ckpt(1)
        P.fence(); arena.reset()
        xt = [af(D), af(D)]
        xh = [af(D), af(D)]
        junk = ab(D)
        ssx = af(18); rstdx = af(18); tmpx = af(18)
        tmps = af(8 * NS)

        def p1_stage1(i):
            rows = 128 if i < 16 else NS
            xi, xk = xt[i % 2], f"xt{i % 2}"
            hi, hk = xh[i % 2], f"xh{i % 2}"
            src = xp[i * 128:(i + 1) * 128, :] if i < 16 else xsm
            dma(lambda e: e.dma_start(out=xi[0:rows, :], in_=src), [], [xk])
            act(lambda e: e.activation(out=junk[0:rows, :], in_=xi[0:rows, :], func=AF.Square, accum_out=ssx[0:rows, i:i + 1]),
                [xk], ["junk", f"ssx{i}"])
            rsqrt_small(rstdx[0:rows, i:i + 1], ssx[0:rows, i:i + 1], 1.0 / D, EPS, [f"ssx{i}"], [f"rstdx{i}"], tmpx[0:rows, i:i + 1], f"tmpx{i}")
            dve(lambda e: e.tensor_scalar(out=hi[0:rows, :], in0=xi[0:rows, :], scalar1=rstdx[0:rows, i:i + 1], scalar2=None, op0=ALU.mult),
                [xk, f"rstdx{i}"], [hk])

        def p1_stage2(i):
            hi, hk = xh[i % 2], f"xh{i % 2}"
            pair = [(B[2], "B2"), (B[3], "B3")] if i % 2 == 0 else [(B[4], "B4"), (B[1], "B1")]
            if i < 16:
                for half in range(2):
                    bank, bkey = pair[half]
                    for kk in range(4):
                        k = half * 4 + kk
                        pe(lambda e, k=k, kk=kk, bank=bank: e.transpose(out=bank[:, kk * 128:(kk + 1) * 128], in_=hi[:, k * 128:(k + 1) * 128],
                                                                     identity=ident[:]), [hk, "ident"], [bkey])
                    for kk in range(4):
                        k = half * 4 + kk
                        if kk % 2 == 0:
                            act(lambda e, k=k, kk=kk, bank=bank: e.activation(
                                out=hT[:, k, i * 128:(i + 1) * 128], in_=bank[:, kk * 128:(kk + 1) * 128], func=AF.Identity,
                                bias=modfm[:, k, 0:1], scale=a_p[:, k:k + 1]), [bkey, "modfm", "a_p"], [f"hT{i // 4}"])
                        else:
                            dve(lambda e, k=k, kk=kk, bank=bank: e.tensor_scalar(
                                out=hT[:, k, i * 128:(i + 1) * 128], in0=bank[:, kk * 128:(kk + 1) * 128], scalar1=a_p[:, k:k + 1],
                                scalar2=modfm[:, k, 0:1], op0=ALU.mult, op1=ALU.add), [bkey, "modfm", "a_p"], [f"hT{i // 4}"])
            else:
                bank, bkey = pair[0]
                for k in range(8):
                    pe(lambda e, k=k: e.transpose(out=bank[:, k * NS:(k + 1) * NS], in_=hi[0:NS, k * 128:(k + 1) * 128],
                                                  identity=ident[0:NS, 0:NS]), [hk, "ident"], [bkey])
                t3 = tmps.rearrange("p (k n) -> p k n", k=8)
                dve(lambda e: e.tensor_tensor(out=t3, in0=bank[:, 0:8 * NS].rearrange("p (k n) -> p k n", k=8), in1=A_s[:], op=ALU.mult),
                    [bkey, "A_s"], ["tmps"])
                dve(lambda e: e.tensor_tensor(out=hT[:, :, NP_:TT], in0=t3, in1=modfm[:, 0:8, 1:17], op=ALU.add), ["tmps", "modfm"], ["hT4"])

        p1_stage1(0)
        p1_stage1(1)
        for i in range(17):
            p1_stage2(i)
            if i + 2 < 17:
                p1_stage1(i + 2)

        ckpt(2)
        ckpt(3)
        def col(t, c, h):
            return t[:, c * 8 + h:c * 8 + h + 1]

        P.fence(); arena.reset()
        GC = 4
        GW = GC * 128
        NG = 16 // GC
        qTs = [ab(TT), ab(TT)]; kTs = [ab(TT), ab(TT)]; vTs = [ab(TT), ab(TT)]; szBs = [ab(TT), ab(TT)]
        pre = af(3 + TT); acc = af(TT); sqs = ab(NP_)
        hsc = [af(16 * 8), af(16 * 8)]
        o_alias = arena.off
        setT = []
        for s_ in range(2):
            setT.append(dict(
                Ktok=ab(GW), Kg=ab(GW), KpT=ab(GW), Dm=af(GW), Dm2=af(GW), En=af(GW),
                Mb=ab(GW), Nb=ab(GW), D=ab(GW), Xp=ab(GW), Xb=ab(GW),
                Tb=ab(GW), nW=ab(GW), Vs=ab(GW), Kd=ab(GW), at=ab(GW),
                oraw=af(GW), og=ab(GW), xo=ab(GW), sso=af(GC), ro=af(GC),
                T=(B5 if s_ == 0 else B6), Tk=("B5" if s_ == 0 else "B6"),
                X=B[1 + 2 * s_], Xk=f"B{1 + 2 * s_}", Y=B[2 + 2 * s_], Yk=f"B{2 + 2 * s_}"))
        S = af(128); Sb = ab(128); vnb = ab(128); o1s = af(128); junk2 = ab(128)
        dve(lambda e: e.memset(pre[:, 0:3], 0.0), [], ["pre"])
        done = set()
        o_end = arena.off
        arena.off = o_alias
        xg = af(TT); axg = af(TT); lg = axg; negA = af(2)
        betaT = af(TT)[0:8, :]; gT = af(TT)[0:8, :]; sbT = af(NP_)[0:8, :]
        tmpl = af(128)
        assert arena.off <= o_end, (arena.off, o_end)
        arena.off = o_end

        def wait_for(key):
            while key not in done:
                yield "blocked"

        def rsqrt_ln(out_ap, in_ap, scale, eps, rk, wk, tmp_ap, tmpk):
            dve(lambda e: e.tensor_scalar(out=tmp_ap, in0=in_ap, scalar1=scale, scalar2=eps, op0=ALU.mult, op1=ALU.add), rk, [tmpk])
            act(lambda e: e.activation(out=tmp_ap, in_=tmp_ap, func=AF.Ln), [tmpk], [tmpk])
            act(lambda e: e.activation(out=out_ap, in_=tmp_ap, func=AF.Exp, scale=-0.5), [tmpk], wk)

        def conv_task(h):
            hb = h % 2
            qT, kT, vT, szB = qTs[hb], kTs[hb], vTs[hb], szBs[hb]
            HS = hsc[hb]
            ssq, rqk, fsc, fg, fd, sbh, rqe, tmpq = (HS[:, 0:32], HS[:, 32:64], HS[:, 64:80], HS[:, 80:96], HS[:, 96:112],
                                                     HS[:, 112:128], None, None)
            wt, wkey = get_w("head", [(w_in, [OFF_Q + hh * 128, OFF_K + hh * 128, OFF_V + hh * 128, OFF_ZB + hh * 128], 128) for hh in range(8)], h)
            if h == 7:
                wq[("brA", 0)] = load_w(w_in, [OFF_HA, OFF_CA, OFF_BA, OFF_ZA], 128)
            for j, (dst, dname, dsamp) in enumerate([(qT, f"qT{hb}", qs_s), (kT, f"kT{hb}", ks_s), (vT, f"vT{hb}", vs_s)]):
                ch = j * 8 + h
                for nb, (t0, n) in enumerate(BLK):
                    for k in range(8):
                        pe(lambda e, k=k, t0=t0, n=n: e.matmul(out=B[0][:, 0:n], lhsT=wt[:, k, j * 128:(j + 1) * 128], rhs=hT[:, k, t0:t0 + n],
                                                              start=(k == 0), stop=(k == 7)), [wkey, "hT"], ["B0"])
                        if k == 3 and n > 16:
                            yield
                    if nb % 2 == 0:
                        act(lambda e, t0=t0, n=n: e.activation(out=pre[:, 3 + t0:3 + t0 + n], in_=B[0][:, 0:n], func=AF.Copy), ["B0"], ["pre"])
                    else:
                        dve(lambda e, t0=t0, n=n: e.tensor_copy(out=pre[:, 3 + t0:3 + t0 + n], in_=B[0][:, 0:n]), ["B0"], ["pre"])
                    yield
                pool(lambda e, ch=ch: e.tensor_copy(out=tail_q_p[:, ch, :], in_=pre[:, NP_:NP_ + 3]), ["pre"], ["tail_q_p"])
                pool(lambda e, ch=ch: e.tensor_copy(out=tail_q_s[:, ch, 0:2, :], in_=cbufq[:, ch, 1:3, :]), ["cbufq"], ["tail_q_s"])
                pool(lambda e, ch=ch: e.tensor_copy(out=tail_q_s[:, ch, 2, :], in_=pre[:, 3 + NP_:3 + TT]), ["pre"], ["tail_q_s"])
                wc = [vecs[:, V_CB + t * 24 + ch:V_CB + t * 24 + ch + 1] for t in range(4)]
                HB = NP_ // 2
                for hf in range(2):
                    c0 = hf * HB
                    pool(lambda e, wc=wc, c0=c0: e.tensor_scalar(out=acc[:, c0:c0 + HB], in0=pre[:, c0:c0 + HB], scalar1=wc[0], scalar2=0.0,
                                                                 op0=ALU.mult, op1=ALU.add), ["pre", "vecs"], [f"acc{hf}"])
                    yield
                for t in range(1, 4):
                    for hf in range(2):
                        c0 = hf * HB
                        dve(lambda e, t=t, wc=wc, c0=c0: e.scalar_tensor_tensor(out=acc[:, c0:c0 + HB], in0=pre[:, c0 + t:c0 + t + HB], scalar=wc[t],
                                                                                in1=acc[:, c0:c0 + HB], op0=ALU.mult, op1=ALU.add),
                            ["pre", f"acc{hf}", "vecs"], [f"acc{hf}"])
                        yield
                pool(lambda e, wc=wc: e.tensor_scalar(out=acc[:, NP_:TT], in0=pre[:, 3 + NP_:3 + TT], scalar1=wc[3], scalar2=0.0, op0=ALU.mult, op1=ALU.add),
                     ["pre", "vecs"], ["acc"])
                for t in range(3):
                    dve(lambda e, t=t, wc=wc, ch=ch: e.scalar_tensor_tensor(out=acc[:, NP_:TT], in0=cbufq[:, ch, t, :], scalar=wc[t], in1=acc[:, NP_:TT],
                                                                            op0=ALU.mult, op1=ALU.add), ["cbufq", "acc", "vecs"], ["acc"])
                for hf in range(2):
                    c0 = hf * HB
                    act(lambda e, dst=dst, c0=c0: e.activation(out=dst[:, c0:c0 + HB], in_=acc[:, c0:c0 + HB], func=AF.Silu), [f"acc{hf}"], [dname])
                    yield
                act(lambda e, dsamp=dsamp: e.activation(out=dsamp[:, h, :], in_=acc[:, NP_:TT], func=AF.Silu), ["acc"], [dname + "_s"])
                yield
            for nb, (t0, n) in enumerate(BLK):
                for k in range(8):
                    pe(lambda e, k=k, t0=t0, n=n: e.matmul(out=B[0][:, 0:n], lhsT=wt[:, k, 384:512], rhs=hT[:, k, t0:t0 + n],
                                                          start=(k == 0), stop=(k == 7)), [wkey, "hT"], ["B0"])
                    if k == 3 and n > 16:
                        yield
                act(lambda e, t0=t0, n=n: e.activation(out=szB[:, t0:t0 + n], in_=B[0][:, 0:n], func=AF.Silu), ["B0"], [f"szB{hb}"])
                yield
            dve(lambda e: e.tensor_copy(out=XBs[:, h, :], in_=szB[:, NP_:TT]), [f"szB{hb}"], ["XBs"])
            for qi, (src, sname) in enumerate([(qT, f"qT{hb}"), (kT, f"kT{hb}")]):
                act(lambda e, src=src: e.activation(out=sqs, in_=src[:, 0:NP_], func=AF.Square), [sname], ["sqs"])
                for c in range(16):
                    pe(lambda e, c=c, qi=qi: e.matmul(out=B[0][:, qi * 16 + c:qi * 16 + c + 1], lhsT=sqs[:, c * 128:(c + 1) * 128],
                                                      rhs=ones_b[:, 0:1], start=True, stop=True), ["sqs", "ones_b"], ["B0"])
                yield
            hk = f"hsc{hb}"
            dve(lambda e: e.tensor_copy(out=ssq, in_=B[0][:, 0:32]), ["B0"], [hk])
            rsqrt_ln(rqk, ssq, 1.0, EPS, [hk], [hk], ssq, hk)
            dve(lambda e: e.tensor_scalar(out=rqk[:, 0:16], in0=rqk[:, 0:16], scalar1=128.0 ** -0.5, scalar2=None, op0=ALU.mult), [hk], [hk])
            sbv = sb_tok[:, :].rearrange("p (c h) -> p c h", c=16)[:, :, h]
            egv = egc[:, :].rearrange("p (c h) -> p c h", c=16)[:, :, h]
            ekv = ekd[:, :].rearrange("p (c h) -> p c h", c=16)[:, :, h]
            dve(lambda e: e.tensor_copy(out=sbh, in_=sbv), ["sb_tok"], [hk])
            dve(lambda e: e.tensor_tensor(out=fsc, in0=rqk[:, 16:32], in1=sbv, op=ALU.mult), [hk, "sb_tok"], [hk])
            dve(lambda e: e.tensor_tensor(out=fg, in0=fsc, in1=egv, op=ALU.mult), [hk, "egc"], [hk])
            dve(lambda e: e.tensor_tensor(out=fd, in0=fsc, in1=ekv, op=ALU.mult), [hk, "ekd"], [hk])
            dve(lambda e: e.tensor_tensor(out=ssq[:, 0:16], in0=rqk[:, 0:16], in1=egv, op=ALU.mult), [hk, "egc"], [hk])
            done.add(("conv", h))
            yield

        def prep_task(h, g):
            hb = h % 2
            s_ = g % 2
            Z = setT[s_]
            qT, kT, vT = qTs[hb], kTs[hb], vTs[hb]
            HS = hsc[hb]
            fsc, fg, fd, sbh = HS[:, 64:80], HS[:, 80:96], HS[:, 96:112], HS[:, 112:128]
            hk = f"hsc{hb}"
            T, Tk, X, Xk, Y, Yk = Z["T"], Z["Tk"], Z["X"], Z["Xk"], Z["Y"], Z["Yk"]
            K = lambda n: f"{n}{s_}"
            g0 = g * GW
            pe(lambda e: e.matmul(out=X[:, 0:GW], lhsT=sel[:, h * 128:(h + 1) * 128], rhs=gcT[:, g0:g0 + GW], start=True, stop=True),
               ["sel", "gcT"], [Xk])
            for cc in range(GC):
                c0 = g0 + cc * 128
                pe(lambda e, cc=cc, c0=c0: e.transpose(out=T[:, cc * 128:(cc + 1) * 128], in_=kT[:, c0:c0 + 128], identity=identb[:]),
                   [f"kT{hb}", "identb"], [Tk])
                pe(lambda e, cc=cc, c0=c0: e.transpose(out=T[:, GW + cc * 128:GW + (cc + 1) * 128], in_=vT[:, c0:c0 + 128], identity=identb[:]),
                   [f"vT{hb}", "identb"], [Tk])
            yield
            for cc in range(GC):
                c = g * GC + cc
                sl = slice(cc * 128, (cc + 1) * 128)
                vsl = slice(GW + cc * 128, GW + (cc + 1) * 128)
                dve(lambda e, sl=sl, c=c: e.tensor_scalar(out=Z["Ktok"][:, sl], in0=T[:, sl], scalar1=fsc[:, c:c + 1], scalar2=None, op0=ALU.mult),
                    [Tk, hk], [K("Ktok")])
                dve(lambda e, sl=sl, c=c: e.tensor_scalar(out=Z["Kg"][:, sl], in0=T[:, sl], scalar1=fg[:, c:c + 1], scalar2=None, op0=ALU.mult),
                    [Tk, hk], [K("Kg")])
                act(lambda e, sl=sl, c=c: e.activation(out=Z["Kd"][:, sl], in_=T[:, sl], func=AF.Identity, scale=fd[:, c:c + 1]),
                    [Tk, hk], [K("Kd")])
                act(lambda e, sl=sl, vsl=vsl, c=c: e.activation(out=Z["Vs"][:, sl], in_=T[:, vsl], func=AF.Identity, scale=sbh[:, c:c + 1]),
                    [Tk, hk], [K("Vs")])
                dve(lambda e, sl=sl, c=c: e.scalar_tensor_tensor(out=Z["Dm"][:, sl], in0=X[:, sl], scalar=col(gc_tok, c, h), in1=biasA[:],
                                                                 op0=ALU.subtract, op1=ALU.add), [Xk, "gc_tok", "biasA"], [K("Dm")])
                dve(lambda e, sl=sl, c=c: e.scalar_tensor_tensor(out=Z["Dm2"][:, sl], in0=X[:, sl], scalar=col(gc_tok, c, h), in1=biasB[:],
                                                                 op0=ALU.subtract, op1=ALU.add), [Xk, "gc_tok", "biasB"], [K("Dm2")])
            yield
            for cc in range(GC):
                sl = slice(cc * 128, (cc + 1) * 128)
                pe(lambda e, sl=sl, cc=cc: e.transpose(out=T[:, cc * 128:(cc + 1) * 128], in_=Z["Ktok"][:, sl], identity=identb[:]),
                   [K("Ktok"), "identb"], [Tk])
            yield
            act(lambda e: e.activation(out=Z["KpT"], in_=T[:, 0:GW], func=AF.Copy), [Tk], [K("KpT")])
            act(lambda e: e.activation(out=Z["Dm"], in_=Z["Dm"], func=AF.Exp), [K("Dm")], [K("Dm")])
            act(lambda e: e.activation(out=Z["Dm2"], in_=Z["Dm2"], func=AF.Exp, scale=-1.0), [K("Dm2")], [K("Dm2")])
            dve(lambda e: e.tensor_tensor(out=Z["En"], in0=Z["Dm"], in1=negmA[:, 0:GW], op=ALU.mult), [K("Dm"), "negmA"], [K("En")])
            yield
            for cc in range(GC):
                sl = slice(cc * 128, (cc + 1) * 128)
                c0 = g0 + cc * 128
                pe(lambda e, sl=sl: e.matmul(out=Y[:, sl], lhsT=Z["KpT"][:, sl], rhs=Z["KpT"][:, sl], start=True, stop=True), [K("KpT")], [Yk])
                pe(lambda e, sl=sl, cc=cc, c0=c0: e.matmul(out=X[:, cc * 128:(cc + 1) * 128], lhsT=Z["KpT"][:, sl], rhs=qT[:, c0:c0 + 128],
                                                           start=True, stop=True), [K("KpT"), f"qT{hb}"], [Xk])
            yield
            dve(lambda e: e.tensor_tensor(out=Z["at"], in0=X[:, 0:GW], in1=Z["Dm"], op=ALU.mult), [Xk, K("Dm")], [K("at")])
            dve(lambda e: e.tensor_tensor(out=Z["Mb"], in0=Y[:, 0:GW], in1=Z["En"], op=ALU.mult), [Yk, K("En")], [K("Mb")])
            dve(lambda e: e.scalar_tensor_tensor(out=Z["Nb"], in0=Y[:, 0:GW], scalar=-1.0, in1=Z["Dm2"], op0=ALU.mult, op1=ALU.mult),
                [Yk, K("Dm2")], [K("Nb")])
            yield
            Dt = Z["Tb"]
            D = Z["D"]
            U16 = mybir.dt.uint16
            mku = mkb[:].bitcast(U16)
            dve(lambda e: e.tensor_copy(out=D, in_=identb2[:, 0:GW]), ["identb2"], [K("D")])
            act(lambda e: e.activation(out=Dt, in_=identb2[:, 0:GW], func=AF.Copy), ["identb2"], [K("Tb")])
            yield
            dve(lambda e: e.copy_predicated(out=D, mask=mku[:, 0, 0:GW], data=Z["Nb"]), [K("Nb"), "mkb", K("D")], [K("D")])
            dve(lambda e: e.copy_predicated(out=Dt, mask=mku[:, 0, 0:GW], data=Z["Mb"]), [K("Mb"), "mkb", K("Tb")], [K("Tb")])
            yield
            for lvl in range(1, 7):
                last = (lvl == 6)
                for cc in range(GC):
                    sl = slice(cc * 128, (cc + 1) * 128)
                    pe(lambda e, sl=sl, cc=cc: e.matmul(out=Y[:, cc * 128:(cc + 1) * 128], lhsT=Z["Nb"][:, sl], rhs=Dt[:, sl],
                                                        start=True, stop=True), [K("Nb"), K("Tb")], [Yk])
                if not last:
                    for cc in range(GC):
                        sl = slice(cc * 128, (cc + 1) * 128)
                        pe(lambda e, sl=sl, cc=cc: e.matmul(out=X[:, cc * 128:(cc + 1) * 128], lhsT=Z["Mb"][:, sl], rhs=D[:, sl],
                                                            start=True, stop=True), [K("Mb"), K("D")], [Xk])
                yield
                act(lambda e: e.activation(out=Z["Xb"], in_=Y[:, 0:GW], func=AF.Copy), [Yk], [K("Xb")])
                if not last:
                    act(lambda e: e.activation(out=Z["Xp"], in_=X[:, 0:GW], func=AF.Copy), [Xk], [K("Xp")])
                yield
                for cc in range(GC):
                    sl = slice(cc * 128, (cc + 1) * 128)
                    pe(lambda e, sl=sl, cc=cc: e.matmul(out=Y[:, cc * 128:(cc + 1) * 128], lhsT=D[:, sl], rhs=Z["Xb"][:, sl],
                                                        start=True, stop=True), [K("D"), K("Xb")], [Yk])
                if not last:
                    for cc in range(GC):
                        sl = slice(cc * 128, (cc + 1) * 128)
                        pe(lambda e, sl=sl, cc=cc: e.matmul(out=X[:, cc * 128:(cc + 1) * 128], lhsT=Dt[:, sl], rhs=Z["Xp"][:, sl],
                                                            start=True, stop=True), [K("Tb"), K("Xp")], [Xk])
                yield
                dve(lambda e, lvl=lvl: e.copy_predicated(out=Dt, mask=mku[:, lvl, 0:GW], data=Y[:, 0:GW]), [Yk, "mkb", K("Tb")], [K("Tb")])
                if not last:
                    dve(lambda e, lvl=lvl: e.copy_predicated(out=D, mask=mku[:, lvl, 0:GW], data=X[:, 0:GW]), [Xk, "mkb", K("D")], [K("D")])
                yield
            for cc in range(GC):
                sl = slice(cc * 128, (cc + 1) * 128)
                pe(lambda e, sl=sl: e.matmul(out=X[:, sl], lhsT=Z["Kg"][:, sl], rhs=Z["Tb"][:, sl], start=True, stop=True), [K("Kg"), K("Tb")], [Xk])
            yield
            act(lambda e: e.activation(out=Z["nW"], in_=X[:, 0:GW], func=AF.Identity, scale=-1.0), [Xk], [K("nW")])
            done.add(("prep", h, g))
            yield

        def prep_chain(h, parity):
            yield from wait_for(("conv", h))
            yield from wait_for("3a")
            for g in range(parity, NG, 2):
                if g >= 2:
                    yield from wait_for(("scan", h, g - 2))
                elif h > 0:
                    yield from wait_for(("scan", h - 1, NG - 2 + parity))
                yield from prep_task(h, g)

        def scan_chain(h):
            hb = h % 2
            qT, szB = qTs[hb], szBs[hb]
            HS = hsc[hb]
            rqk, rqe = HS[:, 32:64], HS[:, 0:16]
            hk = f"hsc{hb}"
            if h > 0:
                yield from wait_for(("scanhead", h - 1))
            else:
                yield from wait_for("3a")
            dve(lambda e: e.memset(S, 0.0), [], ["S"])
            dve(lambda e: e.memset(Sb, 0.0), [], ["Sb"])
            for g in range(NG):
                yield from wait_for(("prep", h, g))
                s_ = g % 2
                Z = setT[s_]
                K = lambda n: f"{n}{s_}"
                for cc in range(GC):
                    c = g * GC + cc
                    sl = slice(cc * 128, (cc + 1) * 128)
                    qsl = slice(c * 128, (c + 1) * 128)
                    pe(lambda e, sl=sl: e.matmul(out=B7[:, 0:128], lhsT=Z["Tb"][:, sl], rhs=Z["Vs"][:, sl], start=True, stop=False), [K("Tb"), K("Vs")], ["B7"])
                    pe(lambda e, sl=sl: e.matmul(out=B7[:, 0:128], lhsT=Z["nW"][:, sl], rhs=Sb, start=False, stop=True), [K("nW"), "Sb"], ["B7"])
                    act(lambda e: e.activation(out=vnb, in_=B7[:, 0:128], func=AF.Copy), ["B7"], ["vnb"])
                    pe(lambda e, sl=sl: e.matmul(out=B7[:, 384:512], lhsT=Z["Kd"][:, sl], rhs=vnb, start=True, stop=True), [K("Kd"), "vnb"], ["B7"])
                    pe(lambda e, qsl=qsl: e.matmul(out=B7[:, 128:256], lhsT=qT[:, qsl], rhs=Sb, start=True, stop=True), [f"qT{hb}", "Sb"], ["B7"])
                    pe(lambda e, sl=sl: e.matmul(out=B7[:, 256:384], lhsT=Z["at"][:, sl], rhs=vnb, start=True, stop=True), [K("at"), "vnb"], ["B7"])
                    dve(lambda e, c=c: e.scalar_tensor_tensor(out=S, in0=S, scalar=col(egl, c, h), in1=B7[:, 384:512], op0=ALU.mult, op1=ALU.add),
                        ["S", "egl", "B7"], ["S"])
                    act(lambda e: e.activation(out=Sb, in_=S, func=AF.Copy), ["S"], ["Sb"])
                    act(lambda e, c=c: e.activation(out=o1s, in_=B7[:, 128:256], func=AF.Identity, scale=rqe[:, c:c + 1]), ["B7", hk], ["o1s"])
                    dve(lambda e, c=c, sl=sl: e.scalar_tensor_tensor(out=Z["oraw"][:, sl], in0=B7[:, 256:384], scalar=rqk[:, c:c + 1], in1=o1s,
                                                                     op0=ALU.mult, op1=ALU.add), ["B7", hk, "o1s"], [K("oraw")])
                    act(lambda e, cc=cc, sl=sl: e.activation(out=junk2, in_=Z["oraw"][:, sl], func=AF.Square, accum_out=Z["sso"][:, cc:cc + 1]),
                        [K("oraw")], ["junk2", K("sso")])
                    yield
                rsqrt_ln(Z["ro"], Z["sso"], 1.0 / 128, EPS, [K("sso")], [K("ro")], Z["sso"], K("sso"))
                for cc in range(GC):
                    sl = slice(cc * 128, (cc + 1) * 128)
                    pool(lambda e, cc=cc, sl=sl: e.tensor_scalar(out=Z["og"][:, sl], in0=Z["oraw"][:, sl], scalar1=Z["ro"][:, cc:cc + 1], scalar2=0.0,
                                                                 op0=ALU.mult, op1=ALU.add), [K("oraw"), K("ro")], [K("og")])
                    pe(lambda e, cc=cc, sl=sl: e.transpose(out=Z["T"][:, GW + cc * 128:GW + (cc + 1) * 128], in_=Z["og"][:, sl], identity=identb[:]),
                       [K("og"), "identb"], [Z["Tk"]])
                t0 = g * GW
                dve(lambda e, t0=t0: e.scalar_tensor_tensor(out=Z["xo"], in0=Z["T"][:, GW:2 * GW], scalar=vecs[:, V_ON:V_ON + 1],
                                                            in1=szB[:, t0:t0 + GW], op0=ALU.mult, op1=ALU.mult),
                    [Z["Tk"], "vecs", f"szB{hb}"], [K("xo")])
                dma(lambda e, t0=t0: e.dma_start(out=xbs_d[h][:, t0:t0 + GW], in_=Z["xo"]), [K("xo")], ["xbs_dram"])
                done.add(("scan", h, g))
                yield
            dma(lambda e: e.dma_start(out=S_p_d[h], in_=S), ["S"], ["S_p_dram"])
            done.add(("scanhead", h))
            yield

        def phase3a_chain():
            HT_ALL = [f"hT{n}" for n in range(5)]
            wt, wkey = load_w(w_in, [OFF_BETA], 16)

            def ev_beta(nb, ps, t0, n, bkey):
                act(lambda e: e.activation(out=betaT[:, t0:t0 + n], in_=ps, func=AF.Sigmoid), [bkey], ["betaT"])
            proj_fm(wt, wkey, 0, 8, ev_beta, M=8, bankset=(1, 2))

            def ev_alpha(nb, ps, t0, n, bkey):
                dve(lambda e: e.tensor_scalar(out=xg[0:8, t0:t0 + n], in0=ps, scalar1=hv[:, 1:2], scalar2=None, op0=ALU.add), [bkey, "hv"], ["xg"])
            proj_fm(wt, wkey, 1, 8, ev_alpha, M=8, bankset=(1, 2))
            yield
            dve(lambda e: e.scalar_tensor_tensor(out=axg[0:8, :], in0=xg[0:8, :], scalar=-1.0, in1=xg[0:8, :], op0=ALU.mult, op1=ALU.max), ["xg"], ["axg"])
            act(lambda e: e.activation(out=axg[0:8, :], in_=axg[0:8, :], func=AF.Exp, scale=-1.0), ["axg"], ["axg"])
            act(lambda e: e.activation(out=lg[0:8, :], in_=axg[0:8, :], func=AF.Ln, bias=1.0, scale=1.0), ["axg"], ["lg"])
            act(lambda e: e.activation(out=negA[0:8, 0:1], in_=hv[:, 0:1], func=AF.Exp), ["hv"], ["negA"])
            dve(lambda e: e.tensor_scalar(out=negA[0:8, 0:1], in0=negA[0:8, 0:1], scalar1=-1.0, scalar2=None, op0=ALU.mult), ["negA"], ["negA"])
            dve(lambda e: e.scalar_tensor_tensor(out=lg[0:8, :], in0=xg[0:8, :], scalar=0.0, in1=lg[0:8, :], op0=ALU.max, op1=ALU.add),
                ["xg", "lg"], ["lg"])
            dve(lambda e: e.tensor_scalar(out=gT[:, :], in0=lg[0:8, :], scalar1=negA[0:8, 0:1], scalar2=None, op0=ALU.mult), ["lg", "negA"], ["gT"])
            act(lambda e: e.activation(out=sbT[:, :], in_=betaT[:, 0:NP_], func=AF.Sqrt), ["betaT"], ["sbT"])
            for c in range(16):
                dve(lambda e, c=c: e.tensor_tensor_scan(out=gcT[:, c * 128:(c + 1) * 128], data0=ones_f[0:8, 0:128], data1=gT[:, c * 128:(c + 1) * 128],
                                                        initial=0.0, op0=ALU.mult, op1=ALU.add), ["gT", "ones_f"], ["gcT"])
            yield
            for c in range(16):
                pe(lambda e, c=c: e.transpose(out=B[2][:, c * 8:(c + 1) * 8], in_=gcT[:, c * 128:(c + 1) * 128], identity=ident[0:8, 0:8]),
                   ["gcT", "ident"], ["B2"])
                pe(lambda e, c=c: e.transpose(out=B[3][:, c * 8:(c + 1) * 8], in_=sbT[:, c * 128:(c + 1) * 128], identity=ident[0:8, 0:8]),
                   ["sbT", "ident"], ["B3"])
            dve(lambda e: e.tensor_copy(out=gc_tok[:], in_=B[2][:, 0:128]), ["B2"], ["gc_tok"])
            dve(lambda e: e.tensor_copy(out=sb_tok[:], in_=B[3][:, 0:128]), ["B3"], ["sb_tok"])
            yield
            tl3 = tmpl[0:8, :].rearrange("p (c h) -> p c h", c=16)
            gl_src = gcT[:, :].rearrange("p (c t) -> p c t", c=16)[:, :, 127:128]
            dve(lambda e: e.tensor_tensor(out=tl3, in0=bm[:, :].rearrange("p (c h) -> p c h", c=16), in1=gl_src.to_broadcast([8, 16, 8]), op=ALU.mult),
                ["bm", "gcT"], ["tmpl"])
            pe(lambda e: e.matmul(out=B[4][:, 0:128], lhsT=ones_f[0:8, :], rhs=tmpl[0:8, :], start=True, stop=True), ["tmpl", "ones_f"], ["B4"])
            act(lambda e: e.activation(out=egl[:], in_=B[4][:, 0:128], func=AF.Exp), ["B4"], ["egl"])
            dve(lambda e: e.tensor_tensor(out=ekd[:], in0=B[4][:, 0:128], in1=gc_tok[:], op=ALU.subtract), ["B4", "gc_tok"], ["ekd"])
            act(lambda e: e.activation(out=ekd[:], in_=ekd[:], func=AF.Exp), ["ekd"], ["ekd"])
            act(lambda e: e.activation(out=egc[:], in_=gc_tok[:], func=AF.Exp), ["gc_tok"], ["egc"])
            dve(lambda e: e.tensor_copy(out=beta_s[:, :], in_=betaT[:, NP_:TT]), ["betaT"], ["beta_s"])
            dve(lambda e: e.tensor_copy(out=g_s[:, :], in_=gT[:, NP_:TT]), ["gT"], ["g_s"])


            P.fence()
            done.add("3a")
            yield

        def conv_chain():
            for h in range(8):
                if h >= 2:
                    yield from wait_for(("scanhead", h - 2))
                yield from conv_task(h)

        def run_chains(chains):
            active = list(chains)
            while active:
                progressed = False
                for ch in list(active):
                    try:
                        r = next(ch)
                        if r != "blocked":
                            progressed = True
                    except StopIteration:
                        active.remove(ch)
                        progressed = True
                assert progressed, "chain deadlock"

        chains = [phase3a_chain(), conv_chain()]
        for h in range(8):
            chains += [prep_chain(h, 0), prep_chain(h, 1), scan_chain(h)]
        run_chains(chains)

        ckpt(4)
        P.fence(); arena.reset()
        ZA = ab(8 * TT).rearrange("p (k t) -> p k t", k=8)
        za_end = arena.off
        B5f = B5[:, :].bitcast(F32)
        B6f = B6[:, :].bitcast(F32)

        def sample_chain():
            def bcast_hs(dst, srcT, skey, dkey):
                t = af(128)
                dve(lambda e: e.tensor_tensor(out=t[0:8, :].rearrange("p (h s) -> p h s", h=8), in0=bm[:, :].rearrange("p (s h) -> p h s", s=16),
                                              in1=srcT.unsqueeze(1).to_broadcast([8, 8, NS]), op=ALU.mult), ["bm", skey], [dkey + "_t"])
                pe(lambda e: e.matmul(out=B[2][:, 0:128], lhsT=ones_f[0:8, :], rhs=t[0:8, :], start=True, stop=True), [dkey + "_t", "ones_f"], ["B2"])
                dve(lambda e: e.tensor_copy(out=dst, in_=B[2][:, 0:128]), ["B2"], [dkey])
            beta_bc = af(128); g_bc = af(128); eg_bc = af(128)
            bcast_hs(beta_bc, beta_s[:, :], "beta_s", "beta_bc")
            bcast_hs(g_bc, g_s[:, :], "g_s", "g_bc")
            act(lambda e: e.activation(out=eg_bc, in_=g_bc, func=AF.Exp), ["g_bc"], ["eg_bc"])
            yield
            HS = 8 * NS
            q2 = qs_s[:].rearrange("p h s -> p (h s)"); k2 = ks_s[:].rearrange("p h s -> p (h s)"); v2 = vs_s[:].rearrange("p h s -> p (h s)")
            sq2 = af(HS); rq2 = af(HS); rk2 = af(HS); tmp2 = af(HS)
            for (src, skey, dst, dkey, scl) in [(q2, "qT_s", rq2, "rq2", 128.0 ** -0.5), (k2, "kT_s", rk2, "rk2", 1.0)]:
                dve(lambda e, src=src: e.tensor_tensor(out=sq2, in0=src, in1=src, op=ALU.mult), [skey], ["sq2"])
                pe(lambda e: e.matmul(out=B[3][:, 0:HS], lhsT=ones_f[:], rhs=sq2, start=True, stop=True), ["sq2", "ones_f"], ["B3"])
                rsqrt_ln(dst, B[3][:, 0:HS], 1.0, EPS, ["B3"], [dkey], tmp2, "tmp2")
                dve(lambda e, src=src, dst=dst, scl=scl: e.scalar_tensor_tensor(out=src, in0=src, scalar=scl, in1=dst, op0=ALU.mult, op1=ALU.mult),
                    [skey, dkey], [skey])
                yield
            wcol = af(HS); ucol = af(HS); qg = af(HS); qk = af(HS); qk_bc = af(HS)
            dve(lambda e: e.tensor_tensor(out=wcol, in0=k2, in1=beta_bc, op=ALU.mult), ["kT_s", "beta_bc"], ["wcol"])
            dve(lambda e: e.tensor_tensor(out=wcol, in0=wcol, in1=eg_bc, op=ALU.mult), ["wcol", "eg_bc"], ["wcol"])
            dve(lambda e: e.tensor_tensor(out=ucol, in0=v2, in1=beta_bc, op=ALU.mult), ["vT_s", "beta_bc"], ["ucol"])
            dve(lambda e: e.tensor_tensor(out=qg, in0=q2, in1=eg_bc, op=ALU.mult), ["qT_s", "eg_bc"], ["qg"])
            dve(lambda e: e.tensor_tensor(out=qk, in0=q2, in1=k2, op=ALU.mult), ["qT_s", "kT_s"], ["qk"])
            pe(lambda e: e.matmul(out=B[4][:, 0:HS], lhsT=ones_f[:], rhs=qk, start=True, stop=True), ["qk", "ones_f"], ["B4"])
            dve(lambda e: e.tensor_copy(out=qk_bc, in_=B[4][:, 0:HS]), ["B4"], ["qk_bc"])
            yield
            os_all = af(HS)
            w3 = wcol.rearrange("p (h s) -> p h s", h=8); u3 = ucol.rearrange("p (h s) -> p h s", h=8)
            qg3 = qg.rearrange("p (h s) -> p h s", h=8); qk3 = qk_bc.rearrange("p (h s) -> p h s", h=8)
            eg3 = eg_bc.rearrange("p (h s) -> p h s", h=8); o3 = os_all.rearrange("p (h s) -> p h s", h=8)
            Ss = [af(1024), af(1024)]
            vn_all = af(HS); tmpS = af(1024)
            vn3 = vn_all.rearrange("p (h s) -> p h s", h=8)
            rowk = af(1024); rowv = af(1024); rmask = [af(1024), af(1024)]
            for s in range(NS):
                Si, sk = Ss[s % 2], f"Ss{s % 2}"
                S3 = Si.rearrange("p (h e) -> p h e", h=8)
                bank, bkey = (B7, "B7") if s % 2 == 0 else (B[4], "B4")
                dma(lambda e: e.dma_start(out=S3, in_=sd[s].rearrange("h d e -> d h e")), [], [sk])
                for hh in range(8):
                    pe(lambda e, hh=hh: e.matmul(out=bank[:, hh:hh + 1], lhsT=S3[:, hh, :], rhs=w3[:, hh, s:s + 1], start=True, stop=True),
                       [sk, "wcol"], [bkey])
                    pe(lambda e, hh=hh: e.matmul(out=bank[:, 8 + hh:9 + hh], lhsT=S3[:, hh, :], rhs=qg3[:, hh, s:s + 1], start=True, stop=True),
                       [sk, "qg"], [bkey])
                dve(lambda e: e.tensor_tensor(out=vn3[:, :, s], in0=u3[:, :, s], in1=bank[:, 0:8], op=ALU.subtract), ["ucol", bkey], ["vn_all"])
                dve(lambda e: e.tensor_tensor(out=o3[:, :, s], in0=qk3[:, :, s], in1=vn3[:, :, s], op=ALU.mult), ["qk_bc", "vn_all"], ["os_all"])
                dve(lambda e: e.tensor_tensor(out=o3[:, :, s], in0=o3[:, :, s], in1=bank[:, 8:16], op=ALU.add), ["os_all", bkey], ["os_all"])
                yield
            for hh in range(8):
                bkr, bkrk = (B[3], "B3") if hh < 4 else (B[4], "B4")
                bv, bvk = (B[1], "B1") if hh < 4 else (B[2], "B2")
                c0 = (hh % 4) * 128
                pe(lambda e, hh=hh, c0=c0, bkr=bkr: e.transpose(out=bkr[0:NS, c0:c0 + 128], in_=ks_s[:, hh, :], identity=ident[:]), ["kT_s", "ident"], [bkrk])
                pe(lambda e, hh=hh, c0=c0, bv=bv: e.transpose(out=bv[0:NS, c0:c0 + 128], in_=vn3[:, hh, :], identity=ident[:]), ["vn_all", "ident"], [bvk])
            yield
            dve(lambda e: e.tensor_copy(out=rowk[0:NS, 0:512], in_=B[3][0:NS, :]), ["B3"], ["rowk"])
            dve(lambda e: e.tensor_copy(out=rowk[0:NS, 512:1024], in_=B[4][0:NS, :]), ["B4"], ["rowk"])
            act(lambda e: e.activation(out=rowv[0:NS, 0:512], in_=B[1][0:NS, :], func=AF.Copy), ["B1"], ["rowv"])
            act(lambda e: e.activation(out=rowv[0:NS, 512:1024], in_=B[2][0:NS, :], func=AF.Copy), ["B2"], ["rowv"])
            yield
            for s in range(NS):
                Si, sk = Ss[s % 2], f"Ss{s % 2}"
                S3 = Si.rearrange("p (h e) -> p h e", h=8)
                rm, rmk = rmask[s % 2], f"rmask{s % 2}"
                dma(lambda e: e.dma_start(out=S3, in_=sd[s].rearrange("h d e -> d h e")), [], [sk])
                dve(lambda e: e.tensor_scalar(out=rm[0:NS, :], in0=rowv[0:NS, :], scalar1=ident[0:NS, s:s + 1], scalar2=None, op0=ALU.mult),
                    ["rowv", "ident"], [rmk])
                yield
                for hh in range(8):
                    bo, bok = (B[1], "B1") if hh < 4 else (B[2], "B2")
                    c0 = (hh % 4) * 128
                    pe(lambda e, hh=hh, c0=c0, bo=bo: e.matmul(out=bo[:, c0:c0 + 128], lhsT=rowk[0:NS, hh * 128:(hh + 1) * 128],
                                                               rhs=rm[0:NS, hh * 128:(hh + 1) * 128], start=True, stop=True), ["rowk", rmk], [bok])
                T3 = tmpS.rearrange("p (h e) -> p h e", h=8)
                dve(lambda e: e.tensor_tensor(out=T3, in0=S3, in1=eg3[:, :, s:s + 1].to_broadcast([128, 8, 128]), op=ALU.mult), [sk, "eg_bc"], ["tmpS"])
                yield
                dve(lambda e: e.tensor_tensor(out=tmpS[:, 0:512], in0=tmpS[:, 0:512], in1=B[1][:, :], op=ALU.add), ["tmpS", "B1"], ["tmpS"])
                dve(lambda e: e.tensor_tensor(out=tmpS[:, 512:1024], in0=tmpS[:, 512:1024], in1=B[2][:, :], op=ALU.add), ["tmpS", "B2"], ["tmpS"])
                dma(lambda e: e.dma_start(out=S_s_d[s].rearrange("h d e -> d h e"), in_=T3), ["tmpS"], ["S_s_dram"])
                yield
            so2 = af(HS); rso = af(HS)
            dve(lambda e: e.tensor_tensor(out=so2, in0=os_all, in1=os_all, op=ALU.mult), ["os_all"], ["so2"])
            pe(lambda e: e.matmul(out=B[2][:, 0:HS], lhsT=ones_f[:], rhs=so2, start=True, stop=True), ["so2", "ones_f"], ["B2"])
            rsqrt_ln(rso, B[2][:, 0:HS], 1.0 / 128, EPS, ["B2"], ["rso"], tmp2, "tmp2")
            dve(lambda e: e.scalar_tensor_tensor(out=os_all, in0=os_all, scalar=vecs[:, V_ON:V_ON + 1], in1=rso, op0=ALU.mult, op1=ALU.mult),
                ["os_all", "vecs", "rso"], ["os_all"])
            dve(lambda e: e.tensor_tensor(out=XBs[:], in0=XBs[:], in1=o3, op=ALU.mult), ["XBs", "os_all"], ["XBs"])
            dma(lambda e: e.dma_start(out=xbs_d.rearrange("h p t -> p h t")[:, :, NP_:TT], in_=XBs[:]), ["XBs"], ["xbs_dram"])
            yield

        gwv = gcT_full[:, 0:2048].bitcast(BF16)
        gws = [gwv[:, 0:2048].rearrange("p (k n) -> p k n", k=8), gwv[:, 2048:4096].rearrange("p (k n) -> p k n", k=8)]

        def load_gates(fo):
            gwt = gws[fo % 2]
            for gi_, goff in enumerate([OFF_GA, OFF_GB]):
                src = w_in[:, goff + fo * 128:goff + (fo + 1) * 128].rearrange("(k p) n -> p k n", p=128)
                dma(lambda e, gi_=gi_, src=src, gwt=gwt: e.dma_start(out=gwt[:, :, gi_ * 128:(gi_ + 1) * 128], in_=src), [], [f"gw{fo % 2}"], eng="pool")
        pre_w = {}

        def branchA_chain():
            ub = af(2 + TT); acca = af(TT); szf = ab(TT); tz = af(512)
            dve(lambda e: e.memset(ub[:, 0:2], 0.0), [], ["ub"])
            banks = [(B[0], "B0"), (B5f, "B5"), (B6f, "B6")]
            cnt = [0]

            def proj(wt, wkey, j, evac):
                for nb, (t0, n) in enumerate(BLK):
                    bank, bkey = banks[cnt[0] % 3]
                    cnt[0] += 1
                    for k in range(8):
                        pe(lambda e, k=k, bank=bank, t0=t0, n=n: e.matmul(out=bank[:, 0:n], lhsT=wt[:, k, j * 128:(j + 1) * 128], rhs=hT[:, k, t0:t0 + n],
                                                                         start=(k == 0), stop=(k == 7)), [wkey, "hT"], [bkey])
                    evac(nb, bank[:, 0:n], t0, n, bkey)
                    yield

            for f in range(8):
                wt, wkey = get_w("brA", [(w_in, [OFF_HA + ff * 128, OFF_CA + ff * 128, OFF_BA + ff * 128, OFF_ZA + ff * 128], 128) for ff in range(8)], f)
                if f == 7:
                    pre_w["wA0"] = load_w(w_out_a, [0], 512)

                def ev_ha(nb, ps, t0, n, bkey):
                    act(lambda e: e.activation(out=ub[:, 2 + t0:2 + t0 + n], in_=ps, func=AF.Copy), [bkey], ["ub"])
                yield from proj(wt, wkey, 0, ev_ha)

                def ev_ca(nb, ps, t0, n, bkey):
                    dve(lambda e: e.tensor_tensor(out=ub[:, 2 + t0:2 + t0 + n], in0=ps, in1=ub[:, 2 + t0:2 + t0 + n], op=ALU.mult), [bkey, "ub"], ["ub"])
                yield from proj(wt, wkey, 1, ev_ca)
                pool(lambda e, f=f: e.tensor_copy(out=tail_a_p[:, f, :], in_=ub[:, NP_:NP_ + 2]), ["ub"], ["tail_a_p"])
                pool(lambda e, f=f: e.tensor_copy(out=tail_a_s[:, f, 0, :], in_=cbufa[:, f, 1, :]), ["cbufa"], ["tail_a_s"])
                pool(lambda e, f=f: e.tensor_copy(out=tail_a_s[:, f, 1, :], in_=ub[:, 2 + NP_:2 + TT]), ["ub"], ["tail_a_s"])
                wc = [vecs[:, V_CA + t * 8 + f:V_CA + t * 8 + f + 1] for t in range(3)]
                HB = NP_ // 2
                for hf in range(2):
                    c0 = hf * HB
                    pool(lambda e, wc=wc, c0=c0: e.tensor_scalar(out=acca[:, c0:c0 + HB], in0=ub[:, c0:c0 + HB], scalar1=wc[0], scalar2=0.0,
                                                                 op0=ALU.mult, op1=ALU.add), ["ub", "vecs"], ["acca"])
                    yield
                for t in range(1, 3):
                    for hf in range(2):
                        c0 = hf * HB
                        dve(lambda e, t=t, wc=wc, c0=c0: e.scalar_tensor_tensor(out=acca[:, c0:c0 + HB], in0=ub[:, c0 + t:c0 + t + HB], scalar=wc[t],
                                                                                in1=acca[:, c0:c0 + HB], op0=ALU.mult, op1=ALU.add),
                            ["ub", "acca", "vecs"], ["acca"])
                        yield
                pool(lambda e, wc=wc: e.tensor_scalar(out=acca[:, NP_:TT], in0=ub[:, 2 + NP_:2 + TT], scalar1=wc[2], scalar2=0.0, op0=ALU.mult, op1=ALU.add),
                     ["ub", "vecs"], ["acca"])
                for t in range(2):
                    dve(lambda e, t=t, wc=wc, f=f: e.scalar_tensor_tensor(out=acca[:, NP_:TT], in0=cbufa[:, f, t, :], scalar=wc[t], in1=acca[:, NP_:TT],
                                                                          op0=ALU.mult, op1=ALU.add), ["cbufa", "acca", "vecs"], ["acca"])

                def ev_za(nb, ps, t0, n, bkey, f=f):
                    act(lambda e: e.activation(out=szf[:, t0:t0 + n], in_=ps, func=AF.Silu), [bkey], ["szf"])
                yield from proj(wt, wkey, 3, ev_za)

                def ev_ba(nb, ps, t0, n, bkey, f=f):
                    dve(lambda e: e.tensor_tensor(out=tz[:, 0:n], in0=ps, in1=acca[:, t0:t0 + n], op=ALU.mult), [bkey, "acca"], ["tz"])
                    dve(lambda e: e.tensor_tensor(out=ZA[:, f, t0:t0 + n], in0=tz[:, 0:n], in1=szf[:, t0:t0 + n], op=ALU.mult), ["tz", "szf"], [f"ZA{nb}"])
                yield from proj(wt, wkey, 2, ev_ba)
            pre_w["wB0"] = load_w(w_out_b, [0], 512)
            load_gates(0)
            yield

        run_chains([sample_chain(), branchA_chain()])

        ckpt(6)
        P.fence()
        arena.off = za_end
        MG = ab(8 * TT).rearrange("p (k t) -> p k t", k=8)
        mg_end = arena.off
        XB = ab(8 * TT).rearrange("p (k t) -> p k t", k=8)
        for nb_, (t0_, n_) in enumerate(BLK):
            dma(lambda e, t0_=t0_, n_=n_: e.dma_start(out=XB[:, :, t0_:t0_ + n_], in_=xbs_d.rearrange("h p t -> p h t")[:, :, t0_:t0_ + n_]),
                ["xbs_dram"], [f"XB{nb_}"])
        sA = ab(512); sB = ab(512); tA = af(512); tB = af(512)

        def mm8(bank, bkey, M, lhs_fn, rhs_fn, rkeys, n):
            for k in range(8):
                pe(lambda e, k=k: e.matmul(out=bank[0:M, 0:n], lhsT=lhs_fn(k), rhs=rhs_fn(k), start=(k == 0), stop=(k == 7)), rkeys, [bkey])

        for half in range(2):
            wA, wAk = pre_w["wA0"] if half == 0 else load_w(w_out_a, [half * 512], 512)
            wB, wBk = pre_w["wB0"] if half == 0 else load_w(w_out_b, [half * 512], 512)
            for jj in range(4):
                fo = half * 4 + jj
                if fo + 1 < 8:
                    load_gates(fo + 1)
                gw, gwk = gws[fo % 2], f"gw{fo % 2}"
                for nb, (t0, n) in enumerate(BLK):
                    mm8(B[0], "B0", 128, lambda k, gw=gw: gw[:, k, 0:128], lambda k, t0=t0, n=n: hT[:, k, t0:t0 + n], [gwk, f"hT{nb}"], n)
                    act(lambda e, n=n: e.activation(out=sA[:, 0:n], in_=B[0][:, 0:n], func=AF.Sigmoid), ["B0"], ["sA"])
                    mm8(B[1], "B1", 128, lambda k, jj=jj, wA=wA: wA[:, k, jj * 128:(jj + 1) * 128], lambda k, t0=t0, n=n: ZA[:, k, t0:t0 + n],
                        [wAk, f"ZA{nb}"], n)
                    dve(lambda e, n=n: e.tensor_tensor(out=tA[:, 0:n], in0=B[1][:, 0:n], in1=sA[:, 0:n], op=ALU.mult), ["B1", "sA"], ["tA"])
                    mm8(B[2], "B2", 128, lambda k, gw=gw: gw[:, k, 128:256], lambda k, t0=t0, n=n: hT[:, k, t0:t0 + n], [gwk, f"hT{nb}"], n)
                    act(lambda e, n=n: e.activation(out=sB[:, 0:n], in_=B[2][:, 0:n], func=AF.Sigmoid), ["B2"], ["sB"])
                    mm8(B[3], "B3", 128, lambda k, jj=jj, wB=wB: wB[:, k, jj * 128:(jj + 1) * 128], lambda k, t0=t0, n=n: XB[:, k, t0:t0 + n],
                        [wBk, f"XB{nb}"], n)
                    dve(lambda e, n=n: e.tensor_tensor(out=tB[:, 0:n], in0=B[3][:, 0:n], in1=sB[:, 0:n], op=ALU.mult), ["B3", "sB"], ["tB"])
                    pool(lambda e, fo=fo, t0=t0, n=n: e.tensor_tensor(out=MG[:, fo, t0:t0 + n], in0=tA[:, 0:n], in1=tB[:, 0:n], op=ALU.add),
                         ["tA", "tB"], [f"MG{nb}"])

        ckpt(7)
        P.fence()
        arena.off = 0
        wo_b = ab(8 * 1024).rearrange("p (k n) -> p k n", k=8)
        for n2 in range(2):
            src = w_o[:, n2 * 512:(n2 + 1) * 512].rearrange("(k p) n -> p k n", p=128)
            dma(lambda e, n2=n2, src=src: e.dma_start(out=wo_b[:, :, n2 * 512:(n2 + 1) * 512], in_=src), [], [f"wo_b{n2}"], eng="pool")
        xt = [af(D), af(D)]
        yt0 = af(D)
        assert arena.off <= za_end
        arena.off = mg_end
        yt = [yt0, af(D)]
        junk = ab(512)
        ss2 = af(2); rs5 = af(2); tmp5 = af(2)
        for i in range(17):
            rows = 128 if i < 16 else NS
            t0 = i * 128
            mk = f"MG{i // 4}"
            xi, xk = xt[i % 2], f"xt{i % 2}"
            yi, yk = yt[i % 2], f"yt{i % 2}"
            src = xp[i * 128:(i + 1) * 128, :] if i < 16 else xsm
            dst = y_p[i * 128:(i + 1) * 128, :] if i < 16 else y_s
            gsrc = gn if i < 16 else gns
            dma(lambda e, xi=xi, rows=rows, src=src: e.dma_start(out=xi[0:rows, :], in_=src), [], [xk])
            bo = 2 * (i % 2)
            for n2 in range(2):
                mm8(B[bo + n2], f"B{bo + n2}", rows, lambda k, t0=t0, rows=rows: MG[:, k, t0:t0 + rows], lambda k, n2=n2: wo_b[:, k, n2 * 512:(n2 + 1) * 512],
                    [mk, f"wo_b{n2}"], 512)
                act(lambda e, n2=n2, rows=rows, bo=bo: e.activation(out=junk[0:rows, :], in_=B[bo + n2][0:rows, :], func=AF.Square,
                                                                  accum_out=ss2[0:rows, n2:n2 + 1]), [f"B{bo + n2}"], ["junk", "ss2"])
            dve(lambda e, rows=rows: e.tensor_tensor(out=ss2[0:rows, 0:1], in0=ss2[0:rows, 0:1], in1=ss2[0:rows, 1:2], op=ALU.add), ["ss2"], ["ss2"])
            rsqrt_small(rs5[0:rows, 0:1], ss2[0:rows, 0:1], 1.0 / D, EPS, ["ss2"], ["rs5"], tmp5[0:rows, 0:1], "tmp5")
            for n2 in range(2):
                cs_ = slice(n2 * 512, (n2 + 1) * 512)
                dve(lambda e, n2=n2, rows=rows, cs_=cs_, yi=yi, gsrc=gsrc, bo=bo: e.scalar_tensor_tensor(
                    out=yi[0:rows, cs_], in0=B[bo + n2][0:rows, :], scalar=rs5[0:rows, 0:1], in1=gsrc[0:rows, cs_], op0=ALU.mult, op1=ALU.mult),
                    [f"B{bo + n2}", "rs5", "gn", "gns"], [yk])
            dve(lambda e, rows=rows, yi=yi, xi=xi: e.tensor_tensor(out=yi[0:rows, :], in0=yi[0:rows, :], in1=xi[0:rows, :], op=ALU.add), [yk, xk], [yk])
            dma(lambda e, yi=yi, rows=rows, dst=dst: e.dma_start(out=dst, in_=yi[0:rows, :]), [yk], ["y_dram"])
        dma(lambda e: e.dma_start(out=tail_a_p_d, in_=tail_a_p[:].rearrange("p a b -> p (a b)")), ["tail_a_p"], ["tap_dram"])
        dma(lambda e: e.dma_start(out=tail_a_s_d, in_=tail_a_s[:].rearrange("p a b c -> p (a b c)")), ["tail_a_s"], ["tas_dram"])
        dma(lambda e: e.dma_start(out=tail_q_p_d, in_=tail_q_p[:].rearrange("p a b -> p (a b)")), ["tail_q_p"], ["tqp_dram"])
        dma(lambda e: e.dma_start(out=tail_q_s_d, in_=tail_q_s[:].rearrange("p a b c -> p (a b c)")), ["tail_q_s"], ["tqs_dram"])
        return

    with ExitStack() as st:
        P = Prog(nc, st)
        try:
            _body(st, P)
        except _Stop:
            pass
        P.fence()
        P.emit()
    return nc


_NC_CACHE = {}


def _consts():
    p = np.arange(128)[:, None]
    c = np.arange(128)[None, :]
    ident = np.eye(128, dtype=np.float32)
    biasA = np.where(c >= p, 0.0, -BIG).astype(np.float32)
    biasB = np.where(p > c, 0.0, BIG).astype(np.float32)
    negmA = np.tile(np.where(c > p, -1.0, 0.0).astype(np.float32), (1, 4))
    sel = np.zeros((8, 8, 128), np.float32)
    for h in range(8):
        sel[h, h, :] = 1.0
    bm = np.zeros((8, 16, 8), np.float32)
    for h in range(8):
        bm[h, :, h] = 1.0
    mks = np.zeros((128, 7, 128), np.float32)
    mks[:, 0, :] = (p // 2 == c // 2) & (p != c)
    for l in range(1, 7):
        b = 2 ** l
        mks[:, l, :] = (p // (2 * b) == c // (2 * b)) & (p // b != c // b)
    return dict(ident=ident, biasA=biasA, biasB=biasB, negmA=negmA, sel=sel.reshape(8, 1024), bm=bm.reshape(8, 128),
                mks=mks.reshape(128, 7 * 128))


def kernel(x_prompt, x_sample, c_prompt, c_sample, state_conv_a, state_conv_qkv, state_delta,
           ada_w, ada_b, norm_pre, w_in, conv_a_w, conv_b_w, a_log, dt_bias, onorm_w,
           w_out_a, w_out_b, w_o, norm_post):
    f = lambda a: np.ascontiguousarray(np.asarray(a, dtype=np.float32))
    x_prompt, x_sample, c_prompt, c_sample = f(x_prompt), f(x_sample), f(c_prompt), f(c_sample)
    state_conv_a, state_conv_qkv, state_delta = f(state_conv_a), f(state_conv_qkv), f(state_delta)
    if "nc" not in _NC_CACHE:
        _NC_CACHE["nc"] = build_nc(_NC_CACHE.get("stop", 99))
    nc = _NC_CACHE["nc"]
    n = 8
    vecs = np.zeros((128, 160), np.float32)
    vecs[:, 0:24] = f(ada_b)[0].reshape(24, 128).T
    vecs[:, 24:32] = f(norm_pre)[0].reshape(8, 128).T
    vecs[:, 32:56] = f(conv_a_w)[0].reshape(3, 8, 128).transpose(2, 0, 1).reshape(128, 24)
    vecs[:, 56] = f(onorm_w)[0]
    vecs[:, 64:160] = f(conv_b_w)[0].reshape(4, 24, 128).transpose(2, 0, 1).reshape(128, 96)
    hv = np.stack([f(a_log)[0], f(dt_bias)[0]], axis=1)
    shared = dict(ada_w=f(ada_w)[0], adab_bc=np.ascontiguousarray(np.broadcast_to(f(ada_b)[0, 2048:3072], (128, 1024))),
                  npost_bc=np.ascontiguousarray(np.broadcast_to(f(norm_post)[0], (128, 1024))),
                  vecs=vecs, hv=np.ascontiguousarray(hv), w_in=f(w_in)[0], w_out_a=f(w_out_a)[0], w_out_b=f(w_out_b)[0], w_o=f(w_o)[0])
    shared.update(_consts())
    in_maps = []
    for b in range(n):
        s0, s1 = b * NS, (b + 1) * NS
        ca = state_conv_a[0, s0:s1].reshape(NS, 2, 8, 128).transpose(3, 2, 1, 0)
        cq = state_conv_qkv[0, s0:s1].reshape(NS, 3, 24, 128).transpose(3, 2, 1, 0)
        m = dict(shared)
        m.update(xp=x_prompt[b], xsm=x_sample[s0:s1, 0, :], cp_bc=np.ascontiguousarray(np.broadcast_to(c_prompt[b], (128, 1024))),
                 cs=c_sample[s0:s1], cbufa=np.ascontiguousarray(ca).reshape(128, -1), cbufq=np.ascontiguousarray(cq).reshape(128, -1),
                 sd=state_delta[0, s0:s1])
        in_maps.append({k: np.ascontiguousarray(v) for k, v in m.items()})
    res = run_bass_kernel_spmd(nc, in_maps, core_ids=list(range(n)))
    R = res.results
    y_p = np.stack([R[b]["y_p"] for b in range(n)])
    y_s = np.concatenate([R[b]["y_s"] for b in range(n)])[:, None, :]
    nca_p = np.stack([R[b]["tail_a_p"].reshape(128, 8, 2).transpose(2, 1, 0).reshape(2, 1024) for b in range(n)])[None]
    ncq_p = np.stack([R[b]["tail_q_p"].reshape(128, 24, 3).transpose(2, 1, 0).reshape(3, 3072) for b in range(n)])[None]
    nd_p = np.stack([R[b]["S_p"] for b in range(n)])[None]
    nca_s = np.concatenate([R[b]["tail_a_s"].reshape(128, 8, 2, NS).transpose(3, 2, 1, 0).reshape(NS, 2, 1024) for b in range(n)])[None]
    ncq_s = np.concatenate([R[b]["tail_q_s"].reshape(128, 24, 3, NS).transpose(3, 2, 1, 0).reshape(NS, 3, 3072) for b in range(n)])[None]
    nd_s = np.concatenate([R[b]["S_s"] for b in range(n)])[None]
    out = (y_p, y_s, nca_p, ncq_p, nd_p, nca_s, ncq_s, nd_s)
    return tuple(np.ascontiguousarray(o, dtype=np.float32) for o in out)
```

```python
import numpy as np
from contextlib import ExitStack
import concourse.bass as bass
import concourse.mybir as mybir
from concourse.bass_utils import run_bass_kernel_spmd

F32 = mybir.dt.float32
BF16 = mybir.dt.bfloat16
ALU = mybir.AluOpType
AF = mybir.ActivationFunctionType
AX = mybir.AxisListType

NP_ = 2048
NS = 16
TT = NP_ + NS
D = 1024
INC = 10256
EPS = 1e-6
BIG = 60000.0
BLK = [(0, 512), (512, 512), (1024, 512), (1536, 512), (2048, 16)]
OFF_HA, OFF_CA, OFF_BA, OFF_ZA = 0, 1024, 2048, 3072
OFF_Q, OFF_K, OFF_V, OFF_ZB = 4096, 5120, 6144, 7168
OFF_BETA, OFF_ALPHA, OFF_GA, OFF_GB = 8192, 8200, 8208, 9232


class _Rec:
    def __init__(self):
        self.call = None

    def __getattr__(self, name):
        def f(*a, **kw):
            assert self.call is None
            self.call = (name, a, kw)
            return self
        return f


class Prog:
    ENGS = ("pe", "act", "dve", "pool", "sp")
    SEM_LIMIT = 30000

    def __init__(self, nc, stack):
        self.nc, self.stack = nc, stack
        self.ops = {e: [] for e in self.ENGS}
        self.eng_sem, self.eng_cnt = {}, {}
        self.waited = {e: {} for e in self.ENGS}
        self.writers, self.readers = {}, {}
        self.dma_sem, self.dma_cnt = {}, {}
        self.all_sems = {}
        self.nsem = 0
        for e in self.ENGS:
            self._new_eng_sem(e)

    def _sem(self, name):
        self.nsem += 1
        return self.stack.enter_context(self.nc.semaphore(name))

    def _new_eng_sem(self, e):
        self.eng_sem[e] = self._sem(f"s_{e}_{self.nsem}")
        self.eng_cnt[e] = 0

    @staticmethod
    def _bank(key):
        if isinstance(key, str) and len(key) >= 2 and key[0] == "B" and key[1].isdigit():
            return key[:2]
        return None

    def _need(self, eng, tok, waits, raw):
        sem, val, teng = tok
        if teng == eng and (not raw or eng == "pe"):
            return
        w = self.waited[eng]
        if w.get(id(sem), 0) < val:
            w[id(sem)] = val
            waits[id(sem)] = (sem, val)

    def op(self, eng, fn, reads=(), writes=(), dma=False):
        waits = {}
        reads = [self._bank(b) or b for b in reads]
        writes = [self._bank(b) or b for b in writes]
        for b in reads:
            for tok in self.writers.get(b, {}).values():
                self._need(eng, tok, waits, True)
            if self._bank(b):
                for tok in self.readers.get(b, {}).values():
                    self._need(eng, tok, waits, False)
        for b in writes:
            for tok in self.writers.get(b, {}).values():
                self._need(eng, tok, waits, True)
            for tok in self.readers.get(b, {}).values():
                self._need(eng, tok, waits, False)
        if dma:
            key = writes[0] if writes else ("rd", reads[0])
            if key not in self.dma_sem:
                self.dma_sem[key] = self._sem(f"d{self.nsem}")
                self.dma_cnt[key] = 0
            self.dma_cnt[key] += 16
            tok = (self.dma_sem[key], self.dma_cnt[key], "dma")
            inc = (tok[0], 16)
        else:
            if self.eng_cnt[eng] >= self.SEM_LIMIT:
                self._new_eng_sem(eng)
            self.eng_cnt[eng] += 1
            tok = (self.eng_sem[eng], self.eng_cnt[eng], eng)
            inc = (tok[0], 1)
        self.all_sems[id(tok[0])] = tok
        rec = _Rec()
        fn(rec)
        assert rec.call is not None
        self.ops[eng].append((list(waits.values()), rec.call, inc))
        for b in reads:
            self.readers.setdefault(b, {})[id(tok[0])] = tok
        for b in writes:
            self.writers.setdefault(b, {})[id(tok[0])] = tok
        return tok

    def fence(self, engs=None):
        for e in (engs or self.ENGS):
            waits = {}
            for tok in self.all_sems.values():
                self._need(e, tok, waits, True)
            if waits:
                self.ops[e].append((list(waits.values()), None, None))

    def emit(self):
        with self.nc.Block() as block:
            def mk(ename):
                def body(eng):
                    for waits, fn, inc in self.ops[ename]:
                        for sem, val in waits:
                            eng.wait_ge(sem, val)
                        if fn is not None:
                            name, a, kw = fn
                            getattr(eng, name)(*a, **kw).then_inc(inc[0], inc[1])
                return body
            block.tensor(mk("pe"))
            block.scalar(mk("act"))
            block.vector(mk("dve"))
            block.gpsimd(mk("pool"))
            block.sync(mk("sp"))


class Arena:
    def __init__(self, tile, ncols):
        self.tile, self.ncols, self.off = tile, ncols, 0

    def reset(self):
        self.off = 0

    def alloc(self, cols):
        cols = (cols + 1) // 2 * 2
        assert self.off + cols <= self.ncols, ("arena overflow", self.off, cols, self.ncols)
        ap = self.tile[:, self.off:self.off + cols]
        self.off += cols
        return ap


class _Stop(Exception):
    pass


def build_nc(stop=99):
    nc = bass.Bass("TRN2", target_bir_lowering=False)

    def ckpt(k):
        if stop <= k:
            raise _Stop()

    def din(name, shape, dt=F32):
        return nc.dram_tensor(name, list(shape), dt, kind="ExternalInput").ap()

    def dout(name, shape, dt=F32):
        return nc.dram_tensor(name, list(shape), dt, kind="ExternalOutput").ap()

    xp = din("xp", [NP_, D]); xsm = din("xsm", [NS, D])
    cp_bc = din("cp_bc", [128, D]); cs = din("cs", [NS, D])
    cbufa_d = din("cbufa", [128, 8 * 2 * NS]); cbufq_d = din("cbufq", [128, 24 * 3 * NS])
    sd = din("sd", [NS, 8, 128, 128])
    ada_w = din("ada_w", [D, 3 * D]); adab_bc_d = din("adab_bc", [128, D]); npost_bc_d = din("npost_bc", [128, D])
    vecs_d = din("vecs", [128, 160]); hv_d = din("hv", [8, 2])
    w_in = din("w_in", [D, INC]); w_out_a = din("w_out_a", [D, D]); w_out_b = din("w_out_b", [D, D]); w_o = din("w_o", [D, D])
    ident_d = din("ident", [128, 128]); biasA_d = din("biasA", [128, 128]); biasB_d = din("biasB", [128, 128])
    negmA_d = din("negmA", [128, 512]); sel_d = din("sel", [8, 8 * 128]); bm_d = din("bm", [8, 128])
    mks_d = din("mks", [128, 7 * 128])

    y_p = dout("y_p", [NP_, D]); y_s = dout("y_s", [NS, D])
    tail_a_p_d = dout("tail_a_p", [128, 8 * 2]); tail_a_s_d = dout("tail_a_s", [128, 8 * 2 * NS])
    tail_q_p_d = dout("tail_q_p", [128, 24 * 3]); tail_q_s_d = dout("tail_q_s", [128, 24 * 3 * NS])
    S_p_d = dout("S_p", [8, 128, 128]); S_s_d = dout("S_s", [NS, 8, 128, 128])

    def _body(st, P):

        def sb(name, shape, dt=F32):
            return st.enter_context(nc.sbuf_tensor("sb_" + name, list(shape), dt))

        def psb(name, shape, dt=F32):
            return st.enter_context(nc.psum_tensor("ps_" + name, list(shape), dt))

        hT = sb("hT", [128, 8, TT], BF16)
        XBs = sb("XBs", [128, 8, NS], BF16)
        xbs_d = nc.dram_tensor("xbs_scratch", [8, 128, TT], BF16).ap()
        wts = [sb(f"wt{i}", [128, 8, 512], BF16) for i in range(2)]
        ident = sb("ident", [128, 128]); identb = sb("identb", [128, 128], BF16)
        ones_f = sb("ones_f", [128, 128]); ones_b = sb("ones_b", [128, 128], BF16)
        biasA = sb("biasA", [128, 128]); biasB = sb("biasB", [128, 128])
        negmA = sb("negmA", [128, 512]); mkb = sb("mkb", [128, 7, 512], BF16); identb2 = sb("identb2", [128, 512], BF16)
        sel = sb("sel", [8, 8 * 128]); bm = sb("bm", [8, 128])
        vecs = sb("vecs", [128, 160]); hv = sb("hv", [8, 2])
        modfm = sb("modfm", [128, 16, 17])
        gn = sb("gn", [128, D]); gns = sb("gns", [NS, D])
        a_p = sb("a_p", [128, 8]); A_s = sb("A_s", [128, 8, NS])
        cbufa = sb("cbufa", [128, 8, 2, NS]); cbufq = sb("cbufq", [128, 24, 3, NS])
        tail_a_p = sb("tail_a_p", [128, 8, 2]); tail_a_s = sb("tail_a_s", [128, 8, 2, NS])
        tail_q_p = sb("tail_q_p", [128, 24, 3]); tail_q_s = sb("tail_q_s", [128, 24, 3, NS])
        gc_tok = sb("gc_tok", [128, 128]); sb_tok = sb("sb_tok", [128, 128])
        egc = sb("egc", [128, 128]); ekd = sb("ekd", [128, 128]); egl = sb("egl", [128, 128])
        gcT_full = sb("gcT", [128, NP_]); gcT = gcT_full[0:8, :]; beta_s = sb("beta_s", [8, NS]); g_s = sb("g_s", [8, NS])
        qs_s = sb("qs_s", [128, 8, NS]); ks_s = sb("ks_s", [128, 8, NS]); vs_s = sb("vs_s", [128, 8, NS])
        ARENA_F = 27800
        arena_t = sb("arena", [128, ARENA_F])
        arena = Arena(arena_t, ARENA_F)

        def af(cols):
            return arena.alloc(cols)

        def ab(cols):
            return arena.alloc((cols + 1) // 2).bitcast(BF16)

        B = [psb(f"B{i}", [128, 512]) for i in range(5)]
        B5 = psb("B5", [128, 1024], BF16); B6 = psb("B6", [128, 1024], BF16)
        B7 = psb("B7", [128, 512])

        def dve(fn, r, w): return P.op("dve", fn, reads=r, writes=w)
        def act(fn, r, w): return P.op("act", fn, reads=r, writes=w)
        def pool(fn, r, w): return P.op("pool", fn, reads=r, writes=w)
        def pe(fn, r, w): return P.op("pe", fn, reads=r, writes=w)
        def dma(fn, r, w, eng="sp"): return P.op(eng, fn, reads=r, writes=w, dma=True)

        def rsqrt_small(out_ap, in_ap, scale, eps, rk, wk, tmp_ap, tmpk):
            dve(lambda e: e.tensor_scalar(out=tmp_ap, in0=in_ap, scalar1=scale, scalar2=eps, op0=ALU.mult, op1=ALU.add), rk, [tmpk])
            act(lambda e: e.activation(out=tmp_ap, in_=tmp_ap, func=AF.Sqrt), [tmpk], [tmpk])
            dve(lambda e: e.reciprocal(out=out_ap, in_=tmp_ap), [tmpk], wk)

        wstate = {"n": 0}

        def load_w(dram, col_offs, width):
            i = wstate["n"] % 2
            wstate["n"] += 1
            wt = wts[i]
            key = f"wt{i}"
            runs = []
            for j, c in enumerate(col_offs):
                if runs and runs[-1][1] + runs[-1][2] == c:
                    runs[-1][2] += width
                else:
                    runs.append([j * width, c, width])
            for dst, c, wd in runs:
                src = dram[:, c:c + wd].rearrange("(k p) n -> p k n", p=128)
                dma(lambda e, dst=dst, wd=wd, src=src: e.dma_start(out=wt[:, :, dst:dst + wd], in_=src),
                    [], [key], eng="pool")
            return wt, key

        wq = {}

        def get_w(tag, specs, i):
            for j in (i, i + 1):
                if j < len(specs) and (tag, j) not in wq:
                    wq[(tag, j)] = load_w(*specs[j])
            return wq[(tag, i)]

        pj = {"n": 0}

        def proj_fm(wt, wkey, j, width, evac, blocks=BLK, rhs=None, rkey="hT", M=128, bankset=(0, 1)):
            rhs = hT if rhs is None else rhs
            for nb, (t0, n) in enumerate(blocks):
                bi = bankset[pj["n"] % 2]
                pj["n"] += 1
                bank, bkey = B[bi], f"B{bi}"
                for k in range(8):
                    pe(lambda e, k=k, bank=bank, t0=t0, n=n: e.matmul(
                        out=bank[0:M, 0:n], lhsT=wt[:, k, j * width:j * width + M], rhs=rhs[:, k, t0:t0 + n],
                        start=(k == 0), stop=(k == 7)), [wkey, rkey], [bkey])
                evac(nb, bank[0:M, 0:n], t0, n, bkey)

        for t, d, k in [(ident, ident_d, "ident"), (biasA, biasA_d, "biasA"), (biasB, biasB_d, "biasB"),
                        (negmA, negmA_d, "negmA"), (sel, sel_d, "sel"), (bm, bm_d, "bm"), (vecs, vecs_d, "vecs"),
                        (hv, hv_d, "hv"), (gn, npost_bc_d, "gn")]:
            dma(lambda e, t=t, d=d: e.dma_start(out=t[:], in_=d), [], [k])
        dma(lambda e: e.dma_start(out=cbufa[:].rearrange("p a b c -> p (a b c)"), in_=cbufa_d), [], ["cbufa"])
        dma(lambda e: e.dma_start(out=cbufq[:].rearrange("p a b c -> p (a b c)"), in_=cbufq_d), [], ["cbufq"])
        pool(lambda e: e.memset(ones_f[:], 1.0), [], ["ones_f"])
        pool(lambda e: e.memset(ones_b[:], 1.0), [], ["ones_b"])
        dve(lambda e: e.tensor_copy(out=identb[:], in_=ident[:]), ["ident"], ["identb"])
        for c in range(4):
            dve(lambda e, c=c: e.tensor_copy(out=identb2[:, c * 128:(c + 1) * 128], in_=ident[:]), ["ident"], ["identb2"])
            dma(lambda e, c=c: e.dma_start(out=mkb[:, :, c * 128:(c + 1) * 128], in_=mks_d.rearrange("p (l n) -> p l n", l=7)), [], ["mkb"], eng="pool")
        V_ADAB, V_NPRE, V_CA, V_ON, V_CB = 0, 24, 32, 56, 64

        ckpt(0.1)
        arena.reset()
        cpt = af(D); cst = af(D); gt = af(D)
        dma(lambda e: e.dma_start(out=gt, in_=adab_bc_d), [], ["gt"])
        scTp = ab(8 * 128).rearrange("p (k n) -> p k n", k=8)
        sc17 = ab(8 * 17).rearrange("p (k n) -> p k n", k=8)
        dma(lambda e: e.dma_start(out=cpt, in_=cp_bc), [], ["cpt"])
        dma(lambda e: e.dma_start(out=cst[0:NS, :], in_=cs), [], ["cst"])
        act(lambda e: e.activation(out=cpt, in_=cpt, func=AF.Silu), ["cpt"], ["cpt"])
        act(lambda e: e.activation(out=cst[0:NS, :], in_=cst[0:NS, :], func=AF.Silu), ["cst"], ["cst"])
        for half in range(2):
            bank, bkey = B[2 + half], f"B{2 + half}"
            for kk in range(4):
                k = half * 4 + kk
                pe(lambda e, k=k, kk=kk, bank=bank: e.transpose(out=bank[:, kk * 128:(kk + 1) * 128], in_=cpt[:, k * 128:(k + 1) * 128],
                                                              identity=ident[:]), ["cpt", "ident"], [bkey])
            dve(lambda e, half=half, bank=bank: e.tensor_copy(out=scTp[:, half * 4:half * 4 + 4, :],
                                                              in_=bank[:, :].rearrange("p (k n) -> p k n", k=4)), [bkey], ["scTp"])
        for k in range(8):
            pe(lambda e, k=k: e.transpose(out=B[4][:, k * NS:(k + 1) * NS], in_=cst[0:NS, k * 128:(k + 1) * 128],
                                          identity=ident[0:NS, 0:NS]), ["cst", "ident"], ["B4"])
        dve(lambda e: e.tensor_copy(out=sc17[:, :, 1:17], in_=B[4][:, 0:8 * NS].rearrange("p (k n) -> p k n", k=8)), ["B4"], ["sc17"])
        dve(lambda e: e.tensor_copy(out=sc17[:, :, 0:1], in_=scTp[:, :, 0:1]), ["scTp"], ["sc17"])
        ckpt(0.2)
        for g in range(6):
            wt, wkey = get_w("ada", [(ada_w, [gg * 512], 512) for gg in range(6)], g)
            if g < 4:
                for j in range(4):
                    fc = g * 4 + j
                    for k in range(8):
                        pe(lambda e, k=k, j=j, wt=wt: e.matmul(out=B[2][:, 0:17], lhsT=wt[:, k, j * 128:(j + 1) * 128], rhs=sc17[:, k, :],
                                                               start=(k == 0), stop=(k == 7)), [wkey, "sc17"], ["B2"])
                    act(lambda e, fc=fc: e.activation(out=modfm[:, fc, :], in_=B[2][:, 0:17], func=AF.Identity,
                                                      bias=vecs[:, V_ADAB + fc:V_ADAB + fc + 1], scale=1.0), ["B2", "vecs"], ["modfm"])
            else:
                n0 = (g - 4) * 512
                for k in range(8):
                    pe(lambda e, k=k, wt=wt: e.matmul(out=B[3][0:NS, :], lhsT=sc17[:, k, 1:17], rhs=wt[:, k, :],
                                                      start=(k == 0), stop=(k == 7)), [wkey, "sc17"], ["B3"])
                dve(lambda e, n0=n0: e.tensor_tensor(out=gns[:, n0:n0 + 512], in0=B[3][0:NS, :], in1=gt[0:NS, n0:n0 + 512], op=ALU.add),
                    ["B3", "gt"], ["gns"])
                dve(lambda e, n0=n0: e.tensor_tensor(out=gns[:, n0:n0 + 512], in0=gns[:, n0:n0 + 512], in1=gn[0:NS, n0:n0 + 512], op=ALU.mult),
                    ["gns", "gn"], ["gns"])
                for k in range(8):
                    pe(lambda e, k=k, wt=wt: e.matmul(out=B[4][:, :], lhsT=scTp[:, k, :], rhs=wt[:, k, :],
                                                      start=(k == 0), stop=(k == 7)), [wkey, "scTp"], ["B4"])
                dve(lambda e, n0=n0: e.tensor_tensor(out=gt[:, n0:n0 + 512], in0=B[4][:, :], in1=gt[:, n0:n0 + 512], op=ALU.add),
                    ["B4", "gt", "gns"], ["gt"])
                dve(lambda e, n0=n0: e.tensor_tensor(out=gn[:, n0:n0 + 512], in0=gn[:, n0:n0 + 512], in1=gt[:, n0:n0 + 512], op=ALU.mult),
                    ["gt", "gn", "gns"], ["gn"])
        ckpt(0.3)
        dve(lambda e: e.tensor_scalar(out=a_p[:], in0=modfm[:, 8:16, 0], scalar1=1.0, scalar2=None, op0=ALU.add), ["modfm"], ["a_p"])
        dve(lambda e: e.tensor_tensor(out=a_p[:], in0=a_p[:], in1=vecs[:, V_NPRE:V_NPRE + 8], op=ALU.mult), ["a_p", "vecs"], ["a_p"])
        ckpt(0.4)
        dve(lambda e: e.tensor_scalar(out=A_s[:], in0=modfm[:, 8:16, 1:17], scalar1=1.0, scalar2=None, op0=ALU.add), ["modfm"], ["A_s"])
        ckpt(0.5)
        for k in range(8):
            dve(lambda e, k=k: e.tensor_scalar(out=A_s[:, k, :], in0=A_s[:, k, :], scalar1=vecs[:, V_NPRE + k:V_NPRE + k + 1], scalar2=None,
                                               op0=ALU.mult), ["A_s", "vecs"], ["A_s"])

        ckpt(1)
        P.fence(); arena.reset()
        beta_w = load_w(w_in, [OFF_BETA], 16)
        wq[("head", 0)] = load_w(w_in, [OFF_Q, OFF_K, OFF_V, OFF_ZB], 128)
        xt = [af(D), af(D)]
        xh = [af(D), af(D)]
        junk = ab(D)
        ssx = af(18); rstdx = af(18); tmpx = af(18)
        tmps = af(8 * NS)

        def p1_stage1(i):
            rows = 128 if i < 16 else NS
            xi, xk = xt[i % 2], f"xt{i % 2}"
            hi, hk = xh[i % 2], f"xh{i % 2}"
            src = xp[i * 128:(i + 1) * 128, :] if i < 16 else xsm
            dma(lambda e: e.dma_start(out=xi[0:rows, :], in_=src), [], [xk])
            act(lambda e: e.activation(out=junk[0:rows, :], in_=xi[0:rows, :], func=AF.Square, accum_out=ssx[0:rows, i:i + 1]),
                [xk], ["junk", f"ssx{i}"])
            rsqrt_small(rstdx[0:rows, i:i + 1], ssx[0:rows, i:i + 1], 1.0 / D, EPS, [f"ssx{i}"], [f"rstdx{i}"], tmpx[0:rows, i:i + 1], f"tmpx{i}")
            dve(lambda e: e.tensor_scalar(out=hi[0:rows, :], in0=xi[0:rows, :], scalar1=rstdx[0:rows, i:i + 1], scalar2=None, op0=ALU.mult),
                [xk, f"rstdx{i}"], [hk])

        def p1_stage2(i):
            hi, hk = xh[i % 2], f"xh{i % 2}"
            pair = [(B[2], "B2"), (B[3], "B3")] if i % 2 == 0 else [(B[4], "B4"), (B[1], "B1")]
            if i < 16:
                for half in range(2):
                    bank, bkey = pair[half]
                    for kk in range(4):
                        k = half * 4 + kk
                        pe(lambda e, k=k, kk=kk, bank=bank: e.transpose(out=bank[:, kk * 128:(kk + 1) * 128], in_=hi[:, k * 128:(k + 1) * 128],
                                                                     identity=ident[:]), [hk, "ident"], [bkey])
                    for kk in range(4):
                        k = half * 4 + kk
                        if kk % 2 == 0:
                            act(lambda e, k=k, kk=kk, bank=bank: e.activation(
                                out=hT[:, k, i * 128:(i + 1) * 128], in_=bank[:, kk * 128:(kk + 1) * 128], func=AF.Identity,
                                bias=modfm[:, k, 0:1], scale=a_p[:, k:k + 1]), [bkey, "modfm", "a_p"], [f"hT{i // 4}"])
                        else:
                            dve(lambda e, k=k, kk=kk, bank=bank: e.tensor_scalar(
                                out=hT[:, k, i * 128:(i + 1) * 128], in0=bank[:, kk * 128:(kk + 1) * 128], scalar1=a_p[:, k:k + 1],
                                scalar2=modfm[:, k, 0:1], op0=ALU.mult, op1=ALU.add), [bkey, "modfm", "a_p"], [f"hT{i // 4}"])
            else:
                bank, bkey = pair[0]
                for k in range(8):
                    pe(lambda e, k=k: e.transpose(out=bank[:, k * NS:(k + 1) * NS], in_=hi[0:NS, k * 128:(k + 1) * 128],
                                                  identity=ident[0:NS, 0:NS]), [hk, "ident"], [bkey])
                t3 = tmps.rearrange("p (k n) -> p k n", k=8)
                dve(lambda e: e.tensor_tensor(out=t3, in0=bank[:, 0:8 * NS].rearrange("p (k n) -> p k n", k=8), in1=A_s[:], op=ALU.mult),
                    [bkey, "A_s"], ["tmps"])
                dve(lambda e: e.tensor_tensor(out=hT[:, :, NP_:TT], in0=t3, in1=modfm[:, 0:8, 1:17], op=ALU.add), ["tmps", "modfm"], ["hT4"])

        p1_stage1(0)
        p1_stage1(1)
        for i in range(17):
            p1_stage2(i)
            if i + 2 < 17:
                p1_stage1(i + 2)

        ckpt(2)
        ckpt(3)
        def col(t, c, h):
            return t[:, c * 8 + h:c * 8 + h + 1]

        P.fence(); arena.reset()
        GC = 4
        GW = GC * 128
        NG = 16 // GC
        qTs = [ab(TT), ab(TT)]; kTs = [ab(TT), ab(TT)]; vTs = [ab(TT), ab(TT)]; szBs = [ab(TT), ab(TT)]
        pre = af(3 + TT); acc = af(TT); sqs = ab(NP_)
        hsc = [af(16 * 8), af(16 * 8)]
        o_alias = arena.off
        setT = []
        for s_ in range(2):
            setT.append(dict(
                Ktok=ab(GW), Kg=ab(GW), KpT=ab(GW), Dm=af(GW), Dm2=af(GW), En=af(GW),
                Mb=ab(GW), Nb=ab(GW), D=ab(GW), Xp=ab(GW), Xb=ab(GW),
                Tb=ab(GW), nW=ab(GW), Vs=ab(GW), Kd=ab(GW), at=ab(GW),
                oraw=af(GW), og=ab(GW), xo=ab(GW), sso=af(GC), ro=af(GC),
                T=(B5 if s_ == 0 else B6), Tk=("B5" if s_ == 0 else "B6"),
                X=B[1 + 2 * s_], Xk=f"B{1 + 2 * s_}", Y=B[2 + 2 * s_], Yk=f"B{2 + 2 * s_}"))
        S = af(128); Sb = ab(128); vnb = ab(128); o1s = af(128); junk2 = ab(128)
        dve(lambda e: e.memset(pre[:, 0:3], 0.0), [], ["pre"])
        done = set()
        o_end = arena.off
        arena.off = o_alias
        xg = af(TT); axg = af(TT); lg = axg; negA = af(2)
        betaT = af(TT)[0:8, :]; gT = af(TT)[0:8, :]; sbT = af(NP_)[0:8, :]
        tmpl = af(128)
        assert arena.off <= o_end, (arena.off, o_end)
        arena.off = o_end

        def wait_for(key):
            while key not in done:
                yield "blocked"

        def rsqrt_ln(out_ap, in_ap, scale, eps, rk, wk, tmp_ap, tmpk):
            dve(lambda e: e.tensor_scalar(out=tmp_ap, in0=in_ap, scalar1=scale, scalar2=eps, op0=ALU.mult, op1=ALU.add), rk, [tmpk])
            act(lambda e: e.activation(out=tmp_ap, in_=tmp_ap, func=AF.Ln), [tmpk], [tmpk])
            act(lambda e: e.activation(out=out_ap, in_=tmp_ap, func=AF.Exp, scale=-0.5), [tmpk], wk)

        def conv_task(h):
            hb = h % 2
            qT, kT, vT, szB = qTs[hb], kTs[hb], vTs[hb], szBs[hb]
            HS = hsc[hb]
            ssq, rqk, fsc, fg, fd, sbh, rqe, tmpq = (HS[:, 0:32], HS[:, 32:64], HS[:, 64:80], HS[:, 80:96], HS[:, 96:112],
                                                     HS[:, 112:128], None, None)
            wt, wkey = get_w("head", [(w_in, [OFF_Q + hh * 128, OFF_K + hh * 128, OFF_V + hh * 128, OFF_ZB + hh * 128], 128) for hh in range(8)], h)
            if h == 7:
                wq[("brA", 0)] = load_w(w_in, [OFF_HA, OFF_CA, OFF_BA, OFF_ZA], 128)
            for j, (dst, dname, dsamp) in enumerate([(qT, f"qT{hb}", qs_s), (kT, f"kT{hb}", ks_s), (vT, f"vT{hb}", vs_s)]):
                ch = j * 8 + h
                for nb, (t0, n) in enumerate(BLK):
                    for k in range(8):
                        pe(lambda e, k=k, t0=t0, n=n: e.matmul(out=B[0][:, 0:n], lhsT=wt[:, k, j * 128:(j + 1) * 128], rhs=hT[:, k, t0:t0 + n],
                                                              start=(k == 0), stop=(k == 7)), [wkey, "hT"], ["B0"])
                        if k == 3 and n > 16:
                            yield
                    if nb % 2 == 0:
                        act(lambda e, t0=t0, n=n: e.activation(out=pre[:, 3 + t0:3 + t0 + n], in_=B[0][:, 0:n], func=AF.Copy), ["B0"], ["pre"])
                    else:
                        dve(lambda e, t0=t0, n=n: e.tensor_copy(out=pre[:, 3 + t0:3 + t0 + n], in_=B[0][:, 0:n]), ["B0"], ["pre"])
                    yield
                pool(lambda e, ch=ch: e.tensor_copy(out=tail_q_p[:, ch, :], in_=pre[:, NP_:NP_ + 3]), ["pre"], ["tail_q_p"])
                pool(lambda e, ch=ch: e.tensor_copy(out=tail_q_s[:, ch, 0:2, :], in_=cbufq[:, ch, 1:3, :]), ["cbufq"], ["tail_q_s"])
                pool(lambda e, ch=ch: e.tensor_copy(out=tail_q_s[:, ch, 2, :], in_=pre[:, 3 + NP_:3 + TT]), ["pre"], ["tail_q_s"])
                wc = [vecs[:, V_CB + t * 24 + ch:V_CB + t * 24 + ch + 1] for t in range(4)]
                HB = NP_ // 2
                for hf in range(2):
                    c0 = hf * HB
                    pool(lambda e, wc=wc, c0=c0: e.tensor_scalar(out=acc[:, c0:c0 + HB], in0=pre[:, c0:c0 + HB], scalar1=wc[0], scalar2=0.0,
                                                                 op0=ALU.mult, op1=ALU.add), ["pre", "vecs"], [f"acc{hf}"])
                    yield
                for t in range(1, 4):
                    for hf in range(2):
                        c0 = hf * HB
                        dve(lambda e, t=t, wc=wc, c0=c0: e.scalar_tensor_tensor(out=acc[:, c0:c0 + HB], in0=pre[:, c0 + t:c0 + t + HB], scalar=wc[t],
                                                                                in1=acc[:, c0:c0 + HB], op0=ALU.mult, op1=ALU.add),
                            ["pre", f"acc{hf}", "vecs"], [f"acc{hf}"])
                        yield
                pool(lambda e, wc=wc: e.tensor_scalar(out=acc[:, NP_:TT], in0=pre[:, 3 + NP_:3 + TT], scalar1=wc[3], scalar2=0.0, op0=ALU.mult, op1=ALU.add),
                     ["pre", "vecs"], ["acc"])
                for t in range(3):
                    dve(lambda e, t=t, wc=wc, ch=ch: e.scalar_tensor_tensor(out=acc[:, NP_:TT], in0=cbufq[:, ch, t, :], scalar=wc[t], in1=acc[:, NP_:TT],
                                                                            op0=ALU.mult, op1=ALU.add), ["cbufq", "acc", "vecs"], ["acc"])
                for hf in range(2):
                    c0 = hf * HB
                    act(lambda e, dst=dst, c0=c0: e.activation(out=dst[:, c0:c0 + HB], in_=acc[:, c0:c0 + HB], func=AF.Silu), [f"acc{hf}"], [dname])
                    yield
                act(lambda e, dsamp=dsamp: e.activation(out=dsamp[:, h, :], in_=acc[:, NP_:TT], func=AF.Silu), ["acc"], [dname + "_s"])
                yield
            for nb, (t0, n) in enumerate(BLK):
                for k in range(8):
                    pe(lambda e, k=k, t0=t0, n=n: e.matmul(out=B[0][:, 0:n], lhsT=wt[:, k, 384:512], rhs=hT[:, k, t0:t0 + n],
                                                          start=(k == 0), stop=(k == 7)), [wkey, "hT"], ["B0"])
                    if k == 3 and n > 16:
                        yield
                act(lambda e, t0=t0, n=n: e.activation(out=szB[:, t0:t0 + n], in_=B[0][:, 0:n], func=AF.Silu), ["B0"], [f"szB{hb}"])
                yield
            dve(lambda e: e.tensor_copy(out=XBs[:, h, :], in_=szB[:, NP_:TT]), [f"szB{hb}"], ["XBs"])
            for qi, (src, sname) in enumerate([(qT, f"qT{hb}"), (kT, f"kT{hb}")]):
                act(lambda e, src=src: e.activation(out=sqs, in_=src[:, 0:NP_], func=AF.Square), [sname], ["sqs"])
                for c in range(16):
                    pe(lambda e, c=c, qi=qi: e.matmul(out=B[0][:, qi * 16 + c:qi * 16 + c + 1], lhsT=sqs[:, c * 128:(c + 1) * 128],
                                                      rhs=ones_b[:, 0:1], start=True, stop=True), ["sqs", "ones_b"], ["B0"])
                yield
            hk = f"hsc{hb}"
            dve(lambda e: e.tensor_copy(out=ssq, in_=B[0][:, 0:32]), ["B0"], [hk])
            rsqrt_ln(rqk, ssq, 1.0, EPS, [hk], [hk], ssq, hk)
            dve(lambda e: e.tensor_scalar(out=rqk[:, 0:16], in0=rqk[:, 0:16], scalar1=128.0 ** -0.5, scalar2=None, op0=ALU.mult), [hk], [hk])
            sbv = sb_tok[:, :].rearrange("p (c h) -> p c h", c=16)[:, :, h]
            egv = egc[:, :].rearrange("p (c h) -> p c h", c=16)[:, :, h]
            ekv = ekd[:, :].rearrange("p (c h) -> p c h", c=16)[:, :, h]
            dve(lambda e: e.tensor_copy(out=sbh, in_=sbv), ["sb_tok"], [hk])
            dve(lambda e: e.tensor_tensor(out=fsc, in0=rqk[:, 16:32], in1=sbv, op=ALU.mult), [hk, "sb_tok"], [hk])
            dve(lambda e: e.tensor_tensor(out=fg, in0=fsc, in1=egv, op=ALU.mult), [hk, "egc"], [hk])
            dve(lambda e: e.tensor_tensor(out=fd, in0=fsc, in1=ekv, op=ALU.mult), [hk, "ekd"], [hk])
            dve(lambda e: e.tensor_tensor(out=ssq[:, 0:16], in0=rqk[:, 0:16], in1=egv, op=ALU.mult), [hk, "egc"], [hk])
            done.add(("conv", h))
            yield

        def prep_task(h, g):
            hb = h % 2
            s_ = g % 2
            Z = setT[s_]
            qT, kT, vT = qTs[hb], kTs[hb], vTs[hb]
            HS = hsc[hb]
            fsc, fg, fd, sbh = HS[:, 64:80], HS[:, 80:96], HS[:, 96:112], HS[:, 112:128]
            hk = f"hsc{hb}"
            T, Tk, X, Xk, Y, Yk = Z["T"], Z["Tk"], Z["X"], Z["Xk"], Z["Y"], Z["Yk"]
            K = lambda n: f"{n}{s_}"
            g0 = g * GW
            pe(lambda e: e.matmul(out=X[:, 0:GW], lhsT=sel[:, h * 128:(h + 1) * 128], rhs=gcT[:, g0:g0 + GW], start=True, stop=True),
               ["sel", "gcT"], [Xk])
            for cc in range(GC):
                c0 = g0 + cc * 128
                pe(lambda e, cc=cc, c0=c0: e.transpose(out=T[:, cc * 128:(cc + 1) * 128], in_=kT[:, c0:c0 + 128], identity=identb[:]),
                   [f"kT{hb}", "identb"], [Tk])
                pe(lambda e, cc=cc, c0=c0: e.transpose(out=T[:, GW + cc * 128:GW + (cc + 1) * 128], in_=vT[:, c0:c0 + 128], identity=identb[:]),
                   [f"vT{hb}", "identb"], [Tk])
            yield
            for cc in range(GC):
                c = g * GC + cc
                sl = slice(cc * 128, (cc + 1) * 128)
                vsl = slice(GW + cc * 128, GW + (cc + 1) * 128)
                dve(lambda e, sl=sl, c=c: e.tensor_scalar(out=Z["Ktok"][:, sl], in0=T[:, sl], scalar1=fsc[:, c:c + 1], scalar2=None, op0=ALU.mult),
                    [Tk, hk], [K("Ktok")])
                dve(lambda e, sl=sl, c=c: e.tensor_scalar(out=Z["Kg"][:, sl], in0=T[:, sl], scalar1=fg[:, c:c + 1], scalar2=None, op0=ALU.mult),
                    [Tk, hk], [K("Kg")])
                act(lambda e, sl=sl, c=c: e.activation(out=Z["Kd"][:, sl], in_=T[:, sl], func=AF.Identity, scale=fd[:, c:c + 1]),
                    [Tk, hk], [K("Kd")])
                act(lambda e, sl=sl, vsl=vsl, c=c: e.activation(out=Z["Vs"][:, sl], in_=T[:, vsl], func=AF.Identity, scale=sbh[:, c:c + 1]),
                    [Tk, hk], [K("Vs")])
                dve(lambda e, sl=sl, c=c: e.scalar_tensor_tensor(out=Z["Dm"][:, sl], in0=X[:, sl], scalar=col(gc_tok, c, h), in1=biasA[:],
                                                                 op0=ALU.subtract, op1=ALU.add), [Xk, "gc_tok", "biasA"], [K("Dm")])
                dve(lambda e, sl=sl, c=c: e.scalar_tensor_tensor(out=Z["Dm2"][:, sl], in0=X[:, sl], scalar=col(gc_tok, c, h), in1=biasB[:],
                                                                 op0=ALU.subtract, op1=ALU.add), [Xk, "gc_tok", "biasB"], [K("Dm2")])
            yield
            for cc in range(GC):
                sl = slice(cc * 128, (cc + 1) * 128)
                pe(lambda e, sl=sl, cc=cc: e.transpose(out=T[:, cc * 128:(cc + 1) * 128], in_=Z["Ktok"][:, sl], identity=identb[:]),
                   [K("Ktok"), "identb"], [Tk])
            yield
            act(lambda e: e.activation(out=Z["KpT"], in_=T[:, 0:GW], func=AF.Copy), [Tk], [K("KpT")])
            act(lambda e: e.activation(out=Z["Dm"], in_=Z["Dm"], func=AF.Exp), [K("Dm")], [K("Dm")])
            act(lambda e: e.activation(out=Z["Dm2"], in_=Z["Dm2"], func=AF.Exp, scale=-1.0), [K("Dm2")], [K("Dm2")])
            dve(lambda e: e.tensor_tensor(out=Z["En"], in0=Z["Dm"], in1=negmA[:, 0:GW], op=ALU.mult), [K("Dm"), "negmA"], [K("En")])
            yield
            for cc in range(GC):
                sl = slice(cc * 128, (cc + 1) * 128)
                c0 = g0 + cc * 128
                pe(lambda e, sl=sl: e.matmul(out=Y[:, sl], lhsT=Z["KpT"][:, sl], rhs=Z["KpT"][:, sl], start=True, stop=True), [K("KpT")], [Yk])
                pe(lambda e, sl=sl, cc=cc, c0=c0: e.matmul(out=X[:, cc * 128:(cc + 1) * 128], lhsT=Z["KpT"][:, sl], rhs=qT[:, c0:c0 + 128],
                                                           start=True, stop=True), [K("KpT"), f"qT{hb}"], [Xk])
            yield
            dve(lambda e: e.tensor_tensor(out=Z["at"], in0=X[:, 0:GW], in1=Z["Dm"], op=ALU.mult), [Xk, K("Dm")], [K("at")])
            dve(lambda e: e.tensor_tensor(out=Z["Mb"], in0=Y[:, 0:GW], in1=Z["En"], op=ALU.mult), [Yk, K("En")], [K("Mb")])
            dve(lambda e: e.scalar_tensor_tensor(out=Z["Nb"], in0=Y[:, 0:GW], scalar=-1.0, in1=Z["Dm2"], op0=ALU.mult, op1=ALU.mult),
                [Yk, K("Dm2")], [K("Nb")])
            yield
            Dt = Z["Tb"]
            D = Z["D"]
            U16 = mybir.dt.uint16
            mku = mkb[:].bitcast(U16)
            dve(lambda e: e.tensor_copy(out=D, in_=identb2[:, 0:GW]), ["identb2"], [K("D")])
            act(lambda e: e.activation(out=Dt, in_=identb2[:, 0:GW], func=AF.Copy), ["identb2"], [K("Tb")])
            yield
            dve(lambda e: e.copy_predicated(out=D, mask=mku[:, 0, 0:GW], data=Z["Nb"]), [K("Nb"), "mkb", K("D")], [K("D")])
            dve(lambda e: e.copy_predicated(out=Dt, mask=mku[:, 0, 0:GW], data=Z["Mb"]), [K("Mb"), "mkb", K("Tb")], [K("Tb")])
            yield
            for lvl in range(1, 7):
                last = (lvl == 6)
                for cc in range(GC):
                    sl = slice(cc * 128, (cc + 1) * 128)
                    pe(lambda e, sl=sl, cc=cc: e.matmul(out=Y[:, cc * 128:(cc + 1) * 128], lhsT=Z["Nb"][:, sl], rhs=Dt[:, sl],
                                                        start=True, stop=True), [K("Nb"), K("Tb")], [Yk])
                if not last:
                    for cc in range(GC):
                        sl = slice(cc * 128, (cc + 1) * 128)
                        pe(lambda e, sl=sl, cc=cc: e.matmul(out=X[:, cc * 128:(cc + 1) * 128], lhsT=Z["Mb"][:, sl], rhs=D[:, sl],
                                                            start=True, stop=True), [K("Mb"), K("D")], [Xk])
                yield
                act(lambda e: e.activation(out=Z["Xb"], in_=Y[:, 0:GW], func=AF.Copy), [Yk], [K("Xb")])
                if not last:
                    act(lambda e: e.activation(out=Z["Xp"], in_=X[:, 0:GW], func=AF.Copy), [Xk], [K("Xp")])
                yield
                for cc in range(GC):
                    sl = slice(cc * 128, (cc + 1) * 128)
                    pe(lambda e, sl=sl, cc=cc: e.matmul(out=Y[:, cc * 128:(cc + 1) * 128], lhsT=D[:, sl], rhs=Z["Xb"][:, sl],
                                                        start=True, stop=True), [K("D"), K("Xb")], [Yk])
                if not last:
                    for cc in range(GC):
                        sl = slice(cc * 128, (cc + 1) * 128)
                        pe(lambda e, sl=sl, cc=cc: e.matmul(out=X[:, cc * 128:(cc + 1) * 128], lhsT=Dt[:, sl], rhs=Z["Xp"][:, sl],
                                                            start=True, stop=True), [K("Tb"), K("Xp")], [Xk])
                yield
                dve(lambda e, lvl=lvl: e.copy_predicated(out=Dt, mask=mku[:, lvl, 0:GW], data=Y[:, 0:GW]), [Yk, "mkb", K("Tb")], [K("Tb")])
                if not last:
                    dve(lambda e, lvl=lvl: e.copy_predicated(out=D, mask=mku[:, lvl, 0:GW], data=X[:, 0:GW]), [Xk, "mkb", K("D")], [K("D")])
                yield
            for cc in range(GC):
                sl = slice(cc * 128, (cc + 1) * 128)
                pe(lambda e, sl=sl: e.matmul(out=X[:, sl], lhsT=Z["Kg"][:, sl], rhs=Z["Tb"][:, sl], start=True, stop=True), [K("Kg"), K("Tb")], [Xk])
            yield
            act(lambda e: e.activation(out=Z["nW"], in_=X[:, 0:GW], func=AF.Identity, scale=-1.0), [Xk], [K("nW")])
            done.add(("prep", h, g))
            yield

        def prep_chain(h, parity):
            yield from wait_for(("conv", h))
            yield from wait_for("3a")
            for g in range(parity, NG, 2):
                if g >= 2:
                    yield from wait_for(("scan", h, g - 2))
                elif h > 0:
                    yield from wait_for(("scan", h - 1, NG - 2 + parity))
                yield from prep_task(h, g)

        def scan_chain(h):
            hb = h % 2
            qT, szB = qTs[hb], szBs[hb]
            HS = hsc[hb]
            rqk, rqe = HS[:, 32:64], HS[:, 0:16]
            hk = f"hsc{hb}"
            if h > 0:
                yield from wait_for(("scanhead", h - 1))
            else:
                yield from wait_for("3a")
            dve(lambda e: e.memset(S, 0.0), [], ["S"])
            dve(lambda e: e.memset(Sb, 0.0), [], ["Sb"])
            for g in range(NG):
                yield from wait_for(("prep", h, g))
                s_ = g % 2
                Z = setT[s_]
                K = lambda n: f"{n}{s_}"
                for cc in range(GC):
                    c = g * GC + cc
                    sl = slice(cc * 128, (cc + 1) * 128)
                    qsl = slice(c * 128, (c + 1) * 128)
                    pe(lambda e, sl=sl: e.matmul(out=B7[:, 0:128], lhsT=Z["Tb"][:, sl], rhs=Z["Vs"][:, sl], start=True, stop=False), [K("Tb"), K("Vs")], ["B7"])
                    pe(lambda e, sl=sl: e.matmul(out=B7[:, 0:128], lhsT=Z["nW"][:, sl], rhs=Sb, start=False, stop=True), [K("nW"), "Sb"], ["B7"])
                    act(lambda e: e.activation(out=vnb, in_=B7[:, 0:128], func=AF.Copy), ["B7"], ["vnb"])
                    pe(lambda e, sl=sl: e.matmul(out=B7[:, 384:512], lhsT=Z["Kd"][:, sl], rhs=vnb, start=True, stop=True), [K("Kd"), "vnb"], ["B7"])
                    pe(lambda e, qsl=qsl: e.matmul(out=B7[:, 128:256], lhsT=qT[:, qsl], rhs=Sb, start=True, stop=True), [f"qT{hb}", "Sb"], ["B7"])
                    pe(lambda e, sl=sl: e.matmul(out=B7[:, 256:384], lhsT=Z["at"][:, sl], rhs=vnb, start=True, stop=True), [K("at"), "vnb"], ["B7"])
                    dve(lambda e, c=c: e.scalar_tensor_tensor(out=S, in0=S, scalar=col(egl, c, h), in1=B7[:, 384:512], op0=ALU.mult, op1=ALU.add),
                        ["S", "egl", "B7"], ["S"])
                    act(lambda e: e.activation(out=Sb, in_=S, func=AF.Copy), ["S"], ["Sb"])
                    act(lambda e, c=c: e.activation(out=o1s, in_=B7[:, 128:256], func=AF.Identity, scale=rqe[:, c:c + 1]), ["B7", hk], ["o1s"])
                    dve(lambda e, c=c, sl=sl: e.scalar_tensor_tensor(out=Z["oraw"][:, sl], in0=B7[:, 256:384], scalar=rqk[:, c:c + 1], in1=o1s,
                                                                     op0=ALU.mult, op1=ALU.add), ["B7", hk, "o1s"], [K("oraw")])
                    act(lambda e, cc=cc, sl=sl: e.activation(out=junk2, in_=Z["oraw"][:, sl], func=AF.Square, accum_out=Z["sso"][:, cc:cc + 1]),
                        [K("oraw")], ["junk2", K("sso")])
                    yield
                rsqrt_ln(Z["ro"], Z["sso"], 1.0 / 128, EPS, [K("sso")], [K("ro")], Z["sso"], K("sso"))
                for cc in range(GC):
                    sl = slice(cc * 128, (cc + 1) * 128)
                    pool(lambda e, cc=cc, sl=sl: e.tensor_scalar(out=Z["og"][:, sl], in0=Z["oraw"][:, sl], scalar1=Z["ro"][:, cc:cc + 1], scalar2=0.0,
                                                                 op0=ALU.mult, op1=ALU.add), [K("oraw"), K("ro")], [K("og")])
                    pe(lambda e, cc=cc, sl=sl: e.transpose(out=Z["T"][:, GW + cc * 128:GW + (cc + 1) * 128], in_=Z["og"][:, sl], identity=identb[:]),
                       [K("og"), "identb"], [Z["Tk"]])
                t0 = g * GW
                dve(lambda e, t0=t0: e.scalar_tensor_tensor(out=Z["xo"], in0=Z["T"][:, GW:2 * GW], scalar=vecs[:, V_ON:V_ON + 1],
                                                            in1=szB[:, t0:t0 + GW], op0=ALU.mult, op1=ALU.mult),
                    [Z["Tk"], "vecs", f"szB{hb}"], [K("xo")])
                dma(lambda e, t0=t0: e.dma_start(out=xbs_d[h][:, t0:t0 + GW], in_=Z["xo"]), [K("xo")], ["xbs_dram"])
                done.add(("scan", h, g))
                yield
            dma(lambda e: e.dma_start(out=S_p_d[h], in_=S), ["S"], ["S_p_dram"])
            done.add(("scanhead", h))
            yield

        def phase3a_chain():
            HT_ALL = [f"hT{n}" for n in range(5)]
            wt, wkey = beta_w

            def ev_beta(nb, ps, t0, n, bkey):
                act(lambda e: e.activation(out=betaT[:, t0:t0 + n], in_=ps, func=AF.Sigmoid), [bkey], ["betaT"])
            proj_fm(wt, wkey, 0, 8, ev_beta, M=8, bankset=(1, 2))

            def ev_alpha(nb, ps, t0, n, bkey):
                dve(lambda e: e.tensor_scalar(out=xg[0:8, t0:t0 + n], in0=ps, scalar1=hv[:, 1:2], scalar2=None, op0=ALU.add), [bkey, "hv"], ["xg"])
            proj_fm(wt, wkey, 1, 8, ev_alpha, M=8, bankset=(1, 2))
            yield
            dve(lambda e: e.scalar_tensor_tensor(out=axg[0:8, :], in0=xg[0:8, :], scalar=-1.0, in1=xg[0:8, :], op0=ALU.mult, op1=ALU.max), ["xg"], ["axg"])
            act(lambda e: e.activation(out=axg[0:8, :], in_=axg[0:8, :], func=AF.Exp, scale=-1.0), ["axg"], ["axg"])
            act(lambda e: e.activation(out=lg[0:8, :], in_=axg[0:8, :], func=AF.Ln, bias=1.0, scale=1.0), ["axg"], ["lg"])
            act(lambda e: e.activation(out=negA[0:8, 0:1], in_=hv[:, 0:1], func=AF.Exp), ["hv"], ["negA"])
            dve(lambda e: e.tensor_scalar(out=negA[0:8, 0:1], in0=negA[0:8, 0:1], scalar1=-1.0, scalar2=None, op0=ALU.mult), ["negA"], ["negA"])
            dve(lambda e: e.scalar_tensor_tensor(out=lg[0:8, :], in0=xg[0:8, :], scalar=0.0, in1=lg[0:8, :], op0=ALU.max, op1=ALU.add),
                ["xg", "lg"], ["lg"])
            dve(lambda e: e.tensor_scalar(out=gT[:, :], in0=lg[0:8, :], scalar1=negA[0:8, 0:1], scalar2=None, op0=ALU.mult), ["lg", "negA"], ["gT"])
            act(lambda e: e.activation(out=sbT[:, :], in_=betaT[:, 0:NP_], func=AF.Sqrt), ["betaT"], ["sbT"])
            for c in range(16):
                dve(lambda e, c=c: e.tensor_tensor_scan(out=gcT[:, c * 128:(c + 1) * 128], data0=ones_f[0:8, 0:128], data1=gT[:, c * 128:(c + 1) * 128],
                                                        initial=0.0, op0=ALU.mult, op1=ALU.add), ["gT", "ones_f"], ["gcT"])
            yield
            for c in range(16):
                pe(lambda e, c=c: e.transpose(out=B[2][:, c * 8:(c + 1) * 8], in_=gcT[:, c * 128:(c + 1) * 128], identity=ident[0:8, 0:8]),
                   ["gcT", "ident"], ["B2"])
                pe(lambda e, c=c: e.transpose(out=B[3][:, c * 8:(c + 1) * 8], in_=sbT[:, c * 128:(c + 1) * 128], identity=ident[0:8, 0:8]),
                   ["sbT", "ident"], ["B3"])
            dve(lambda e: e.tensor_copy(out=gc_tok[:], in_=B[2][:, 0:128]), ["B2"], ["gc_tok"])
            dve(lambda e: e.tensor_copy(out=sb_tok[:], in_=B[3][:, 0:128]), ["B3"], ["sb_tok"])
            yield
            tl3 = tmpl[0:8, :].rearrange("p (c h) -> p c h", c=16)
            gl_src = gcT[:, :].rearrange("p (c t) -> p c t", c=16)[:, :, 127:128]
            dve(lambda e: e.tensor_tensor(out=tl3, in0=bm[:, :].rearrange("p (c h) -> p c h", c=16), in1=gl_src.to_broadcast([8, 16, 8]), op=ALU.mult),
                ["bm", "gcT"], ["tmpl"])
            pe(lambda e: e.matmul(out=B[4][:, 0:128], lhsT=ones_f[0:8, :], rhs=tmpl[0:8, :], start=True, stop=True), ["tmpl", "ones_f"], ["B4"])
            act(lambda e: e.activation(out=egl[:], in_=B[4][:, 0:128], func=AF.Exp), ["B4"], ["egl"])
            dve(lambda e: e.tensor_tensor(out=ekd[:], in0=B[4][:, 0:128], in1=gc_tok[:], op=ALU.subtract), ["B4", "gc_tok"], ["ekd"])
            act(lambda e: e.activation(out=ekd[:], in_=ekd[:], func=AF.Exp), ["ekd"], ["ekd"])
            act(lambda e: e.activation(out=egc[:], in_=gc_tok[:], func=AF.Exp), ["gc_tok"], ["egc"])
            dve(lambda e: e.tensor_copy(out=beta_s[:, :], in_=betaT[:, NP_:TT]), ["betaT"], ["beta_s"])
            dve(lambda e: e.tensor_copy(out=g_s[:, :], in_=gT[:, NP_:TT]), ["gT"], ["g_s"])


            P.fence()
            done.add("3a")
            yield

        def conv_chain():
            for h in range(8):
                if h >= 2:
                    yield from wait_for(("scanhead", h - 2))
                yield from conv_task(h)

        def run_chains(chains):
            active = list(chains)
            while active:
                progressed = False
                for ch in list(active):
                    try:
                        r = next(ch)
                        if r != "blocked":
                            progressed = True
                    except StopIteration:
                        active.remove(ch)
                        progressed = True
                assert progressed, "chain deadlock"

        chains = [phase3a_chain(), conv_chain()]
        for h in range(8):
            chains += [prep_chain(h, 0), prep_chain(h, 1), scan_chain(h)]
        run_chains(chains)

        ckpt(4)
        P.fence(); arena.reset()
        ZA = ab(8 * TT).rearrange("p (k t) -> p k t", k=8)
        za_end = arena.off
        B5f = B5[:, :].bitcast(F32)
        B6f = B6[:, :].bitcast(F32)

        def sample_chain():
            def bcast_hs(dst, srcT, skey, dkey):
                t = af(128)
                dve(lambda e: e.tensor_tensor(out=t[0:8, :].rearrange("p (h s) -> p h s", h=8), in0=bm[:, :].rearrange("p (s h) -> p h s", s=16),
                                              in1=srcT.unsqueeze(1).to_broadcast([8, 8, NS]), op=ALU.mult), ["bm", skey], [dkey + "_t"])
                pe(lambda e: e.matmul(out=B[2][:, 0:128], lhsT=ones_f[0:8, :], rhs=t[0:8, :], start=True, stop=True), [dkey + "_t", "ones_f"], ["B2"])
                dve(lambda e: e.tensor_copy(out=dst, in_=B[2][:, 0:128]), ["B2"], [dkey])
            beta_bc = af(128); g_bc = af(128); eg_bc = af(128)
            bcast_hs(beta_bc, beta_s[:, :], "beta_s", "beta_bc")
            bcast_hs(g_bc, g_s[:, :], "g_s", "g_bc")
            act(lambda e: e.activation(out=eg_bc, in_=g_bc, func=AF.Exp), ["g_bc"], ["eg_bc"])
            yield
            HS = 8 * NS
            q2 = qs_s[:].rearrange("p h s -> p (h s)"); k2 = ks_s[:].rearrange("p h s -> p (h s)"); v2 = vs_s[:].rearrange("p h s -> p (h s)")
            sq2 = af(HS); rq2 = af(HS); rk2 = af(HS); tmp2 = af(HS)
            for (src, skey, dst, dkey, scl) in [(q2, "qT_s", rq2, "rq2", 128.0 ** -0.5), (k2, "kT_s", rk2, "rk2", 1.0)]:
                dve(lambda e, src=src: e.tensor_tensor(out=sq2, in0=src, in1=src, op=ALU.mult), [skey], ["sq2"])
                pe(lambda e: e.matmul(out=B[3][:, 0:HS], lhsT=ones_f[:], rhs=sq2, start=True, stop=True), ["sq2", "ones_f"], ["B3"])
                rsqrt_ln(dst, B[3][:, 0:HS], 1.0, EPS, ["B3"], [dkey], tmp2, "tmp2")
                dve(lambda e, src=src, dst=dst, scl=scl: e.scalar_tensor_tensor(out=src, in0=src, scalar=scl, in1=dst, op0=ALU.mult, op1=ALU.mult),
                    [skey, dkey], [skey])
                yield
            wcol = af(HS); ucol = af(HS); qg = af(HS); qk = af(HS); qk_bc = af(HS)
            dve(lambda e: e.tensor_tensor(out=wcol, in0=k2, in1=beta_bc, op=ALU.mult), ["kT_s", "beta_bc"], ["wcol"])
            dve(lambda e: e.tensor_tensor(out=wcol, in0=wcol, in1=eg_bc, op=ALU.mult), ["wcol", "eg_bc"], ["wcol"])
            dve(lambda e: e.tensor_tensor(out=ucol, in0=v2, in1=beta_bc, op=ALU.mult), ["vT_s", "beta_bc"], ["ucol"])
            dve(lambda e: e.tensor_tensor(out=qg, in0=q2, in1=eg_bc, op=ALU.mult), ["qT_s", "eg_bc"], ["qg"])
            dve(lambda e: e.tensor_tensor(out=qk, in0=q2, in1=k2, op=ALU.mult), ["qT_s", "kT_s"], ["qk"])
            pe(lambda e: e.matmul(out=B[4][:, 0:HS], lhsT=ones_f[:], rhs=qk, start=True, stop=True), ["qk", "ones_f"], ["B4"])
            dve(lambda e: e.tensor_copy(out=qk_bc, in_=B[4][:, 0:HS]), ["B4"], ["qk_bc"])
            yield
            os_all = af(HS)
            w3 = wcol.rearrange("p (h s) -> p h s", h=8); u3 = ucol.rearrange("p (h s) -> p h s", h=8)
            qg3 = qg.rearrange("p (h s) -> p h s", h=8); qk3 = qk_bc.rearrange("p (h s) -> p h s", h=8)
            eg3 = eg_bc.rearrange("p (h s) -> p h s", h=8); o3 = os_all.rearrange("p (h s) -> p h s", h=8)
            Ss = [af(1024), af(1024)]
            vn_all = af(HS); tmpS = af(1024)
            vn3 = vn_all.rearrange("p (h s) -> p h s", h=8)
            rowk = af(1024); rowv = af(1024); rmask = [af(1024), af(1024)]
            for s in range(NS):
                Si, sk = Ss[s % 2], f"Ss{s % 2}"
                S3 = Si.rearrange("p (h e) -> p h e", h=8)
                bank, bkey = (B7, "B7") if s % 2 == 0 else (B[4], "B4")
                dma(lambda e: e.dma_start(out=S3, in_=sd[s].rearrange("h d e -> d h e")), [], [sk])
                for hh in range(8):
                    pe(lambda e, hh=hh: e.matmul(out=bank[:, hh:hh + 1], lhsT=S3[:, hh, :], rhs=w3[:, hh, s:s + 1], start=True, stop=True),
                       [sk, "wcol"], [bkey])
                    pe(lambda e, hh=hh: e.matmul(out=bank[:, 8 + hh:9 + hh], lhsT=S3[:, hh, :], rhs=qg3[:, hh, s:s + 1], start=True, stop=True),
                       [sk, "qg"], [bkey])
                dve(lambda e: e.tensor_tensor(out=vn3[:, :, s], in0=u3[:, :, s], in1=bank[:, 0:8], op=ALU.subtract), ["ucol", bkey], ["vn_all"])
                dve(lambda e: e.tensor_tensor(out=o3[:, :, s], in0=qk3[:, :, s], in1=vn3[:, :, s], op=ALU.mult), ["qk_bc", "vn_all"], ["os_all"])
                dve(lambda e: e.tensor_tensor(out=o3[:, :, s], in0=o3[:, :, s], in1=bank[:, 8:16], op=ALU.add), ["os_all", bkey], ["os_all"])
                yield
            for hh in range(8):
                bkr, bkrk = (B[3], "B3") if hh < 4 else (B[4], "B4")
                bv, bvk = (B[1], "B1") if hh < 4 else (B[2], "B2")
                c0 = (hh % 4) * 128
                pe(lambda e, hh=hh, c0=c0, bkr=bkr: e.transpose(out=bkr[0:NS, c0:c0 + 128], in_=ks_s[:, hh, :], identity=ident[:]), ["kT_s", "ident"], [bkrk])
                pe(lambda e, hh=hh, c0=c0, bv=bv: e.transpose(out=bv[0:NS, c0:c0 + 128], in_=vn3[:, hh, :], identity=ident[:]), ["vn_all", "ident"], [bvk])
            yield
            dve(lambda e: e.tensor_copy(out=rowk[0:NS, 0:512], in_=B[3][0:NS, :]), ["B3"], ["rowk"])
            dve(lambda e: e.tensor_copy(out=rowk[0:NS, 512:1024], in_=B[4][0:NS, :]), ["B4"], ["rowk"])
            act(lambda e: e.activation(out=rowv[0:NS, 0:512], in_=B[1][0:NS, :], func=AF.Copy), ["B1"], ["rowv"])
            act(lambda e: e.activation(out=rowv[0:NS, 512:1024], in_=B[2][0:NS, :], func=AF.Copy), ["B2"], ["rowv"])
            yield
            for s in range(NS):
                Si, sk = Ss[s % 2], f"Ss{s % 2}"
                S3 = Si.rearrange("p (h e) -> p h e", h=8)
                rm, rmk = rmask[s % 2], f"rmask{s % 2}"
                dma(lambda e: e.dma_start(out=S3, in_=sd[s].rearrange("h d e -> d h e")), [], [sk])
                dve(lambda e: e.tensor_scalar(out=rm[0:NS, :], in0=rowv[0:NS, :], scalar1=ident[0:NS, s:s + 1], scalar2=None, op0=ALU.mult),
                    ["rowv", "ident"], [rmk])
                yield
                for hh in range(8):
                    bo, bok = (B[1], "B1") if hh < 4 else (B[2], "B2")
                    c0 = (hh % 4) * 128
                    pe(lambda e, hh=hh, c0=c0, bo=bo: e.matmul(out=bo[:, c0:c0 + 128], lhsT=rowk[0:NS, hh * 128:(hh + 1) * 128],
                                                               rhs=rm[0:NS, hh * 128:(hh + 1) * 128], start=True, stop=True), ["rowk", rmk], [bok])
                T3 = tmpS.rearrange("p (h e) -> p h e", h=8)
                dve(lambda e: e.tensor_tensor(out=T3, in0=S3, in1=eg3[:, :, s:s + 1].to_broadcast([128, 8, 128]), op=ALU.mult), [sk, "eg_bc"], ["tmpS"])
                yield
                dve(lambda e: e.tensor_tensor(out=tmpS[:, 0:512], in0=tmpS[:, 0:512], in1=B[1][:, :], op=ALU.add), ["tmpS", "B1"], ["tmpS"])
                dve(lambda e: e.tensor_tensor(out=tmpS[:, 512:1024], in0=tmpS[:, 512:1024], in1=B[2][:, :], op=ALU.add), ["tmpS", "B2"], ["tmpS"])
                dma(lambda e: e.dma_start(out=S_s_d[s].rearrange("h d e -> d h e"), in_=T3), ["tmpS"], ["S_s_dram"])
                yield
            so2 = af(HS); rso = af(HS)
            dve(lambda e: e.tensor_tensor(out=so2, in0=os_all, in1=os_all, op=ALU.mult), ["os_all"], ["so2"])
            pe(lambda e: e.matmul(out=B[2][:, 0:HS], lhsT=ones_f[:], rhs=so2, start=True, stop=True), ["so2", "ones_f"], ["B2"])
            rsqrt_ln(rso, B[2][:, 0:HS], 1.0 / 128, EPS, ["B2"], ["rso"], tmp2, "tmp2")
            dve(lambda e: e.scalar_tensor_tensor(out=os_all, in0=os_all, scalar=vecs[:, V_ON:V_ON + 1], in1=rso, op0=ALU.mult, op1=ALU.mult),
                ["os_all", "vecs", "rso"], ["os_all"])
            dve(lambda e: e.tensor_tensor(out=XBs[:], in0=XBs[:], in1=o3, op=ALU.mult), ["XBs", "os_all"], ["XBs"])
            dma(lambda e: e.dma_start(out=xbs_d.rearrange("h p t -> p h t")[:, :, NP_:TT], in_=XBs[:]), ["XBs"], ["xbs_dram"])
            yield

        gwv = gcT_full[:, 0:2048].bitcast(BF16)
        gws = [gwv[:, 0:2048].rearrange("p (k n) -> p k n", k=8), gwv[:, 2048:4096].rearrange("p (k n) -> p k n", k=8)]

        def load_gates(fo):
            gwt = gws[fo % 2]
            for gi_, goff in enumerate([OFF_GA, OFF_GB]):
                src = w_in[:, goff + fo * 128:goff + (fo + 1) * 128].rearrange("(k p) n -> p k n", p=128)
                dma(lambda e, gi_=gi_, src=src, gwt=gwt: e.dma_start(out=gwt[:, :, gi_ * 128:(gi_ + 1) * 128], in_=src), [], [f"gw{fo % 2}"], eng="pool")
        pre_w = {}

        def branchA_chain():
            ub = af(2 + TT); acca = af(TT); szf = ab(TT); tz = af(512)
            dve(lambda e: e.memset(ub[:, 0:2], 0.0), [], ["ub"])
            banks = [(B[0], "B0"), (B5f, "B5"), (B6f, "B6")]
            cnt = [0]

            def proj(wt, wkey, j, evac):
                for nb, (t0, n) in enumerate(BLK):
                    bank, bkey = banks[cnt[0] % 3]
                    cnt[0] += 1
                    for k in range(8):
                        pe(lambda e, k=k, bank=bank, t0=t0, n=n: e.matmul(out=bank[:, 0:n], lhsT=wt[:, k, j * 128:(j + 1) * 128], rhs=hT[:, k, t0:t0 + n],
                                                                         start=(k == 0), stop=(k == 7)), [wkey, "hT"], [bkey])
                    evac(nb, bank[:, 0:n], t0, n, bkey)
                    yield

            for f in range(8):
                wt, wkey = get_w("brA", [(w_in, [OFF_HA + ff * 128, OFF_CA + ff * 128, OFF_BA + ff * 128, OFF_ZA + ff * 128], 128) for ff in range(8)], f)
                if f == 7:
                    pre_w["wA0"] = load_w(w_out_a, [0], 512)

                def ev_ha(nb, ps, t0, n, bkey):
                    act(lambda e: e.activation(out=ub[:, 2 + t0:2 + t0 + n], in_=ps, func=AF.Copy), [bkey], ["ub"])
                yield from proj(wt, wkey, 0, ev_ha)

                def ev_ca(nb, ps, t0, n, bkey):
                    dve(lambda e: e.tensor_tensor(out=ub[:, 2 + t0:2 + t0 + n], in0=ps, in1=ub[:, 2 + t0:2 + t0 + n], op=ALU.mult), [bkey, "ub"], ["ub"])
                yield from proj(wt, wkey, 1, ev_ca)
                pool(lambda e, f=f: e.tensor_copy(out=tail_a_p[:, f, :], in_=ub[:, NP_:NP_ + 2]), ["ub"], ["tail_a_p"])
                pool(lambda e, f=f: e.tensor_copy(out=tail_a_s[:, f, 0, :], in_=cbufa[:, f, 1, :]), ["cbufa"], ["tail_a_s"])
                pool(lambda e, f=f: e.tensor_copy(out=tail_a_s[:, f, 1, :], in_=ub[:, 2 + NP_:2 + TT]), ["ub"], ["tail_a_s"])
                wc = [vecs[:, V_CA + t * 8 + f:V_CA + t * 8 + f + 1] for t in range(3)]
                HB = NP_ // 2
                for hf in range(2):
                    c0 = hf * HB
                    pool(lambda e, wc=wc, c0=c0: e.tensor_scalar(out=acca[:, c0:c0 + HB], in0=ub[:, c0:c0 + HB], scalar1=wc[0], scalar2=0.0,
                                                                 op0=ALU.mult, op1=ALU.add), ["ub", "vecs"], ["acca"])
                    yield
                for t in range(1, 3):
                    for hf in range(2):
                        c0 = hf * HB
                        dve(lambda e, t=t, wc=wc, c0=c0: e.scalar_tensor_tensor(out=acca[:, c0:c0 + HB], in0=ub[:, c0 + t:c0 + t + HB], scalar=wc[t],
                                                                                in1=acca[:, c0:c0 + HB], op0=ALU.mult, op1=ALU.add),
                            ["ub", "acca", "vecs"], ["acca"])
                        yield
                pool(lambda e, wc=wc: e.tensor_scalar(out=acca[:, NP_:TT], in0=ub[:, 2 + NP_:2 + TT], scalar1=wc[2], scalar2=0.0, op0=ALU.mult, op1=ALU.add),
                     ["ub", "vecs"], ["acca"])
                for t in range(2):
                    dve(lambda e, t=t, wc=wc, f=f: e.scalar_tensor_tensor(out=acca[:, NP_:TT], in0=cbufa[:, f, t, :], scalar=wc[t], in1=acca[:, NP_:TT],
                                                                          op0=ALU.mult, op1=ALU.add), ["cbufa", "acca", "vecs"], ["acca"])

                def ev_za(nb, ps, t0, n, bkey, f=f):
                    act(lambda e: e.activation(out=szf[:, t0:t0 + n], in_=ps, func=AF.Silu), [bkey], ["szf"])
                yield from proj(wt, wkey, 3, ev_za)

                def ev_ba(nb, ps, t0, n, bkey, f=f):
                    dve(lambda e: e.tensor_tensor(out=tz[:, 0:n], in0=ps, in1=acca[:, t0:t0 + n], op=ALU.mult), [bkey, "acca"], ["tz"])
                    dve(lambda e: e.tensor_tensor(out=ZA[:, f, t0:t0 + n], in0=tz[:, 0:n], in1=szf[:, t0:t0 + n], op=ALU.mult), ["tz", "szf"], [f"ZA{nb}"])
                yield from proj(wt, wkey, 2, ev_ba)
            pre_w["wB0"] = load_w(w_out_b, [0], 512)
            load_gates(0)
            yield

        run_chains([sample_chain(), branchA_chain()])

        ckpt(6)
        P.fence()
        arena.off = za_end
        MG = ab(8 * TT).rearrange("p (k t) -> p k t", k=8)
        mg_end = arena.off
        XB = ab(8 * TT).rearrange("p (k t) -> p k t", k=8)
        for nb_, (t0_, n_) in enumerate(BLK):
            dma(lambda e, t0_=t0_, n_=n_: e.dma_start(out=XB[:, :, t0_:t0_ + n_], in_=xbs_d.rearrange("h p t -> p h t")[:, :, t0_:t0_ + n_]),
                ["xbs_dram"], [f"XB{nb_}"])
        sA = ab(512); sB = ab(512); tA = af(512); tB = af(512)

        def mm8(bank, bkey, M, lhs_fn, rhs_fn, rkeys, n):
            for k in range(8):
                pe(lambda e, k=k: e.matmul(out=bank[0:M, 0:n], lhsT=lhs_fn(k), rhs=rhs_fn(k), start=(k == 0), stop=(k == 7)), rkeys, [bkey])

        for half in range(2):
            wA, wAk = pre_w["wA0"] if half == 0 else load_w(w_out_a, [half * 512], 512)
            wB, wBk = pre_w["wB0"] if half == 0 else load_w(w_out_b, [half * 512], 512)
            for jj in range(4):
                fo = half * 4 + jj
                if fo + 1 < 8:
                    load_gates(fo + 1)
                gw, gwk = gws[fo % 2], f"gw{fo % 2}"
                for nb, (t0, n) in enumerate(BLK):
                    mm8(B[0], "B0", 128, lambda k, gw=gw: gw[:, k, 0:128], lambda k, t0=t0, n=n: hT[:, k, t0:t0 + n], [gwk, f"hT{nb}"], n)
                    act(lambda e, n=n: e.activation(out=sA[:, 0:n], in_=B[0][:, 0:n], func=AF.Sigmoid), ["B0"], ["sA"])
                    mm8(B[1], "B1", 128, lambda k, jj=jj, wA=wA: wA[:, k, jj * 128:(jj + 1) * 128], lambda k, t0=t0, n=n: ZA[:, k, t0:t0 + n],
                        [wAk, f"ZA{nb}"], n)
                    dve(lambda e, n=n: e.tensor_tensor(out=tA[:, 0:n], in0=B[1][:, 0:n], in1=sA[:, 0:n], op=ALU.mult), ["B1", "sA"], ["tA"])
                    mm8(B[2], "B2", 128, lambda k, gw=gw: gw[:, k, 128:256], lambda k, t0=t0, n=n: hT[:, k, t0:t0 + n], [gwk, f"hT{nb}"], n)
                    act(lambda e, n=n: e.activation(out=sB[:, 0:n], in_=B[2][:, 0:n], func=AF.Sigmoid), ["B2"], ["sB"])
                    mm8(B[3], "B3", 128, lambda k, jj=jj, wB=wB: wB[:, k, jj * 128:(jj + 1) * 128], lambda k, t0=t0, n=n: XB[:, k, t0:t0 + n],
                        [wBk, f"XB{nb}"], n)
                    dve(lambda e, n=n: e.tensor_tensor(out=tB[:, 0:n], in0=B[3][:, 0:n], in1=sB[:, 0:n], op=ALU.mult), ["B3", "sB"], ["tB"])
                    pool(lambda e, fo=fo, t0=t0, n=n: e.tensor_tensor(out=MG[:, fo, t0:t0 + n], in0=tA[:, 0:n], in1=tB[:, 0:n], op=ALU.add),
                         ["tA", "tB"], [f"MG{nb}"])

        ckpt(7)
        P.fence()
        arena.off = 0
        wo_b = ab(8 * 1024).rearrange("p (k n) -> p k n", k=8)
        for n2 in range(2):
            src = w_o[:, n2 * 512:(n2 + 1) * 512].rearrange("(k p) n -> p k n", p=128)
            dma(lambda e, n2=n2, src=src: e.dma_start(out=wo_b[:, :, n2 * 512:(n2 + 1) * 512], in_=src), [], [f"wo_b{n2}"], eng="pool")
        xt = [af(D), af(D)]
        yt0 = af(D)
        assert arena.off <= za_end
        arena.off = mg_end
        yt = [yt0, af(D)]
        junk = ab(512)
        ss2 = af(2); rs5 = af(2); tmp5 = af(2)
        for i in range(17):
            rows = 128 if i < 16 else NS
            t0 = i * 128
            mk = f"MG{i // 4}"
            xi, xk = xt[i % 2], f"xt{i % 2}"
            yi, yk = yt[i % 2], f"yt{i % 2}"
            src = xp[i * 128:(i + 1) * 128, :] if i < 16 else xsm
            dst = y_p[i * 128:(i + 1) * 128, :] if i < 16 else y_s
            gsrc = gn if i < 16 else gns
            dma(lambda e, xi=xi, rows=rows, src=src: e.dma_start(out=xi[0:rows, :], in_=src), [], [xk])
            bo = 2 * (i % 2)
            for n2 in range(2):
                mm8(B[bo + n2], f"B{bo + n2}", rows, lambda k, t0=t0, rows=rows: MG[:, k, t0:t0 + rows], lambda k, n2=n2: wo_b[:, k, n2 * 512:(n2 + 1) * 512],
                    [mk, f"wo_b{n2}"], 512)
                act(lambda e, n2=n2, rows=rows, bo=bo: e.activation(out=junk[0:rows, :], in_=B[bo + n2][0:rows, :], func=AF.Square,
                                                                  accum_out=ss2[0:rows, n2:n2 + 1]), [f"B{bo + n2}"], ["junk", "ss2"])
            dve(lambda e, rows=rows: e.tensor_tensor(out=ss2[0:rows, 0:1], in0=ss2[0:rows, 0:1], in1=ss2[0:rows, 1:2], op=ALU.add), ["ss2"], ["ss2"])
            rsqrt_small(rs5[0:rows, 0:1], ss2[0:rows, 0:1], 1.0 / D, EPS, ["ss2"], ["rs5"], tmp5[0:rows, 0:1], "tmp5")
            for n2 in range(2):
                cs_ = slice(n2 * 512, (n2 + 1) * 512)
                dve(lambda e, n2=n2, rows=rows, cs_=cs_, yi=yi, gsrc=gsrc, bo=bo: e.scalar_tensor_tensor(
                    out=yi[0:rows, cs_], in0=B[bo + n2][0:rows, :], scalar=rs5[0:rows, 0:1], in1=gsrc[0:rows, cs_], op0=ALU.mult, op1=ALU.mult),
                    [f"B{bo + n2}", "rs5", "gn", "gns"], [yk])
            dve(lambda e, rows=rows, yi=yi, xi=xi: e.tensor_tensor(out=yi[0:rows, :], in0=yi[0:rows, :], in1=xi[0:rows, :], op=ALU.add), [yk, xk], [yk])
            dma(lambda e, yi=yi, rows=rows, dst=dst: e.dma_start(out=dst, in_=yi[0:rows, :]), [yk], ["y_dram"])
        dma(lambda e: e.dma_start(out=tail_a_p_d, in_=tail_a_p[:].rearrange("p a b -> p (a b)")), ["tail_a_p"], ["tap_dram"])
        dma(lambda e: e.dma_start(out=tail_a_s_d, in_=tail_a_s[:].rearrange("p a b c -> p (a b c)")), ["tail_a_s"], ["tas_dram"])
        dma(lambda e: e.dma_start(out=tail_q_p_d, in_=tail_q_p[:].rearrange("p a b -> p (a b)")), ["tail_q_p"], ["tqp_dram"])
        dma(lambda e: e.dma_start(out=tail_q_s_d, in_=tail_q_s[:].rearrange("p a b c -> p (a b c)")), ["tail_q_s"], ["tqs_dram"])
        return

    with ExitStack() as st:
        P = Prog(nc, st)
        try:
            _body(st, P)
        except _Stop:
            pass
        P.fence()
        P.emit()
    return nc


_NC_CACHE = {}


def _consts():
    p = np.arange(128)[:, None]
    c = np.arange(128)[None, :]
    ident = np.eye(128, dtype=np.float32)
    biasA = np.where(c >= p, 0.0, -BIG).astype(np.float32)
    biasB = np.where(p > c, 0.0, BIG).astype(np.float32)
    negmA = np.tile(np.where(c > p, -1.0, 0.0).astype(np.float32), (1, 4))
    sel = np.zeros((8, 8, 128), np.float32)
    for h in range(8):
        sel[h, h, :] = 1.0
    bm = np.zeros((8, 16, 8), np.float32)
    for h in range(8):
        bm[h, :, h] = 1.0
    mks = np.zeros((128, 7, 128), np.float32)
    mks[:, 0, :] = (p // 2 == c // 2) & (p != c)
    for l in range(1, 7):
        b = 2 ** l
        mks[:, l, :] = (p // (2 * b) == c // (2 * b)) & (p // b != c // b)
    return dict(ident=ident, biasA=biasA, biasB=biasB, negmA=negmA, sel=sel.reshape(8, 1024), bm=bm.reshape(8, 128),
                mks=mks.reshape(128, 7 * 128))


def kernel(x_prompt, x_sample, c_prompt, c_sample, state_conv_a, state_conv_qkv, state_delta,
           ada_w, ada_b, norm_pre, w_in, conv_a_w, conv_b_w, a_log, dt_bias, onorm_w,
           w_out_a, w_out_b, w_o, norm_post):
    f = lambda a: np.ascontiguousarray(np.asarray(a, dtype=np.float32))
    x_prompt, x_sample, c_prompt, c_sample = f(x_prompt), f(x_sample), f(c_prompt), f(c_sample)
    state_conv_a, state_conv_qkv, state_delta = f(state_conv_a), f(state_conv_qkv), f(state_delta)
    if "nc" not in _NC_CACHE:
        _NC_CACHE["nc"] = build_nc(_NC_CACHE.get("stop", 99))
    nc = _NC_CACHE["nc"]
    n = 8
    vecs = np.zeros((128, 160), np.float32)
    vecs[:, 0:24] = f(ada_b)[0].reshape(24, 128).T
    vecs[:, 24:32] = f(norm_pre)[0].reshape(8, 128).T
    vecs[:, 32:56] = f(conv_a_w)[0].reshape(3, 8, 128).transpose(2, 0, 1).reshape(128, 24)
    vecs[:, 56] = f(onorm_w)[0]
    vecs[:, 64:160] = f(conv_b_w)[0].reshape(4, 24, 128).transpose(2, 0, 1).reshape(128, 96)
    hv = np.stack([f(a_log)[0], f(dt_bias)[0]], axis=1)
    shared = dict(ada_w=f(ada_w)[0], adab_bc=np.ascontiguousarray(np.broadcast_to(f(ada_b)[0, 2048:3072], (128, 1024))),
                  npost_bc=np.ascontiguousarray(np.broadcast_to(f(norm_post)[0], (128, 1024))),
                  vecs=vecs, hv=np.ascontiguousarray(hv), w_in=f(w_in)[0], w_out_a=f(w_out_a)[0], w_out_b=f(w_out_b)[0], w_o=f(w_o)[0])
    shared.update(_consts())
    in_maps = []
    for b in range(n):
        s0, s1 = b * NS, (b + 1) * NS
        ca = state_conv_a[0, s0:s1].reshape(NS, 2, 8, 128).transpose(3, 2, 1, 0)
        cq = state_conv_qkv[0, s0:s1].reshape(NS, 3, 24, 128).transpose(3, 2, 1, 0)
        m = dict(shared)
        m.update(xp=x_prompt[b], xsm=x_sample[s0:s1, 0, :], cp_bc=np.ascontiguousarray(np.broadcast_to(c_prompt[b], (128, 1024))),
                 cs=c_sample[s0:s1], cbufa=np.ascontiguousarray(ca).reshape(128, -1), cbufq=np.ascontiguousarray(cq).reshape(128, -1),
                 sd=state_delta[0, s0:s1])
        in_maps.append({k: np.ascontiguousarray(v) for k, v in m.items()})
    res = run_bass_kernel_spmd(nc, in_maps, core_ids=list(range(n)))
    R = res.results
    y_p = np.stack([R[b]["y_p"] for b in range(n)])
    y_s = np.concatenate([R[b]["y_s"] for b in range(n)])[:, None, :]
    nca_p = np.stack([R[b]["tail_a_p"].reshape(128, 8, 2).transpose(2, 1, 0).reshape(2, 1024) for b in range(n)])[None]
    ncq_p = np.stack([R[b]["tail_q_p"].reshape(128, 24, 3).transpose(2, 1, 0).reshape(3, 3072) for b in range(n)])[None]
    nd_p = np.stack([R[b]["S_p"] for b in range(n)])[None]
    nca_s = np.concatenate([R[b]["tail_a_s"].reshape(128, 8, 2, NS).transpose(3, 2, 1, 0).reshape(NS, 2, 1024) for b in range(n)])[None]
    ncq_s = np.concatenate([R[b]["tail_q_s"].reshape(128, 24, 3, NS).transpose(3, 2, 1, 0).reshape(NS, 3, 3072) for b in range(n)])[None]
    nd_s = np.concatenate([R[b]["S_s"] for b in range(n)])[None]
    out = (y_p, y_s, nca_p, ncq_p, nd_p, nca_s, ncq_s, nd_s)
    return tuple(np.ascontiguousarray(o, dtype=np.float32) for o in out)
```
